# Optimizing a Trainium2 kernel written in Bass

```python
import math, functools
import jax, jax.numpy as jnp
from jax import lax
import numpy as np

D_MODEL = 2048
BATCH = 8
SEQ = 4096
DEPTH = 1
DEC_BATCH = 16
DEC_SEQ = 16
PAST_LEN = 4096

CHUNK = 64
BAND_PAST_CHUNKS = 8
BAND_PAST = BAND_PAST_CHUNKS * CHUNK
BAND = BAND_PAST + CHUNK
ATT_HEADS = 16
ATT_HEAD_DIM = 64
ATT_WIDTH = ATT_HEADS * ATT_HEAD_DIM
REL_CLIP = 128
SSM_GROUP = 16
SSM_WIDTH = 1024
SSM_GROUPS = SSM_WIDTH // SSM_GROUP
SSM_STATE = 64
N_MEM = 256
MEM_HEADS = 4
MEM_HEAD_DIM = 256
MEM_WIDTH = MEM_HEADS * MEM_HEAD_DIM
N_BRANCH = 3
D_FF = 5632
EPS = 1e-6
DT_MIN = 1e-3
DT_MAX = 1e-1
IN_WIDTH = SSM_WIDTH + 3 * ATT_WIDTH + MEM_WIDTH + N_BRANCH * D_MODEL
SPLITS = [SSM_WIDTH, SSM_WIDTH + ATT_WIDTH, SSM_WIDTH + 2 * ATT_WIDTH, SSM_WIDTH + 3 * ATT_WIDTH,
          SSM_WIDTH + 3 * ATT_WIDTH + MEM_WIDTH]
NEG_INF = -1e30

kernel_name = "hybrid_streaming_encoder_step"


def rms_norm(x, g):
    xf = x.astype(jnp.float32)
    y = xf * lax.rsqrt(jnp.mean(xf * xf, axis=-1, keepdims=True) + EPS)
    return (y * g.astype(jnp.float32)).astype(x.dtype)


def swiglu(x, wg, wu, wd):
    return (jax.nn.silu(x @ wg) * (x @ wu)) @ wd


def macaron_ffn(h, g_pre, g_post, wg, wu, wd):
    return h + 0.5 * rms_norm(swiglu(rms_norm(h, g_pre), wg, wu, wd), g_post)


def s5_discretize(a_re, a_im, log_dt, b_re, b_im):
    a_re = a_re.astype(jnp.float32)
    a_im = a_im.astype(jnp.float32)
    dt = jnp.exp(log_dt.astype(jnp.float32))[:, None]
    mag = jnp.exp(a_re * dt)
    ab_re = mag * jnp.cos(a_im * dt)
    ab_im = mag * jnp.sin(a_im * dt)
    den = a_re * a_re + a_im * a_im
    n_re = ab_re - 1.0
    n_im = ab_im
    c_re = (n_re * a_re + n_im * a_im) / den
    c_im = (n_im * a_re - n_re * a_im) / den
    b_re = b_re.astype(jnp.float32)
    b_im = b_im.astype(jnp.float32)
    bb_re = c_re[..., None] * b_re - c_im[..., None] * b_im
    bb_im = c_re[..., None] * b_im + c_im[..., None] * b_re
    return ab_re, ab_im, bb_re, bb_im


def _complex_affine_combine(e1, e2):
    a1r, a1i, b1r, b1i = e1
    a2r, a2i, b2r, b2i = e2
    ar = a2r * a1r - a2i * a1i
    ai = a2r * a1i + a2i * a1r
    br = a2r * b1r - a2i * b1i + b2r
    bi = a2r * b1i + a2i * b1r + b2i
    return ar, ai, br, bi


def s5_branch(u, s0_re, s0_im, p):
    n_b, t, _ = u.shape
    uf = u.astype(jnp.float32).reshape(n_b, t, SSM_GROUPS, SSM_GROUP)
    ab_re, ab_im, bb_re, bb_im = s5_discretize(p["ssm_a_re"], p["ssm_a_im"], p["ssm_log_dt"],
                                               p["ssm_b_re"], p["ssm_b_im"])
    bu_re = jnp.einsum("gph,btgh->btgp", bb_re, uf)
    bu_im = jnp.einsum("gph,btgh->btgp", bb_im, uf)
    s0_re = s0_re.astype(jnp.float32)
    s0_im = s0_im.astype(jnp.float32)
    bu_re = bu_re.at[:, 0].add(ab_re * s0_re - ab_im * s0_im)
    bu_im = bu_im.at[:, 0].add(ab_re * s0_im + ab_im * s0_re)
    a_re_t = jnp.broadcast_to(ab_re, (1, t, SSM_GROUPS, SSM_STATE))
    a_im_t = jnp.broadcast_to(ab_im, (1, t, SSM_GROUPS, SSM_STATE))
    _, _, s_re, s_im = lax.associative_scan(_complex_affine_combine, (a_re_t, a_im_t, bu_re, bu_im), axis=1)
    c_re = p["ssm_c_re"].astype(jnp.float32)
    c_im = p["ssm_c_im"].astype(jnp.float32)
    y = (jnp.einsum("ghp,btgp->btgh", c_re, s_re) - jnp.einsum("ghp,btgp->btgh", c_im, s_im)
         + p["ssm_d"].astype(jnp.float32) * uf)
    y = jax.nn.gelu(y.reshape(n_b, t, SSM_WIDTH))
    out = y * jax.nn.sigmoid(y @ p["ssm_w_glu"].astype(jnp.float32) + p["ssm_b_glu"].astype(jnp.float32))
    return out.astype(u.dtype), s_re[:, -1], s_im[:, -1]


def band_attend(q, k, v, q_pos, k_pos, rel_bias):
    s = jnp.einsum("bqhd,bkhd->bhqk", q, k).astype(jnp.float32) * (ATT_HEAD_DIM ** -0.5)
    rel = jnp.clip(q_pos[:, None] - k_pos[None, :], -REL_CLIP, REL_CLIP) + REL_CLIP
    s = s + rel_bias.astype(jnp.float32)[:, rel]
    q_chunk = q_pos // CHUNK
    k_chunk = k_pos // CHUNK
    ok = ((k_pos[None, :] >= 0) & (k_chunk[None, :] <= q_chunk[:, None])
          & (k_chunk[None, :] >= q_chunk[:, None] - BAND_PAST_CHUNKS))
    s = jnp.where(ok, s, NEG_INF)
    prob = jax.nn.softmax(s, axis=-1)
    return jnp.einsum("bhqk,bkhd->bqhd", prob.astype(v.dtype), v)


def band_attention_prompt(q, k, v, rel_bias):
    n_b, t, _, _ = q.shape
    n_chunks = t // CHUNK
    kp = jnp.pad(k, ((0, 0), (BAND_PAST, 0), (0, 0), (0, 0)))
    vp = jnp.pad(v, ((0, 0), (BAND_PAST, 0), (0, 0), (0, 0)))

    def one_chunk(c):
        start = c * CHUNK
        q_c = lax.dynamic_slice_in_dim(q, start, CHUNK, axis=1)
        k_c = lax.dynamic_slice_in_dim(kp, start, BAND, axis=1)
        v_c = lax.dynamic_slice_in_dim(vp, start, BAND, axis=1)
        q_pos = start + jnp.arange(CHUNK)
        k_pos = start - BAND_PAST + jnp.arange(BAND)
        return band_attend(q_c, k_c, v_c, q_pos, k_pos, rel_bias)

    o = lax.map(one_chunk, jnp.arange(n_chunks))
    return jnp.moveaxis(o, 0, 1).reshape(n_b, t, ATT_WIDTH)


def band_attention_sample(q, k, v, rel_bias, k_cache, v_cache):
    n_b, t, _, _ = q.shape
    w = k_cache.shape[1]
    k_all = jnp.concatenate([k_cache.astype(k.dtype), k], axis=1)
    v_all = jnp.concatenate([v_cache.astype(v.dtype), v], axis=1)
    q_pos = PAST_LEN + jnp.arange(t)
    k_pos = PAST_LEN - w + jnp.arange(w + t)
    o = band_attend(q, k_all, v_all, q_pos, k_pos, rel_bias)
    return o.reshape(n_b, t, ATT_WIDTH)


def memory_attend(q, mem_k, mem_v):
    n_b, t, _, _ = q.shape
    s = jnp.einsum("bqhd,bmhd->bhqm", q, mem_k.astype(q.dtype)).astype(jnp.float32) * (MEM_HEAD_DIM ** -0.5)
    prob = jax.nn.softmax(s, axis=-1)
    o = jnp.einsum("bhqm,bmhd->bqhd", prob.astype(q.dtype), mem_v.astype(q.dtype))
    return o.reshape(n_b, t, MEM_WIDTH)


def memory_kv(mem, g, w_k, w_v):
    n_b = mem.shape[0]
    m = rms_norm(mem, g)
    mk = (m @ w_k).reshape(n_b, N_MEM, MEM_HEADS, MEM_HEAD_DIM)
    mv = (m @ w_v).reshape(n_b, N_MEM, MEM_HEADS, MEM_HEAD_DIM)
    return mk, mv


def layer_forward(h, p, attn_fn, s0_re, s0_im, mem_k, mem_v):
    h = macaron_ffn(h, p["ffn1_norm_pre"], p["ffn1_norm_post"], p["ffn1_w_gate"], p["ffn1_w_up"], p["ffn1_w_down"])
    n_b, t, _ = h.shape
    u = rms_norm(h, p["mix_norm_pre"])
    proj = u @ p["w_in"]
    u_s, q, k, v, q_mem, gate_logits = jnp.split(proj, SPLITS, axis=-1)
    q = q.reshape(n_b, t, ATT_HEADS, ATT_HEAD_DIM)
    k = k.reshape(n_b, t, ATT_HEADS, ATT_HEAD_DIM)
    v = v.reshape(n_b, t, ATT_HEADS, ATT_HEAD_DIM)
    q_mem = q_mem.reshape(n_b, t, MEM_HEADS, MEM_HEAD_DIM)
    o_s, s_re, s_im = s5_branch(u_s, s0_re, s0_im, p)
    o_a = attn_fn(q, k, v, p["att_rel_bias"])
    o_m = memory_attend(q_mem, mem_k, mem_v)
    gates = jax.nn.sigmoid(gate_logits.reshape(n_b, t, N_BRANCH, D_MODEL))
    merged = (gates[:, :, 0] * (o_s @ p["w_branch_ssm"])
              + gates[:, :, 1] * (o_a @ p["w_branch_att"])
              + gates[:, :, 2] * (o_m @ p["w_branch_mem"]))
    h = h + rms_norm(merged @ p["w_out"], p["mix_norm_post"])
    h = macaron_ffn(h, p["ffn2_norm_pre"], p["ffn2_norm_post"], p["ffn2_w_gate"], p["ffn2_w_up"], p["ffn2_w_down"])
    return h, k, v, s_re, s_im


def setup_inputs(seed: int = 0) -> dict:
    key = jax.random.key(seed)
    ks = iter(jax.random.split(key, 64))
    f32 = jnp.float32

    def nrm(shape, scale):
        return jax.random.normal(next(ks), shape, f32) * scale

    def gain(shape):
        return 1.0 + nrm(shape, 0.02)

    att_cache = min(BAND_PAST, PAST_LEN)
    inp = {}
    inp["x_prompt"] = nrm((BATCH, SEQ, D_MODEL), 1.0)
    inp["x_sample"] = nrm((DEC_BATCH, DEC_SEQ, D_MODEL), 1.0)
    inp["mem_prompt"] = nrm((BATCH, N_MEM, D_MODEL), 1.0)
    inp["cache_att_k"] = nrm((DEPTH, DEC_BATCH, att_cache, ATT_HEADS, ATT_HEAD_DIM), 1.0)
    inp["cache_att_v"] = nrm((DEPTH, DEC_BATCH, att_cache, ATT_HEADS, ATT_HEAD_DIM), 1.0)
    inp["cache_mem_k"] = nrm((DEPTH, DEC_BATCH, N_MEM, MEM_HEADS, MEM_HEAD_DIM), 1.0)
    inp["cache_mem_v"] = nrm((DEPTH, DEC_BATCH, N_MEM, MEM_HEADS, MEM_HEAD_DIM), 1.0)
    inp["state_ssm_re"] = nrm((DEPTH, DEC_BATCH, SSM_GROUPS, SSM_STATE), 0.1)
    inp["state_ssm_im"] = nrm((DEPTH, DEC_BATCH, SSM_GROUPS, SSM_STATE), 0.1)
    inp["ffn1_norm_pre"] = gain((DEPTH, D_MODEL))
    inp["ffn1_norm_post"] = gain((DEPTH, D_MODEL))
    inp["ffn1_w_gate"] = nrm((DEPTH, D_MODEL, D_FF), D_MODEL ** -0.5)
    inp["ffn1_w_up"] = nrm((DEPTH, D_MODEL, D_FF), D_MODEL ** -0.5)
    inp["ffn1_w_down"] = nrm((DEPTH, D_FF, D_MODEL), D_FF ** -0.5)
    inp["mix_norm_pre"] = gain((DEPTH, D_MODEL))
    inp["mix_norm_post"] = gain((DEPTH, D_MODEL))
    inp["w_in"] = nrm((DEPTH, D_MODEL, IN_WIDTH), D_MODEL ** -0.5)
    inp["ssm_a_re"] = -0.5 + nrm((DEPTH, SSM_GROUPS, SSM_STATE), 0.01)
    inp["ssm_a_im"] = (math.pi * jnp.arange(SSM_STATE, dtype=f32))[None, None, :] + nrm((DEPTH, SSM_GROUPS, SSM_STATE), 0.01)
    inp["ssm_log_dt"] = jax.random.uniform(next(ks), (DEPTH, SSM_GROUPS), f32, math.log(DT_MIN), math.log(DT_MAX))
    inp["ssm_b_re"] = nrm((DEPTH, SSM_GROUPS, SSM_STATE, SSM_GROUP), (2 * SSM_GROUP) ** -0.5)
    inp["ssm_b_im"] = nrm((DEPTH, SSM_GROUPS, SSM_STATE, SSM_GROUP), (2 * SSM_GROUP) ** -0.5)
    inp["ssm_c_re"] = nrm((DEPTH, SSM_GROUPS, SSM_GROUP, SSM_STATE), SSM_STATE ** -0.5)
    inp["ssm_c_im"] = nrm((DEPTH, SSM_GROUPS, SSM_GROUP, SSM_STATE), SSM_STATE ** -0.5)
    inp["ssm_d"] = nrm((DEPTH, SSM_GROUPS, SSM_GROUP), 1.0)
    inp["ssm_w_glu"] = nrm((DEPTH, SSM_WIDTH, SSM_WIDTH), SSM_WIDTH ** -0.5)
    inp["ssm_b_glu"] = nrm((DEPTH, SSM_WIDTH), 0.01)
    inp["att_rel_bias"] = nrm((DEPTH, ATT_HEADS, 2 * REL_CLIP + 1), 0.5)
    inp["mem_norm"] = gain((DEPTH, D_MODEL))
    inp["w_mem_k"] = nrm((DEPTH, D_MODEL, MEM_WIDTH), D_MODEL ** -0.5)
    inp["w_mem_v"] = nrm((DEPTH, D_MODEL, MEM_WIDTH), D_MODEL ** -0.5)
    inp["w_branch_ssm"] = nrm((DEPTH, SSM_WIDTH, D_MODEL), SSM_WIDTH ** -0.5)
    inp["w_branch_att"] = nrm((DEPTH, ATT_WIDTH, D_MODEL), ATT_WIDTH ** -0.5)
    inp["w_branch_mem"] = nrm((DEPTH, MEM_WIDTH, D_MODEL), MEM_WIDTH ** -0.5)
    inp["w_out"] = nrm((DEPTH, D_MODEL, D_MODEL), D_MODEL ** -0.5)
    inp["ffn2_norm_pre"] = gain((DEPTH, D_MODEL))
    inp["ffn2_norm_post"] = gain((DEPTH, D_MODEL))
    inp["ffn2_w_gate"] = nrm((DEPTH, D_MODEL, D_FF), D_MODEL ** -0.5)
    inp["ffn2_w_up"] = nrm((DEPTH, D_MODEL, D_FF), D_MODEL ** -0.5)
    inp["ffn2_w_down"] = nrm((DEPTH, D_FF, D_MODEL), D_FF ** -0.5)
    return inp


def reference(x_prompt, x_sample, mem_prompt, cache_att_k, cache_att_v, cache_mem_k, cache_mem_v,
              state_ssm_re, state_ssm_im,
              ffn1_norm_pre, ffn1_norm_post, ffn1_w_gate, ffn1_w_up, ffn1_w_down,
              mix_norm_pre, mix_norm_post, w_in,
              ssm_a_re, ssm_a_im, ssm_log_dt, ssm_b_re, ssm_b_im, ssm_c_re, ssm_c_im, ssm_d,
              ssm_w_glu, ssm_b_glu, att_rel_bias, mem_norm, w_mem_k, w_mem_v,
              w_branch_ssm, w_branch_att, w_branch_mem, w_out,
              ffn2_norm_pre, ffn2_norm_post, ffn2_w_gate, ffn2_w_up, ffn2_w_down):
    n_bp, t_p, _ = x_prompt.shape
    keep = min(BAND_PAST, t_p)
    h_p = x_prompt
    h_s = x_sample
    att_k_p, att_v_p, mem_k_p, mem_v_p, ssm_re_p, ssm_im_p = [], [], [], [], [], []
    att_k_s, att_v_s, ssm_re_s, ssm_im_s = [], [], [], []
    for l in range(DEPTH):
        p = {
            "ffn1_norm_pre": ffn1_norm_pre[l], "ffn1_norm_post": ffn1_norm_post[l],
            "ffn1_w_gate": ffn1_w_gate[l], "ffn1_w_up": ffn1_w_up[l], "ffn1_w_down": ffn1_w_down[l],
            "mix_norm_pre": mix_norm_pre[l], "mix_norm_post": mix_norm_post[l], "w_in": w_in[l],
            "ssm_a_re": ssm_a_re[l], "ssm_a_im": ssm_a_im[l], "ssm_log_dt": ssm_log_dt[l],
            "ssm_b_re": ssm_b_re[l], "ssm_b_im": ssm_b_im[l], "ssm_c_re": ssm_c_re[l], "ssm_c_im": ssm_c_im[l],
            "ssm_d": ssm_d[l], "ssm_w_glu": ssm_w_glu[l], "ssm_b_glu": ssm_b_glu[l],
            "att_rel_bias": att_rel_bias[l],
            "w_branch_ssm": w_branch_ssm[l], "w_branch_att": w_branch_att[l], "w_branch_mem": w_branch_mem[l],
            "w_out": w_out[l],
            "ffn2_norm_pre": ffn2_norm_pre[l], "ffn2_norm_post": ffn2_norm_post[l],
            "ffn2_w_gate": ffn2_w_gate[l], "ffn2_w_up": ffn2_w_up[l], "ffn2_w_down": ffn2_w_down[l],
        }
        mk_p, mv_p = memory_kv(mem_prompt, mem_norm[l], w_mem_k[l], w_mem_v[l])
        zero_state = jnp.zeros((n_bp, SSM_GROUPS, SSM_STATE), jnp.float32)
        h_p, k_p, v_p, sr_p, si_p = layer_forward(h_p, p, band_attention_prompt, zero_state, zero_state, mk_p, mv_p)
        att_k_p.append(k_p[:, t_p - keep:])
        att_v_p.append(v_p[:, t_p - keep:])
        mem_k_p.append(mk_p)
        mem_v_p.append(mv_p)
        ssm_re_p.append(sr_p)
        ssm_im_p.append(si_p)
        attn_s = functools.partial(band_attention_sample, k_cache=cache_att_k[l], v_cache=cache_att_v[l])
        h_s, k_s, v_s, sr_s, si_s = layer_forward(h_s, p, attn_s, state_ssm_re[l], state_ssm_im[l],
                                                  cache_mem_k[l], cache_mem_v[l])
        att_k_s.append(k_s)
        att_v_s.append(v_s)
        ssm_re_s.append(sr_s)
        ssm_im_s.append(si_s)
    return (h_p, h_s,
            jnp.stack(att_k_p), jnp.stack(att_v_p), jnp.stack(mem_k_p), jnp.stack(mem_v_p),
            jnp.stack(ssm_re_p), jnp.stack(ssm_im_p),
            jnp.stack(att_k_s), jnp.stack(att_v_s), jnp.stack(ssm_re_s), jnp.stack(ssm_im_s))
```

```python
import contextlib
import math
import os
import numpy as np
import concourse.bass as bass
import concourse.mybir as mybir
from concourse.bass_utils import run_bass_kernel_spmd

F32 = mybir.dt.float32
BF16 = mybir.dt.bfloat16
I32 = mybir.dt.int32
AF = mybir.ActivationFunctionType
ALU = mybir.AluOpType
AX = mybir.AxisListType

D = 2048
DC = 16
FC = 44
T = 256
NT = 16
SEQ = 4096
TS = 32
EPS = 1e-6
NEG = -1e30
NDS = 12
NSLOT = 8
SAME_SYNC = True
EPOCH_TILES = 3
STOP = int(os.environ.get('KSTOP', '99'))
NTILES = int(os.environ.get('KTILES', '16'))
NCORES = int(os.environ.get('KCORES', '8'))
KSUB = int(os.environ.get('KSUB', '99'))
KSUB2 = int(os.environ.get('KSUB2', '99'))
TSTOP = int(os.environ.get('KTSTOP', '99'))

WTAB = [("f1g", 16, 44), ("f1u", 16, 44), ("f1d", 44, 16), ("win", 16, 88), ("glu", 8, 8),
        ("bs", 8, 16), ("ba", 8, 16), ("bm", 8, 16), ("wo", 16, 16),
        ("f2g", 16, 44), ("f2u", 16, 44), ("f2d", 44, 16), ("mk", 16, 8), ("mv", 16, 8)]
WOFF = {}
_o = 0
for _n, _kc, _nb in WTAB:
    WOFF[_n] = (_o, _kc, _nb)
    _o += _kc * _nb * 128
WX = _o
CH = 2048
NCHUNK = WX // CH

SPC = {}
_o = 0
for _n, _w in [("g_f1pre", 16), ("g_f1post", 16), ("g_mpre", 16), ("g_mpost", 16), ("g_f2pre", 16),
               ("g_f2post", 16), ("g_mem", 16), ("dcol", 8), ("bglu", 8), ("ch", 16),
               ("are", 32), ("aim", 32), ("ldt", 32), ("s0", 128), ("jidx", 64)]:
    SPC[_n] = (_o, _w)
    _o += _w
SPW = _o
BTW = 16 * 5 * 64
SBT_OFF = BTW
BT_TOT = BTW + 16 * 16 + 16 * 16


class Op:
    __slots__ = ("eng", "fn", "deps", "idx", "sig", "sigval", "epoch", "dsem")


class Prog:
    ENG = ("pe", "act", "dve", "pool", "sp")

    def __init__(self):
        self.ops = {e: [] for e in self.ENG}
        self.lastw = {}
        self.rd = {}
        self.const = set()
        self.epoch = 0

    def op(self, eng, fn, reads=(), writes=()):
        o = Op()
        o.eng = eng
        o.fn = fn
        o.sig = False
        o.epoch = self.epoch
        deps = {}
        ex = [k for k in reads if k[0] in ("psL", "psB")]
        if ex:
            writes = list(writes) + ex
            reads = [k for k in reads if k[0] not in ("psL", "psB")]

        def add(d):
            if d is None:
                return
            if d.eng == "sp":
                deps[("sp", d.idx)] = d
            else:
                k = d.eng
                if k not in deps or deps[k].idx < d.idx:
                    deps[k] = d

        for k in reads:
            add(self.lastw.get(k))
        for k in writes:
            add(self.lastw.get(k))
            for r in self.rd.get(k, ()):
                add(r)
        if eng == "pe":
            deps.pop("pe", None)
        elif eng != "sp" and not SAME_SYNC:
            deps.pop(eng, None)
        o.deps = list(deps.values())
        for d in o.deps:
            d.sig = True
        o.idx = len(self.ops[eng])
        self.ops[eng].append(o)
        for k in writes:
            self.lastw[k] = o
            self.rd[k] = []
        for k in reads:
            if k not in self.const:
                self.rd.setdefault(k, []).append(o)
        return o

    def assign(self):
        self.nepoch = self.epoch + 1
        for e in self.ENG:
            if e == "sp":
                for o in self.ops[e]:
                    o.dsem = o.idx % NDS
                    o.sigval = 16 * (o.idx // NDS + 1)
            else:
                cnt = {}
                for o in self.ops[e]:
                    if o.sig:
                        cnt[o.epoch] = cnt.get(o.epoch, 0) + 1
                        o.sigval = cnt[o.epoch]

    def emit(self, e, eng, sems, dsems):
        waited = {}

        def w(key, sem, val):
            if waited.get(key, 0) < val:
                eng.wait_ge(sem, val)
                waited[key] = val

        for o in self.ops[e]:
            if e == "sp" and o.idx >= NDS:
                p = self.ops["sp"][o.idx - NDS]
                w(("d", p.dsem), dsems[p.dsem], p.sigval)
            for d in o.deps:
                if d.eng == "sp":
                    w(("d", d.dsem), dsems[d.dsem], d.sigval)
                else:
                    w((d.eng, d.epoch), sems[d.eng][d.epoch], d.sigval)
            inst = o.fn(eng)
            if e == "sp":
                inst.then_inc(dsems[o.dsem], 16)
            elif o.sig:
                inst.then_inc(sems[e][o.epoch], 1)
        if e == "sp":
            n = len(self.ops["sp"])
            for i in range(max(0, n - NDS), n):
                p = self.ops["sp"][i]
                w(("d", p.dsem), dsems[p.dsem], p.sigval)


def build_nc():
    nc = bass.Bass("TRN2", target_bir_lowering=False)
    P = Prog()

    def din(name, shape, dt=F32):
        return nc.dram_tensor(name, list(shape), dt, kind="ExternalInput").ap()

    def dout(name, shape, dt=F32):
        return nc.dram_tensor(name, list(shape), dt, kind="ExternalOutput").ap()

    xT = din("xT", [D, SEQ])
    xsT = din("xsT", [D, TS])
    memT = din("memT", [D, 256])
    ckT = din("ckT", [2, 1024, 512])
    cv = din("cv", [2, 512, 1024])
    cmkT = din("cmkT", [2, 1024, 256])
    cmv = din("cmv", [2, 256, 1024])
    wall = din("wall", [128, WX])
    spk = din("spk", [128, SPW])
    ssmB = din("ssmB", [128, 4, 5, 1024])
    ssmC = din("ssmC", [128, 4, 2048])
    btd = din("btd", [128, BT_TOT])
    yT = dout("yT", [D, SEQ])
    ysT = dout("ysT", [D, TS])
    okT = dout("okT", [1024, 512])
    ov = dout("ov", [512, 1024])
    omkT = dout("omkT", [1024, 256])
    omv = dout("omv", [256, 1024])
    ost = dout("ost", [128, 2, 32])
    oskT = dout("oskT", [1024, TS])
    osv = dout("osv", [2, 16, 1024])
    osst = dout("osst", [128, 2, 2, 32])
    wbf = nc.dram_tensor("wbf", [128, WX], BF16, kind="Internal").ap()
    ssmw = nc.dram_tensor("ssmw", [128, 4, 2, 2048], BF16, kind="Internal").ap()
    tabd = nc.dram_tensor("tabd", [128, 3, 32, 64], F32, kind="Internal").ap()

    es = contextlib.ExitStack()
    with es:
        def sb(name, shape, dt):
            return es.enter_context(nc.sbuf_tensor(name, list(shape), dt))

        h = sb("h", [128, DC, T], F32)
        xn = sb("xn", [128, DC, T], BF16)
        yb = sb("yb", [128, DC, T], F32)
        R = sb("R", [128, 20480], BF16)
        kring = sb("kring", [128, 8, 6, 128], BF16)
        vring = sb("vring", [128, 6, 1024], BF16)
        mkT = sb("mkT", [128, 8, 256], BF16)
        mvb = sb("mvb", [128, 2, 1024], BF16)
        btl = sb("btl", [128, 2, 2, 5, 64], F32)
        sbt = sb("sbt", [128, 2, 16, 16], F32)
        tabq = sb("tabq", [128, 3, 16, 64], F32)
        rt16 = sb("rt16", [128, 16, 16], F32)
        sw = sb("sw", [128, 6, 1024], F32)
        ssb = sb("ssb", [128, 16, 2, 64], BF16)
        wring = sb("wring", [128, NSLOT, 2048], BF16)
        pT = sb("pT", [128, 20, 64], BF16)
        ones_f = sb("ones_f", [128, 128], F32)
        ones_b = sb("ones_b", [128, 128], BF16)
        spt = sb("spt", [128, SPW], F32)
        sm = sb("sm", [128, 8, T], F32)
        rsb = sb("rsb", [128, 2, T], F32)
        stt = sb("stt", [128, 2, 2, 32], F32)
        rth = sb("rth", [128, 3, 32], F32)
        ps_t = es.enter_context(nc.psum_tensor("ps", [128, 8, 512], F32))

        if os.environ.get('KVERB'):
            print("SBUF bytes remaining per partition:", nc.sbuf_bytes_remaining)
        def rview(off_bytes, dt, n, pat=None, **kw):
            eb = 2
            a = R[:, off_bytes // eb: off_bytes // eb + (n * (4 if dt == F32 else 2)) // eb]
            if dt == F32:
                a = a.bitcast(F32)
            if pat:
                a = a.rearrange(pat, **kw)
            return a

        act = rview(0, BF16, FC * T, "p (c t) -> p c t", c=FC)
        sqb = rview(0, F32, DC * T, "p (c t) -> p c t", c=DC)
        qT = rview(0, BF16, 8 * T, "p (c t) -> p c t", c=8)
        usf = rview(4096, F32, 8 * T, "p (c t) -> p c t", c=8)
        usb = rview(12288, BF16, 8 * T, "p (c t) -> p c t", c=8)
        oa = rview(16384, BF16, 8 * T, "p (c t) -> p c t", c=8)
        om = rview(20480, BF16, 8 * T, "p (c t) -> p c t", c=8)
        os_ = rview(24576, BF16, 8 * T, "p (c t) -> p c t", c=8)
        mg = rview(28672, BF16, DC * T, "p (c t) -> p c t", c=DC)
        xpre = rview(24576, F32, DC * T, "p (c t) -> p c t", c=DC)
        XPK = [("os", c) for c in range(8)] + [("mg", c) for c in range(DC)]
        NSF, NSB = 5, 4
        stg_f = [rview(i * 8192, F32, CH) for i in range(NSF)]
        ybflat = yb[:, :, :].rearrange("p c t -> p (c t)").bitcast(BF16)
        stg_b = [ybflat[:, i * CH:(i + 1) * CH] for i in range(NSB)]
        pq = [rview(i * 4096, F32, 1024) for i in range(10)]

        def spc(name):
            o, w_ = SPC[name]
            return spt[:, o:o + w_]

        def dma(out, in_, reads, writes):
            return P.op("sp", lambda e: e.dma_start(out=out, in_=in_), reads, writes)

        def act_fn(out, in_, func, reads, writes, bias=None, scale=None):
            kw = {}
            if bias is not None:
                kw["bias"] = bias
            if scale is not None:
                kw["scale"] = scale
            return P.op("act", lambda e: e.activation(out=out, in_=in_, func=func, **kw), reads, writes)

        def tt(eng, out, a, b, op, reads, writes):
            return P.op(eng, lambda e: e.tensor_tensor(out=out, in0=a, in1=b, op=op), reads, writes)

        def ts(eng, out, a, s1, op0, reads, writes, s2=None, op1=None):
            if op1 is None:
                return P.op(eng, lambda e: e.tensor_scalar(out=out, in0=a, scalar1=s1, scalar2=None, op0=op0), reads, writes)
            return P.op(eng, lambda e: e.tensor_scalar(out=out, in0=a, scalar1=s1, scalar2=s2, op0=op0, op1=op1), reads, writes)

        def stt_(out, a, s, b, op0, op1, reads, writes):
            return P.op("dve", lambda e: e.scalar_tensor_tensor(out=out, in0=a, scalar=s, in1=b, op0=op0, op1=op1), reads, writes)

        def cp(eng, out, in_, reads, writes):
            if eng == "act":
                return P.op("act", lambda e: e.copy(out=out, in_=in_), reads, writes)
            return P.op(eng, lambda e: e.tensor_copy(out=out, in_=in_), reads, writes)

        def mmg(out, pairs, reads, writes):
            def fn(e):
                n = len(pairs)
                inst = None
                for i, (l, r) in enumerate(pairs):
                    inst = e.matmul(out, lhsT=l, rhs=r, start=(i == 0), stop=(i == n - 1))
                return inst
            return P.op("pe", fn, reads, writes)

        def mms(items, reads, writes):
            def fn(e):
                inst = None
                for (o, l, r) in items:
                    inst = e.matmul(o, lhsT=l, rhs=r, start=True, stop=True)
                return inst
            return P.op("pe", fn, reads, writes)

        class Pool_:
            def __init__(self, name, n):
                self.name, self.n, self.i, self.base = name, n, 0, 0

            def next(self):
                k = self.base + self.i % self.n
                self.i += 1
                return k

        plL = Pool_("pb", 5)
        plL.base = 3
        plS = Pool_("pS", 3)
        plW = Pool_("w", NSLOT)
        plPG = Pool_("pTg", 4)
        plM = Pool_("sm", 8)
        plR = Pool_("rsb", 2)

        def bk(i):
            return ps_t[:, i, :]

        def pbk(i):
            return ("pb", i)

        def psL(i, n):
            return ps_t[:, i, 0:n]

        def wload(name, j, k0=0, kn=None):
            off, kc, nb = WOFF[name]
            if kn is None:
                kn = kc
            s = plW.next()
            c0 = off + j * kc * 128 + k0 * 128
            c1 = c0 + kn * 128
            rk = [("wbf", ci) for ci in range(c0 // CH, (c1 - 1) // CH + 1)]
            dma(wring[:, s, 0:kn * 128], wbf[:, c0:c1], rk, [("w", s)])
            return wring[:, s, 0:kn * 128].rearrange("p (k m) -> p k m", m=128), ("w", s)

        def lin(name, j, rhs, rkeys, n, pi=None, col=0):
            off, kc, nb = WOFF[name]
            wv, wk = wload(name, j)
            if pi is None:
                pi = plL.next()
            mmg(ps_t[:, pi, col:col + n], [(wv[:, k, :], rhs[:, k, 0:n]) for k in range(kc)], [wk] + rkeys, [("psL", pi)])
            return pi

        def rstd_of(src, skeys, n, nchunks=DC):
            sq = sqb[:, 0:nchunks, 0:n]
            act_fn(sq, src, AF.Square, skeys, [("sqb",)])
            s1 = plM.next()
            P.op("dve", lambda e: e.tensor_reduce(out=sm[:, s1, 0:n], in_=sq.rearrange("p c t -> p t c"),
                                                   axis=AX.X, op=ALU.add), [("sqb",)], [("sm", s1)])
            s3 = plM.next()
            hi = sm[:, s3, 0:n].bitcast(BF16)[:, 0:n]
            lo = sm[:, s3, 0:n].bitcast(BF16)[:, n:2 * n]
            cp("dve", hi, sm[:, s1, 0:n], [("sm", s1)], [("sm", s3)])
            tt("dve", sm[:, s1, 0:n], sm[:, s1, 0:n], hi, ALU.subtract, [("sm", s1), ("sm", s3)], [("sm", s1)])
            cp("dve", lo, sm[:, s1, 0:n], [("sm", s1), ("sm", s3)], [("sm", s3)])
            pi = plL.next()
            mmg(psL(pi, n), [(ones_b[:, :], hi), (ones_b[:, :], lo)], [("sm", s3), ("ones",)], [("psL", pi)])
            s2 = plR.next()
            ts("dve", rsb[:, s2, 0:n], psL(pi, n), 1.0 / D, ALU.mult, [("psL", pi)], [("rsb", s2)], s2=EPS, op1=ALU.add)
            act_fn(rsb[:, s2, 0:n], rsb[:, s2, 0:n], AF.Sqrt, [("rsb", s2)], [("rsb", s2)])
            P.op("dve", lambda e: e.reciprocal(out=rsb[:, s2, 0:n], in_=rsb[:, s2, 0:n]), [("rsb", s2)], [("rsb", s2)])
            return s2

        def prenorm(src, srckey, gname, n):
            keys = [(srckey, c) for c in range(DC)]
            s2 = rstd_of(src[:, :, 0:n], keys, n)
            g = spc(gname)
            for c in range(DC):
                stt_(xn[:, c, 0:n], src[:, c, 0:n], g[:, c:c + 1], rsb[:, s2, 0:n], ALU.mult, ALU.mult,
                     [(srckey, c), ("rsb", s2), ("spt",)], [("xn", c)])

        def postnorm_res(gname, factor, n):
            keys = [("yb", c) for c in range(DC)]
            s2 = rstd_of(yb[:, :, 0:n], keys, n)
            g = spc(gname)
            for c in range(DC):
                stt_(yb[:, c, 0:n], yb[:, c, 0:n], g[:, c:c + 1], rsb[:, s2, 0:n], ALU.mult, ALU.mult,
                     [("yb", c), ("rsb", s2), ("spt",)], [("yb", c)])
            for c in range(DC):
                stt_(h[:, c, 0:n], yb[:, c, 0:n], float(factor), h[:, c, 0:n], ALU.mult, ALU.add,
                     [("yb", c), ("h", c)], [("h", c)])

        def prenorm_next_stats(nn, src=None, kfn=None):
            if src is None:
                src = xpre
                kfn = lambda c: XPK
            a0, t1_, t2_ = 0, 1, 2
            tmps = [t1_, t2_]
            for c in range(DC):
                tsl = tmps[c % 2]
                dst = sm[:, a0, 0:nn] if c == 0 else sm[:, tsl, 0:nn]
                dk = ("sm", a0) if c == 0 else ("sm", tsl)
                act_fn(dst, src[:, c, 0:nn], AF.Square, kfn(c), [dk])
                if c > 0:
                    tt("dve", sm[:, a0, 0:nn], sm[:, a0, 0:nn], sm[:, tsl, 0:nn], ALU.add,
                       [("sm", a0), ("sm", tsl)], [("sm", a0)])
            hi = sm[:, t1_, 0:nn].bitcast(BF16)[:, 0:nn]
            lo = sm[:, t1_, 0:nn].bitcast(BF16)[:, nn:2 * nn]
            cp("dve", hi, sm[:, a0, 0:nn], [("sm", a0)], [("sm", t1_)])
            tt("dve", sm[:, a0, 0:nn], sm[:, a0, 0:nn], hi, ALU.subtract, [("sm", a0), ("sm", t1_)], [("sm", a0)])
            cp("dve", lo, sm[:, a0, 0:nn], [("sm", a0), ("sm", t1_)], [("sm", t1_)])
            return (hi, lo, t1_)

        def prenorm_next_rstd(st, nn):
            hi, lo, t1_ = st
            pi = plL.next()
            mmg(psL(pi, nn), [(ones_b[:, :], hi), (ones_b[:, :], lo)], [("sm", t1_), ("ones",)], [("psL", pi)])
            s2 = plR.next()
            ts("dve", rsb[:, s2, 0:nn], psL(pi, nn), 1.0 / D, ALU.mult, [("psL", pi)], [("rsb", s2)], s2=EPS, op1=ALU.add)
            act_fn(rsb[:, s2, 0:nn], rsb[:, s2, 0:nn], AF.Sqrt, [("rsb", s2)], [("rsb", s2)])
            P.op("dve", lambda e: e.reciprocal(out=rsb[:, s2, 0:nn], in_=rsb[:, s2, 0:nn]), [("rsb", s2)], [("rsb", s2)])
            return s2

        def prenorm_next_apply(s2, gname, nn):
            g = spc(gname)
            for c in range(DC):
                stt_(xn[:, c, 0:nn], xpre[:, c, 0:nn], g[:, c:c + 1], rsb[:, s2, 0:nn], ALU.mult, ALU.mult,
                     XPK + [("rsb", s2), ("spt",)], [("xn", c)])

        def ffn(pre, post, wg, wu, wd, n, skip_prenorm=False, hooks=None, defer_post=False):
            if not skip_prenorm:
                prenorm(h, "h", pre, n)
            xk = [("xn", c) for c in range(DC)]
            for j in range(FC):
                if hooks is not None and j in hooks:
                    hooks[j]()
                pg = lin(wg, j, xn, xk, n)
                lin(wu, j, xn, xk, n, pi=pg, col=256)
                s1 = plM.next()
                act_fn(sm[:, s1, 0:n], psL(pg, n), AF.Silu, [("psL", pg)], [("sm", s1)])
                tt("dve", act[:, j, 0:n], sm[:, s1, 0:n], ps_t[:, pg, 256:256 + n], ALU.mult,
                   [("sm", s1), ("psL", pg)], [("act", j)])
            if hooks is not None and "mid" in hooks:
                hooks["mid"]()
            for j in range(DC):
                pi = plL.next()
                parts = [(0, 16), (16, 16), (32, 12)]
                for pidx, (k0, kn) in enumerate(parts):
                    wv, wk = wload(wd, j, k0, kn)

                    def fn(e, wv=wv, k0=k0, kn=kn, pidx=pidx, pi=pi):
                        inst = None
                        for k in range(kn):
                            inst = e.matmul(psL(pi, n), lhsT=wv[:, k, :], rhs=act[:, k0 + k, 0:n],
                                            start=(pidx == 0 and k == 0), stop=(pidx == 2 and k == kn - 1))
                        return inst
                    P.op("pe", fn, [wk] + [("act", k0 + k) for k in range(kn)] + ([("psL", pi)] if pidx else []),
                         [("psL", pi)])
                cp("act", yb[:, j, 0:n], psL(pi, n), [("psL", pi)], [("yb", j)])
            if not defer_post:
                postnorm_res(post, 0.5, n)

        C1 = 6.28125
        C2 = 2.0 * math.pi - 6.28125

        def sin_of(dst, src, shift, tmpA, tmpI, skeys, dkey, akey, ikey):
            ts("dve", tmpA, src, float(shift), ALU.add, skeys, [akey], s2=1.0 / (2.0 * math.pi), op1=ALU.mult)
            cp("dve", tmpI, tmpA, [akey], [ikey])
            cp("dve", tmpA, tmpI, [ikey], [akey])
            ts("dve", dst, src, float(shift), ALU.add, skeys, [dkey])
            stt_(dst, tmpA, -C1, dst, ALU.mult, ALU.add, [akey, dkey], [dkey])
            stt_(dst, tmpA, -C2, dst, ALU.mult, ALU.add, [akey, dkey], [dkey])
            ts("dve", dst, dst, -math.pi, ALU.max, [dkey], [dkey], s2=math.pi, op1=ALU.min)
            act_fn(dst, dst, AF.Sin, [dkey], [dkey])

        P.op("dve", lambda e: e.memset(ones_f[:, :], 1.0), [], [("ones",)])
        P.op("dve", lambda e: e.memset(ones_b[:, :], 1.0), [], [("ones",)])
        P.op("pool", lambda e: e.memset(stt[:, :, :, :], 0.0), [], [("stt", 0), ("stt", 1)])
        dma(spt[:, :], spk, [], [("spt",)])
        dma(sbt[:, :, :, :], btd[:, SBT_OFF:SBT_OFF + 512].rearrange("p (a h q) -> p a h q", a=2, h=16), [], [("sbt",)])
        P.const.add(("ones",))
        P.const.add(("spt",))
        P.const.add(("sbt",))

        cengs = ["act", "dve", "pool"]

        def conv_in(ci):
            dma(stg_f[ci % NSF], wall[:, ci * CH:(ci + 1) * CH], [], [("stgf", ci % NSF)])
        for ci in range(min(NSF, NCHUNK)):
            conv_in(ci)
        for ci in range(NCHUNK):
            cp(cengs[ci % 3], stg_b[ci % NSB], stg_f[ci % NSF], [("stgf", ci % NSF)], [("stgb", ci % NSB)])
            dma(wbf[:, ci * CH:(ci + 1) * CH], stg_b[ci % NSB], [("stgb", ci % NSB)], [("wbf", ci)])
            if ci + NSF < NCHUNK:
                conv_in(ci + NSF)
        for ci in range(NCHUNK):
            P.const.add(("wbf", ci))
        P.op("act", lambda e: e.copy(out=sm[:, 1, 0:1], in_=ones_f[:, 0:1]), [("ones",)],
             [("stgf", i) for i in range(NSF)] + [("stgb", i) for i in range(NSB)] + [("pq", i) for i in range(10)]
             + [("yb", i) for i in range(DC)] + [("sm", 1)])

        if STOP >= 1:
            PI = math.pi
            ts("dve", rth[:, 0, :], spc("ldt"), 0.0, ALU.add, [("spt",)], [("rth",)])
            act_fn(rth[:, 0, :], rth[:, 0, :], AF.Exp, [("rth",)], [("rth",)])
            tt("dve", rth[:, 1, :], spc("are"), rth[:, 0, :], ALU.mult, [("rth",), ("spt",)], [("rth",)])
            act_fn(rth[:, 1, :], rth[:, 1, :], AF.Exp, [("rth",)], [("rth",)])
            tt("dve", rth[:, 2, :], spc("aim"), rth[:, 0, :], ALU.mult, [("rth",), ("spt",)], [("rth",)])
            jx = spc("jidx")
            for half in range(2):
                ph = pq[0].rearrange("p (a j) -> p a j", j=64)
                cs = pq[1].rearrange("p (a j) -> p a j", j=64)
                sn = pq[2].rearrange("p (a j) -> p a j", j=64)
                rr = pq[3].rearrange("p (a j) -> p a j", j=64)
                for a in range(16):
                    pr = half * 16 + a
                    ts("dve", ph[:, a, :], jx, rth[:, 2, pr:pr + 1], ALU.mult, [("rth",), ("spt",)], [("pq", 0)])
                sin_of(cs, ph, 0.5 * PI, pq[4].rearrange("p (a j) -> p a j", j=64), pq[5].bitcast(I32).rearrange("p (a j) -> p a j", j=64),
                       [("pq", 0)], ("pq", 1), ("pq", 4), ("pq", 5))
                sin_of(sn, ph, 0.0, pq[4].rearrange("p (a j) -> p a j", j=64), pq[5].bitcast(I32).rearrange("p (a j) -> p a j", j=64),
                       [("pq", 0)], ("pq", 2), ("pq", 4), ("pq", 5))
                cp("pool", rr, rth[:, 1, half * 16:(half + 1) * 16].unsqueeze(2).to_broadcast([128, 16, 64]),
                   [("rth",)], [("pq", 3)])
                P.op("pool", lambda e, rr=rr: e.memset(rr[:, :, 0:1], 0.0), [("pq", 3)], [("pq", 3)])
                dma(tabd[:, 0, half * 16:(half + 1) * 16, :], cs, [("pq", 1)], [("tabd",)])
                dma(tabd[:, 1, half * 16:(half + 1) * 16, :], sn, [("pq", 2)], [("tabd",)])
                dma(tabd[:, 2, half * 16:(half + 1) * 16, :], rr, [("pq", 3)], [("tabd",)])
            for qt in range(4):
                A_re, A_im, LDT, b_re, b_im = pq[0], pq[1], pq[2], pq[3], pq[4]
                t0, t1, t2, t3, t4 = pq[5], pq[6], pq[7], pq[8], pq[9]
                for i, dst in enumerate([A_re, A_im, LDT, b_re, b_im]):
                    dma(dst, ssmB[:, qt, i, :], [], [("pq", i)])
                K = lambda *ix: [("pq", i) for i in ix]
                act_fn(LDT, LDT, AF.Exp, K(2), K(2))
                tt("dve", t1, A_im, LDT, ALU.mult, K(1, 2), K(6))
                sin_of(t2, t1, 0.5 * PI, t0, t4.bitcast(I32), K(6), ("pq", 7), ("pq", 5), ("pq", 9))
                sin_of(t3, t1, 0.0, t0, t4.bitcast(I32), K(6), ("pq", 8), ("pq", 5), ("pq", 9))
                tt("dve", t0, A_re, LDT, ALU.mult, K(0, 2), K(5))
                act_fn(t0, t0, AF.Exp, K(5), K(5))
                tt("dve", t2, t2, t0, ALU.mult, K(7, 5), K(7))
                tt("dve", t3, t3, t0, ALU.mult, K(8, 5), K(8))
                ts("dve", t2, t2, -1.0, ALU.add, K(7), K(7))
                tt("dve", t0, A_re, A_re, ALU.mult, K(0), K(5))
                tt("dve", t1, A_im, A_im, ALU.mult, K(1), K(6))
                tt("dve", t0, t0, t1, ALU.add, K(5, 6), K(5))
                P.op("dve", lambda e, t0=t0: e.reciprocal(out=t0, in_=t0), K(5), K(5))
                tt("dve", t1, t2, A_re, ALU.mult, K(7, 0), K(6))
                tt("dve", t4, t3, A_im, ALU.mult, K(8, 1), K(9))
                tt("dve", t1, t1, t4, ALU.add, K(6, 9), K(6))
                tt("dve", t1, t1, t0, ALU.mult, K(6, 5), K(6))
                tt("dve", t4, t3, A_re, ALU.mult, K(8, 0), K(9))
                tt("dve", t3, t2, A_im, ALU.mult, K(7, 1), K(8))
                tt("dve", t4, t4, t3, ALU.subtract, K(9, 8), K(9))
                tt("dve", t4, t4, t0, ALU.mult, K(9, 5), K(9))
                tt("dve", t0, t1, b_re, ALU.mult, K(6, 3), K(5))
                tt("dve", t2, t4, b_im, ALU.mult, K(9, 4), K(7))
                tt("dve", t0, t0, t2, ALU.subtract, K(5, 7), K(5))
                tt("dve", t2, t1, b_im, ALU.mult, K(6, 4), K(7))
                tt("dve", t3, t4, b_re, ALU.mult, K(9, 3), K(8))
                tt("dve", t2, t2, t3, ALU.add, K(7, 8), K(7))
                bst = pq[1].bitcast(BF16)[:, 0:2048].rearrange("p (a r m) -> p a r m", a=8, r=2)
                cp("pool", bst[:, :, 0, :], t0.rearrange("p (a m) -> p a m", a=8), K(5, 1), K(1))
                cp("pool", bst[:, :, 1, :], t2.rearrange("p (a m) -> p a m", a=8), K(7, 1), K(1))
                dma(ssmw[:, qt, 0, :], bst.rearrange("p a r m -> p (a r m)"), K(1), [("ssmw", qt, 0)])
                cst_f = pq[3].bitcast(F32)
                cf = R[:, (3 * 4096) // 2:(5 * 4096) // 2].bitcast(F32)
                dma(cf, ssmC[:, qt, :], K(5, 7), K(3, 4))
                cfv = cf.rearrange("p (r a m) -> p r a m", r=2, a=8)
                cbt = pq[6].bitcast(BF16)[:, 0:2048].rearrange("p (a r m) -> p a r m", a=8, r=2)
                cp("pool", cbt[:, :, 0, :], cfv[:, 0, :, :], K(3, 4, 6), K(6))
                ts("dve", cbt[:, :, 1, :], cfv[:, 1, :, :], -1.0, ALU.mult, K(3, 4, 6), K(6))
                dma(ssmw[:, qt, 1, :], cbt.rearrange("p a r m -> p (a r m)"), K(6), [("ssmw", qt, 1)])
            for qt in range(4):
                P.const.add(("ssmw", qt, 0))
                P.const.add(("ssmw", qt, 1))
            P.const.add(("tabd",))
        def ssm(n, L, nseg, st_views, final_out, fillers, nfill):
            plL.n = 3
            plL.i = 0
            plW.base, plW.n = 4, NSLOT - 4
            psbk = [("psL", 6), ("psL", 7)]
            nsteps = 2 * nseg
            per_pt = (nfill + 4 * nsteps - 1) // (4 * nsteps)

            def fill():
                for _ in range(per_pt):
                    next(fillers, None)
            for hf in range(2):
                p16 = slice(hf * 16, hf * 16 + 16)
                Bv, Cv, wkB, wkC = [], [], [], []
                for qi in range(2):
                    qt = 2 * hf + qi
                    sB = 2 * qi
                    dma(wring[:, sB, :], ssmw[:, qt, 0, :], [("ssmw", qt, 0)], [("w", sB)])
                    sC = 2 * qi + 1
                    dma(wring[:, sC, :], ssmw[:, qt, 1, :], [("ssmw", qt, 1)], [("w", sC)])
                    Bv.append(wring[:, sB, :].rearrange("p (a m) -> p a m", m=128))
                    Cv.append(wring[:, sC, :].rearrange("p (a m) -> p a m", m=128))
                    wkB.append(("w", sB))
                    wkC.append(("w", sC))
                dma(tabq[:, :, :, :], tabd[:, :, p16, :], [("tabd",)], [("tabq",)])
                if L != 64:
                    cp("pool", rt16[:, :, :], rth[:, 1, p16].unsqueeze(2).to_broadcast([128, 16, L]), [("rth",)], [("rt16",)])
                    P.op("pool", lambda e: e.memset(rt16[:, :, 0:1], 0.0), [("rt16",)], [("rt16",)])
                Ct = tabq[:, 0, :, 0:L]
                St = tabq[:, 1, :, 0:L]
                if L == 64:
                    Rt2 = tabq[:, 2, :, :].rearrange("p a j -> p (a j)")
                    rtk = ("tabq",)
                else:
                    Rt2 = rt16[:, :, :].rearrange("p a j -> p (a j)")
                    rtk = ("rt16",)
                for seg in range(nseg):
                    tok = slice(seg * L, (seg + 1) * L)
                    zre, zim, zk = st_views[seg]
                    tv = [sw[:, i, 0:16 * L].rearrange("p (a j) -> p a j", a=16) for i in range(6)]
                    for sbt_ in range(2):
                        p8 = slice(8 * sbt_, 8 * sbt_ + 8)
                        psb = ps_t[:, 6:8, :].rearrange("p a b -> p (a b)")[:, 0:16 * L]
                        psb4 = psb.rearrange("p (a r j) -> p a r j", a=8, r=2)
                        items = []
                        for pq_ in range(8):
                            pp = 8 * sbt_ + pq_
                            ch = 4 * hf + pp // 4
                            for ri in range(2):
                                items.append((psb4[:, pq_, ri, :], Bv[pp // 8][:, (pp % 8) * 2 + ri, :], usb[:, ch, tok]))
                        mms(items, wkB + [("usb", 4 * hf + i) for i in range(4)], psbk)
                        bre = psb4[:, :, 0, :]
                        bim = psb4[:, :, 1, :]
                        sk = lambda i: ("sw", i, sbt_)
                        tt("dve", tv[0][:, p8, :], Ct[:, p8, :], bre, ALU.mult, [("tabq",)] + psbk, [sk(0)])
                        tt("dve", tv[1][:, p8, :], St[:, p8, :], bim, ALU.mult, [("tabq",)] + psbk, [sk(1)])
                        tt("dve", tv[2][:, p8, :], Ct[:, p8, :], bim, ALU.mult, [("tabq",)] + psbk, [sk(2)])
                        tt("dve", tv[3][:, p8, :], St[:, p8, :], bre, ALU.mult, [("tabq",)] + psbk, [sk(3)])
                    SW = lambda i: [("sw", i, 0), ("sw", i, 1)]
                    tt("pool", tv[4], tv[0], tv[1], ALU.add, SW(0) + SW(1), SW(4))
                    tt("pool", tv[5], tv[2], tv[3], ALU.subtract, SW(2) + SW(3), SW(5))
                    fill()
                    s1 = plM.next()
                    tt("dve", sm[:, s1, 0:16], rth[:, 1, p16], zre[:, p16], ALU.mult, [("rth",), zk], [("sm", s1)])
                    tt("dve", sm[:, s1, 16:32], rth[:, 1, p16], zim[:, p16], ALU.mult, [("rth",), zk], [("sm", s1)])
                    tt("dve", tv[4][:, :, 0:1], tv[4][:, :, 0:1], sm[:, s1, 0:16].unsqueeze(2), ALU.add,
                       SW(4) + [("sm", s1)], SW(4))
                    tt("dve", tv[5][:, :, 0:1], tv[5][:, :, 0:1], sm[:, s1, 16:32].unsqueeze(2), ALU.add,
                       SW(5) + [("sm", s1)], SW(5))
                    w4 = sw[:, 4, 0:16 * L]
                    w5 = sw[:, 5, 0:16 * L]
                    P.op("dve", lambda e, w4=w4, Rt2=Rt2: e.tensor_tensor_scan(out=w4, data0=Rt2, data1=w4, initial=0.0,
                                                                           op0=ALU.mult, op1=ALU.add),
                         SW(4) + [rtk], SW(4))
                    P.op("dve", lambda e, w5=w5, Rt2=Rt2: e.tensor_tensor_scan(out=w5, data0=Rt2, data1=w5, initial=0.0,
                                                                           op0=ALU.mult, op1=ALU.add),
                         SW(5) + [rtk], SW(5))
                    fill()
                    tt("dve", tv[0], Ct, tv[4], ALU.mult, [("tabq",)] + SW(4), SW(0))
                    tt("dve", tv[1], St, tv[5], ALU.mult, [("tabq",)] + SW(5), SW(1))
                    tt("pool", tv[2], Ct, tv[5], ALU.mult, [("tabq",)] + SW(5), SW(2))
                    tt("pool", tv[3], St, tv[4], ALU.mult, [("tabq",)] + SW(4), SW(3))
                    sre = ssb[:, :, 0, 0:L]
                    sim = ssb[:, :, 1, 0:L]
                    tt("dve", sre, tv[0], tv[1], ALU.subtract, SW(0) + SW(1), [("ssb",)])
                    tt("pool", sim, tv[2], tv[3], ALU.add, SW(2) + SW(3), [("ssb",)])
                    tt("dve", zre[:, p16].unsqueeze(2), tv[0][:, :, L - 1:L], tv[1][:, :, L - 1:L], ALU.subtract,
                       SW(0) + SW(1), [zk])
                    tt("dve", zim[:, p16].unsqueeze(2), tv[2][:, :, L - 1:L], tv[3][:, :, L - 1:L], ALU.add,
                       SW(2) + SW(3), [zk])
                    fill()
                    dc = spc("dcol")
                    for fc in range(4):
                        ch = 4 * hf + fc
                        pc = plL.next()
                        prs = []
                        for pp in range(4 * fc, 4 * fc + 4):
                            for ri in range(2):
                                prs.append((Cv[pp // 8][:, (pp % 8) * 2 + ri, :], ssb[:, pp, ri, 0:L]))
                        mmg(ps_t[:, pc, 0:L], prs, wkC + [("ssb",)], [("psL", pc)])
                        stt_(usf[:, ch, tok], usf[:, ch, tok], dc[:, ch:ch + 1], ps_t[:, pc, 0:L], ALU.mult, ALU.add,
                             [("usf", ch), ("psL", pc), ("spt",)], [("usf", ch)])
                    fill()
            for _ in fillers:
                pass
            plL.n = 5
            plW.base, plW.n = 0, NSLOT
            if final_out is not None:
                final_out()
            for c in range(8):
                s1 = plM.next()
                a = sm[:, s1, 0:n]
                y = usf[:, c, 0:n]
                tt("dve", a, y, y, ALU.mult, [("usf", c)], [("sm", s1)])
                ts("dve", a, a, 0.044715, ALU.mult, [("sm", s1)], [("sm", s1)], s2=1.0, op1=ALU.add)
                tt("dve", a, a, y, ALU.mult, [("sm", s1), ("usf", c)], [("sm", s1)])
                act_fn(a, a, AF.Sigmoid, [("sm", s1)], [("sm", s1)], scale=2.0 * math.sqrt(2.0 / math.pi))
                tt("dve", y, y, a, ALU.mult, [("sm", s1), ("usf", c)], [("usf", c)])
                cp("pool", usb[:, c, 0:n], y, [("usf", c)], [("usb", c)])
            bg = spc("bglu")
            for j in range(8):
                pi = lin("glu", j, usb, [("usb", c) for c in range(8)], n)
                s1 = plM.next()
                act_fn(sm[:, s1, 0:n], psL(pi, n), AF.Sigmoid, [("psL", pi), ("spt",)], [("sm", s1)], bias=bg[:, j:j + 1])
                tt("dve", os_[:, j, 0:n], usf[:, j, 0:n], sm[:, s1, 0:n], ALU.mult, [("usf", j), ("sm", s1)], [("os", j)])

        def memattn(n, tok):
            for _ in memattn_gen(n, tok):
                pass

        def memattn_gen(n, tok):
            for hm in range(4):
                yield
                pms = []
                for mb in range(2):
                    pi = plL.next()
                    mmg(psL(pi, n), [(mkT[:, 2 * hm + dcc, mb * 128:(mb + 1) * 128], qT[:, 2 * hm + dcc, tok])
                                     for dcc in range(2)],
                        [("mkT",), ("q", 2 * hm), ("q", 2 * hm + 1)], [("psL", pi)])
                    s1 = plM.next()
                    pm = sm[:, s1, 0:n].bitcast(BF16)[:, 0:n]
                    act_fn(pm, psL(pi, n), AF.Exp, [("psL", pi)], [("sm", s1)])
                    pms.append((pm, s1))
                pd = plL.next()
                mmg(psL(pd, n), [(ones_b[:, :], pm) for pm, _ in pms], [("sm", s) for _, s in pms] + [("ones",)], [("psL", pd)])
                s2 = plM.next()
                P.op("dve", lambda e, s2=s2, pd=pd: e.reciprocal(out=sm[:, s2, 0:n], in_=psL(pd, n)), [("psL", pd)], [("sm", s2)])
                for dcc in range(2):
                    po = plL.next()
                    c = 2 * hm + dcc
                    mmg(psL(po, n), [(mvb[:, mb, c * 128:(c + 1) * 128], pms[mb][0]) for mb in range(2)],
                        [("mvb",)] + [("sm", s) for _, s in pms], [("psL", po)])
                    tt("dve", om[:, c, tok], psL(po, n), sm[:, s2, 0:n], ALU.mult, [("psL", po), ("sm", s2)], [("om", c)])

        def attn_S(c, e2, qcols, nq, blocks):
            prow = slice(e2 * 64, e2 * 64 + 64)
            sbk = plS.next()
            items = []
            allk = []
            for bi_, (kfn, vl, nk, bfn, keys) in enumerate(blocks):
                items.append((ps_t[0:nk, sbk, bi_ * 64:bi_ * 64 + nq], kfn(prow), qT[prow, c, qcols]))
                allk += keys
            mms(items, allk + [("q", c)], [("psL", sbk)])
            return sbk

        def attn_rest(c, e2, qcols, nq, blocks, okey, sbk):
            chv = spc("ch")
            hh = 2 * c + e2
            prow = slice(e2 * 64, e2 * 64 + 64)
            pg = plPG.next()
            nb_ = len(blocks)
            merge_ok = (nq == 64) and all(bl[2] == 128 for bl in blocks)
            runs = []
            for bi_, (kfn, vl, nk, bfn, keys) in enumerate(blocks):
                bt = bfn(e2) if bfn is not None else None
                typ = None if bt is None else (bt[2] if len(bt) > 2 else -100 - bi_)
                if runs and merge_ok:
                    r = runs[-1]
                    if (r["typ0"] is None and typ is None) or \
                       (r["typ0"] is not None and typ is not None and typ == r["typ0"] + r["n"] and typ >= 0):
                        r["n"] += 1
                        continue
                runs.append({"b0": bi_, "n": 1, "typ0": typ, "bt": bt, "nk": nk})
            pts = []
            for r in runs:
                b0, nr, nk = r["b0"], r["n"], r["nk"]
                if merge_ok:
                    ps = ps_t[:, sbk, b0 * 64:(b0 + nr) * 64]
                    pv = pT[:, pg * 5 + b0:pg * 5 + b0 + nr, :].rearrange("p a q -> p (a q)")
                else:
                    ps = ps_t[0:nk, sbk, b0 * 64:b0 * 64 + nq]
                    pv = pT[0:nk, pg * 5 + b0, 0:nq]
                if r["typ0"] is None:
                    act_fn(pv, ps, AF.Exp, [("psL", sbk), ("spt",)], [("pTg", pg)], bias=chv[0:nk, hh:hh + 1])
                else:
                    bap, bkey = r["bt"][0], r["bt"][1]
                    s1 = plM.next()
                    if merge_ok:
                        bap = r["bt"][3](r["typ0"], nr)
                        tmpv = sm[:, s1, 0:nr * 64]
                    else:
                        tmpv = sm[0:nk, s1, 0:nq]
                    tt("dve", tmpv, ps, bap, ALU.add, [("psL", sbk), bkey], [("sm", s1)])
                    act_fn(pv, tmpv, AF.Exp, [("sm", s1)], [("pTg", pg)])
            for bi_, (kfn, vl, nk, bfn, keys) in enumerate(blocks):
                pts.append((pT[0:nk, pg * 5 + bi_, 0:nq], vl, nk, keys))
            ob = plL.next()
            mmg(ps_t[:, ob, 0:nq], [(vl, pv) for (pv, vl, nk, keys) in pts],
                [("pTg", pg)] + sum([k for (_, _, _, k) in pts], []), [("psL", ob)])
            mmg(ps_t[:, ob, 64:64 + nq], [(ones_b[0:nk, :], pv) for (pv, vl, nk, keys) in pts],
                [("pTg", pg), ("ones",)], [("psL", ob)])
            s2 = plM.next()
            P.op("dve", lambda e, s2=s2, ob=ob, prow=prow: e.reciprocal(out=sm[prow, s2, 0:nq], in_=ps_t[prow, ob, 64:64 + nq]),
                 [("psL", ob)], [("sm", s2)])
            tt("dve", oa[prow, c, qcols], ps_t[prow, ob, 0:nq], sm[prow, s2, 0:nq], ALU.mult,
               [("psL", ob), ("sm", s2)], [okey])

        LA = 2

        def attn_gen(work, pre=None):
            sbs = {}
            nxt = 0
            for k in range(len(work)):
                if pre is not None:
                    pre(k)
                while nxt < len(work) and nxt <= k + LA:
                    sbs[nxt] = attn_S(*work[nxt][0:5])
                    nxt += 1
                attn_rest(*work[k], sbs.pop(k))
                yield

        def attn_run(work, pre=None):
            for _ in attn_gen(work, pre):
                pass

        def tile_body(n, mode, ti, pre_done=False, nxt=None, dhooks=None):
            xk = [("xn", c) for c in range(DC)]
            if dhooks is not None:
                plM.base, plM.n = 3, 5
            ffn("g_f1pre", "g_f1post", "f1g", "f1u", "f1d", n, skip_prenorm=pre_done, hooks=dhooks)
            if TSTOP < 4:
                return
            prenorm(h, "h", "g_mpre", n)
            for c in range(8):
                pi = lin("win", c, xn, xk, n)
                cp("act", usf[:, c, 0:n], psL(pi, n), [("psL", pi)], [("usf", c)])
                cp("dve", usb[:, c, 0:n], psL(pi, n), [("psL", pi)], [("usb", c)])

            def pre_gen():
                need_kv_out = (mode == "sample") or ti >= NT - 2
                kf = yb[:, 0:8, 0:n]
                vf = yb[:, 8:16, :].rearrange("p c t -> p (c t)")
                if mode == "prompt":
                    kslots = [(2 * ti) % 6, (2 * ti + 1) % 6]
                for c in range(8):
                    pi = lin("win", 16 + c, xn, xk, n)
                    if mode == "prompt":
                        for tb in range(2):
                            cp("act", kring[:, c, kslots[tb], :], psL(pi, n)[:, tb * 128:(tb + 1) * 128], [("psL", pi)],
                               [("kr", c, kslots[tb])])
                    else:
                        cp("act", kring[:, c, 4, 0:16], psL(pi, n)[:, 0:16], [("psL", pi)], [("kr", c, 4)])
                        cp("act", kring[:, c, 5, 0:16], psL(pi, n)[:, 16:32], [("psL", pi)], [("kr", c, 5)])
                    if need_kv_out:
                        cp("dve", kf[:, c, :], psL(pi, n), [("psL", pi)], [("yb", c)])
                    yield
                if need_kv_out:
                    if mode == "prompt":
                        t0 = (ti - (NT - 2)) * T
                        dma(okT.rearrange("(c p) t -> p c t", p=128)[:, :, t0:t0 + T], kf, [("yb", c) for c in range(8)], [("okT", ti)])
                    else:
                        dma(oskT.rearrange("(c p) t -> p c t", p=128), kf, [("yb", c) for c in range(8)], [("oskT",)])
                for c in range(8):
                    wv, wk = wload("win", 24 + c)
                    if mode == "prompt":
                        for tb in range(2):
                            pi = plL.next()
                            mmg(psL(pi, 128), [(xn[:, k, tb * 128:(tb + 1) * 128], wv[:, k, :]) for k in range(16)], [wk] + xk, [("psL", pi)])
                            cp("act", vring[:, kslots[tb], c * 128:(c + 1) * 128], psL(pi, 128), [("psL", pi)], [("vr", kslots[tb], c)])
                            if need_kv_out:
                                cp("dve", vf[:, tb * 1024 + c * 128: tb * 1024 + (c + 1) * 128], psL(pi, 128), [("psL", pi)], [("yb", 8 + tb * 4 + c // 2)])
                    else:
                        for b2 in range(2):
                            pi = plL.next()
                            mmg(psL(pi, 128)[0:16, :], [(xn[:, k, b2 * 16:(b2 + 1) * 16], wv[:, k, :]) for k in range(16)], [wk] + xk, [("psL", pi)])
                            cp("act", vring[0:16, 4 + b2, c * 128:(c + 1) * 128], psL(pi, 128)[0:16, :], [("psL", pi)], [("vr", 4 + b2, c)])
                            cp("dve", vf[0:16, b2 * 1024 + c * 128: b2 * 1024 + (c + 1) * 128], psL(pi, 128)[0:16, :], [("psL", pi)], [("yb", 8 + b2 * 4 + c // 2)])
                    yield
                if need_kv_out:
                    vkeys = [("yb", 8 + i) for i in range(8)]
                    if mode == "prompt":
                        t0 = (ti - (NT - 2)) * T
                        dma(ov[t0:t0 + T, :].rearrange("(b p) f -> p b f", p=128), vf.rearrange("p (b f) -> p b f", b=2), vkeys, [("ov", ti)])
                    else:
                        dma(osv.rearrange("b p f -> p b f"), vf[0:16, :].rearrange("p (b f) -> p b f", b=2), vkeys, [("osv",)])
                for c in range(8):
                    pi = lin("win", 8 + c, xn, xk, n)
                    ts("dve", qT[:, c, 0:n], psL(pi, n), 0.125, ALU.mult, [("psL", pi)], [("q", c)])
                    yield
                if mode == "prompt":
                    work = []
                    for c in range(8):
                        bb = c % 2
                        for qi in range(4):
                            qc = 4 * ti + qi
                            par = qc % 2
                            blocks = []
                            for b in range(5):
                                gb = qc // 2 - 4 + b
                                if gb < 0:
                                    continue
                                sl = gb % 6
                                typ = {(0, 3): 0, (0, 4): 1, (1, 0): 2, (1, 3): 3, (1, 4): 4}.get((par, b))
                                kfn = (lambda prow, sl=sl, c=c: kring[prow, c, sl, :])
                                vl = vring[:, sl, c * 128:(c + 1) * 128]
                                if typ is None:
                                    bfn = None
                                else:
                                    bfn = (lambda e2, typ=typ, bb=bb: (btl[:, bb, e2, typ, :], ("btl", bb), typ,
                                                                      (lambda t0, nr, e2=e2, bb=bb: btl[:, bb, e2, t0:t0 + nr, :].rearrange("p a q -> p (a q)"))))
                                blocks.append((kfn, vl, 128, bfn, [("kr", c, sl), ("vr", sl, c)]))
                            for e2 in range(2):
                                work.append((c, e2, slice(qi * 64, qi * 64 + 64), 64, blocks, ("oa", c)))
                    def pre(k):
                        if k % 8 == 0:
                            c_ = k // 8
                            bb_ = c_ % 2
                            dma(btl[:, bb_, :, :, :],
                                btd[:, (2 * c_) * 320:(2 * c_ + 2) * 320].rearrange("p (h t q) -> p h t q", h=2, t=5),
                                [], [("btl", bb_)])
                    yield from attn_gen(work, pre)
                else:
                    for b2 in range(2):
                        st = yb[:, :, :].rearrange("p c t -> p (c t)")
                        ybk = [("yb", i) for i in range(16)]
                        dma(st.rearrange("p (c t) -> p c t", c=8), ckT[b2].rearrange("(c p) t -> p c t", p=128), [], ybk)
                        for c in range(8):
                            cp("pool", kring[:, c, 0:4, :], st[:, c * 512:(c + 1) * 512].rearrange("p (s k) -> p s k", s=4), ybk,
                               [("kr", c, s) for s in range(4)])
                        dma(st.rearrange("p (s f) -> p s f", s=4), cv[b2].rearrange("(s p) f -> p s f", p=128), [], ybk)
                        for s in range(4):
                            cp("pool", vring[:, s, :], st[:, s * 1024:(s + 1) * 1024], ybk, [("vr", s, c) for c in range(8)])
                        swork = []
                        for c in range(8):
                            blocks = []
                            for b in range(4):
                                kfn = (lambda prow, b=b, c=c: kring[prow, c, b, :])
                                bfn = None if b < 3 else (lambda e2, c=c: (sbt[:, 0, 2 * c + e2, :], ("sbt",)))
                                blocks.append((kfn, vring[:, b, c * 128:(c + 1) * 128], 128, bfn, [("kr", c, b), ("vr", b, c)]))
                            kfn = (lambda prow, b2=b2, c=c: kring[prow, c, 4 + b2, 0:16])
                            blocks.append((kfn, vring[0:16, 4 + b2, c * 128:(c + 1) * 128], 16,
                                           (lambda e2, c=c: (sbt[0:16, 1, 2 * c + e2, :], ("sbt",))), [("kr", c, 4 + b2), ("vr", 4 + b2, c)]))
                            swork.append((c, 0, slice(b2 * 16, b2 * 16 + 16), 16, blocks, ("oa", c)))
                            swork.append((c, 1, slice(b2 * 16, b2 * 16 + 16), 16, blocks, ("oa", c)))
                        attn_run(swork)
                for c in range(8):
                    pi = lin("win", 32 + c, xn, xk, n)
                    ts("dve", qT[:, c, 0:n], psL(pi, n), 0.0625, ALU.mult, [("psL", pi)], [("q", c)])
                    yield
                if mode == "prompt":
                    yield from memattn_gen(n, slice(0, n))
                else:
                    for b2 in range(2):
                        st = yb[:, :, :].rearrange("p c t -> p (c t)")
                        ybk = [("yb", i) for i in range(16)]
                        dma(st[:, 0:2048].rearrange("p (c t) -> p c t", c=8), cmkT[b2].rearrange("(c p) t -> p c t", p=128), [], ybk)
                        cp("pool", mkT[:, :, :], st[:, 0:2048].rearrange("p (c t) -> p c t", c=8), ybk, [("mkT",)])
                        dma(st[:, 2048:4096].rearrange("p (s f) -> p s f", s=2), cmv[b2].rearrange("(s p) f -> p s f", p=128), [], ybk)
                        cp("pool", mvb[:, :, :], st[:, 2048:4096].rearrange("p (s f) -> p s f", s=2), ybk, [("mvb",)])
                        memattn(16, slice(b2 * 16, b2 * 16 + 16))
            def mk_unit(j):
                def unit():
                    tms = []
                    for bi, wn, src, skey in ((1, "ba", oa, "oa"), (2, "bm", om, "om")):
                        pb_ = lin(wn, j, src, [(skey, c) for c in range(8)], n)
                        lin("win", 40 + 16 * bi + j, xn, xk, n, pi=pb_, col=256)
                        s1 = plM.next()
                        s2_ = plM.next()
                        act_fn(sm[:, s1, 0:n], ps_t[:, pb_, 256:256 + n], AF.Sigmoid, [("psL", pb_)], [("sm", s1)])
                        cp("act", sm[:, s2_, 0:n], psL(pb_, n), [("psL", pb_)], [("sm", s2_)])
                        tt("pool", sm[:, s1, 0:n], sm[:, s1, 0:n], sm[:, s2_, 0:n], ALU.mult, [("sm", s1), ("sm", s2_)], [("sm", s1)])
                        tms.append(s1)
                    tt("pool", yb[:, j, 0:n], sm[:, tms[0], 0:n], sm[:, tms[1], 0:n], ALU.add,
                       [("sm", tms[0]), ("sm", tms[1])], [("yb", j)])
                return unit
            def all_gen():
                yield from pre_gen()
                if TSTOP >= 7:
                    for j in range(DC):
                        mk_unit(j)()
                        yield
            fillers = all_gen()
            if mode != "prompt":
                for _ in fillers:
                    pass
            if mode == "prompt":
                stv = [(stt[:, 0, 0, :], stt[:, 0, 1, :], ("stt", 0))] * 4
                fo = None
                if ti == NT - 1:
                    fo = lambda: dma(ost, stt[:, 0, :, :], [("stt", 0)], [("ost",)])
                ssm(n, 64, 4, stv, fo, fillers, 116)
            else:
                s0 = spc("s0").rearrange("p (b r a) -> p b r a", b=2, r=2)
                cp("pool", stt[:, :, :, :], s0, [("spt",), ("stt", 0), ("stt", 1)], [("stt", 0), ("stt", 1)])
                stv = [(stt[:, b2, 0, :], stt[:, b2, 1, :], ("stt", b2)) for b2 in range(2)]
                ssm(n, 16, 2, stv, lambda: dma(osst, stt[:, :, :, :], [("stt", 0), ("stt", 1)], [("osst",)]), fillers, 0)
            if TSTOP < 7:
                return
            for j in range(DC):
                pb_ = lin("bs", j, os_, [("os", c) for c in range(8)], n)
                lin("win", 40 + j, xn, xk, n, pi=pb_, col=256)
                s1 = plM.next()
                act_fn(sm[:, s1, 0:n], ps_t[:, pb_, 256:256 + n], AF.Sigmoid, [("psL", pb_)], [("sm", s1)])
                tt("dve", sm[:, s1, 0:n], sm[:, s1, 0:n], psL(pb_, n), ALU.mult, [("sm", s1), ("psL", pb_)], [("sm", s1)])
                tt("pool", mg[:, j, 0:n], sm[:, s1, 0:n], yb[:, j, 0:n], ALU.add, [("sm", s1), ("yb", j)], [("mg", j)])
            for j in range(DC):
                pi = lin("wo", j, mg, [("mg", c) for c in range(DC)], n)
                cp("act", yb[:, j, 0:n], psL(pi, n), [("psL", pi)], [("yb", j)])
            postnorm_res("g_mpost", 1.0, n)
            if TSTOP < 8:
                return
            hooks = None
            if nxt is not None:
                src_next, nn = nxt
                dma(xpre[:, :, 0:nn], src_next, [], XPK)
                stt_box = {}

                def h_stats():
                    stt_box["st"] = prenorm_next_stats(nn)

                def h_rstd():
                    stt_box["s2"] = prenorm_next_rstd(stt_box["st"], nn)
                    plM.base, plM.n = 0, 8

                def h_apply():
                    prenorm_next_apply(stt_box["s2"], "g_f1pre", nn)
                hooks = {8: h_stats, 30: h_rstd, "mid": h_apply}
                plM.base, plM.n = 3, 5
            ffn("g_f2pre", "g_f2post", "f2g", "f2u", "f2d", n, hooks=hooks, defer_post=(nxt is not None))

        P.const.add(("rth",))
        P.op("act", lambda e: e.copy(out=sm[:, 0, 0:1], in_=ones_f[:, 0:1]), [("ones",)],
             [("pq", i) for i in range(10)] + [("stgf", i) for i in range(NSF)] + [("stgb", i) for i in range(NSB)]
             + [("yb", i) for i in range(DC)] + [("sm", 0), ("sqb",)])
        hk = [("h", c) for c in range(DC)]
        if STOP >= 2:
            dma(h[:, :, :], memT.rearrange("(c p) t -> p c t", p=128), [], hk)
            prenorm(h, "h", "g_mem", 256)
            xk_ = [("xn", c) for c in range(DC)]
            for c in range(8 if KSUB >= 1 else 0):
                pi = lin("mk", c, xn, xk_, 256)
                if KSUB2 >= 1:
                    cp("act", mkT[:, c, :], psL(pi, 256), [("psL", pi)], [("mkTc", c)])
                if KSUB2 >= 2:
                    cp("dve", yb[:, c, :], psL(pi, 256), [("psL", pi)], [("yb", c)])
            if KSUB2 >= 3:
                dma(omkT.rearrange("(c p) t -> p c t", p=128), yb[:, 0:8, :], [("yb", c) for c in range(8)], [("omkT",)])
            vfm = yb[:, 8:16, :].rearrange("p c t -> p (c t)")
            for c in range(8 if KSUB >= 2 else 0):
                wv, wk = wload("mv", c)
                for mb in range(2):
                    pi = plL.next()
                    mmg(psL(pi, 128), [(xn[:, k, mb * 128:(mb + 1) * 128], wv[:, k, :]) for k in range(16)], [wk] + xk_, [("psL", pi)])
                    cp("act", mvb[:, mb, c * 128:(c + 1) * 128], psL(pi, 128), [("psL", pi)], [("mvb",)])
                    cp("dve", vfm[:, mb * 1024 + c * 128: mb * 1024 + (c + 1) * 128], psL(pi, 128), [("psL", pi)], [("yb", 8 + mb * 4 + c // 2)])
            dma(omv.rearrange("(b p) f -> p b f", p=128), vfm.rearrange("p (b f) -> p b f", b=2), [("yb", 8 + i) for i in range(8)], [("omv",)])

        xTv = xT.rearrange("(c p) t -> p c t", p=128)
        yTv = yT.rearrange("(c p) t -> p c t", p=128)
        ntl = min(NT, NTILES) if STOP >= 3 else 0
        xsv = xsT.rearrange("(c p) t -> p c t", p=128)
        do_sample = STOP >= 10
        pre_done = False
        dhooks = None

        def make_deferred(store_fn, n_prev, n_cur):
            box = {}

            def d_stats():
                box["st"] = prenorm_next_stats(n_prev, yb, lambda c: [("yb", c)])

            def d_rstd():
                box["s2"] = prenorm_next_rstd(box["st"], n_prev)
                plM.base, plM.n = 0, 8

            def d_apply():
                s2 = box["s2"]
                g = spc("g_f2post")
                for c in range(DC):
                    stt_(yb[:, c, 0:n_prev], yb[:, c, 0:n_prev], g[:, c:c + 1], rsb[:, s2, 0:n_prev], ALU.mult, ALU.mult,
                         [("yb", c), ("rsb", s2), ("spt",)], [("yb", c)])
                for c in range(DC):
                    stt_(h[:, c, 0:n_prev], yb[:, c, 0:n_prev], 0.5, h[:, c, 0:n_prev], ALU.mult, ALU.add,
                         [("yb", c), ("h", c)], [("h", c)])

            def d_store():
                store_fn()
                cp("pool", h[:, :, 0:n_cur], xpre[:, :, 0:n_cur], XPK, hk)
            return {2: d_stats, 14: d_rstd, 20: d_apply, "mid": d_store}

        for ti in range(ntl):
            P.epoch = 1 + ti // EPOCH_TILES
            if not pre_done:
                dma(h[:, :, :], xTv[:, :, ti * T:(ti + 1) * T], [], hk)
            if ti + 1 < ntl:
                nxt = (xTv[:, :, (ti + 1) * T:(ti + 2) * T], T)
            elif do_sample and TSTOP >= 99:
                nxt = (xsv, TS)
            else:
                nxt = None
            if TSTOP < 99:
                nxt = None
            tile_body(T, "prompt", ti, pre_done, nxt, dhooks)
            store_fn = (lambda ti=ti: dma(yTv[:, :, ti * T:(ti + 1) * T], h[:, :, :], hk, [("yT", ti)]))
            if nxt is not None:
                dhooks = make_deferred(store_fn, T, nxt[1])
                pre_done = True
            else:
                store_fn()
                dhooks = None
                pre_done = False
        P.epoch += 1
        if do_sample:
            if not pre_done:
                dma(h[:, :, 0:TS], xsv, [], hk)
            tile_body(TS, "sample", 0, pre_done, None, dhooks)
            dma(ysT.rearrange("(c p) t -> p c t", p=128), h[:, :, 0:TS], hk, [("ysT",)])
        elif dhooks is not None:
            raise RuntimeError("deferred work pending")

        P.assign()
        semnames = {}
        sems = {}
        for e in ("pe", "act", "dve", "pool"):
            sems[e] = [es.enter_context(nc.semaphore(f"s_{e}_{i}")) for i in range(P.nepoch)]
        dsems = [es.enter_context(nc.semaphore(f"s_d_{i}")) for i in range(NDS)]
        block = es.enter_context(nc.Block())

        @block.sync
        def _(e):
            P.emit("sp", e, sems, dsems)

        @block.tensor
        def _(e):
            P.emit("pe", e, sems, dsems)

        @block.scalar
        def _(e):
            P.emit("act", e, sems, dsems)

        @block.vector
        def _(e):
            P.emit("dve", e, sems, dsems)

        @block.gpsimd
        def _(e):
            P.emit("pool", e, sems, dsems)
    return nc


def _tile_w(w, kc):
    K, N = w.shape
    nb = N // 128
    a = w.reshape(kc, 128, nb, 128).transpose(1, 2, 0, 3)
    return a.reshape(128, nb * kc * 128)


def _gcol(g, nch):
    return np.ascontiguousarray(g.reshape(nch, 128).T)


_NC_CACHE = {}


def kernel(**inp):
    f = lambda k: np.asarray(inp[k], dtype=np.float32)
    ws = {"f1g": f("ffn1_w_gate")[0], "f1u": f("ffn1_w_up")[0], "f1d": f("ffn1_w_down")[0], "win": f("w_in")[0],
          "glu": f("ssm_w_glu")[0], "bs": f("w_branch_ssm")[0], "ba": f("w_branch_att")[0], "bm": f("w_branch_mem")[0],
          "wo": f("w_out")[0], "f2g": f("ffn2_w_gate")[0], "f2u": f("ffn2_w_up")[0], "f2d": f("ffn2_w_down")[0],
          "mk": f("w_mem_k")[0], "mv": f("w_mem_v")[0]}
    wall = np.empty((128, WX), np.float32)
    for n, kc, nb in WTAB:
        o = WOFF[n][0]
        wall[:, o:o + kc * nb * 128] = _tile_w(ws[n], kc)

    a_re = f("ssm_a_re")[0]; a_im = f("ssm_a_im")[0]; ldt = f("ssm_log_dt")[0]
    b_re = f("ssm_b_re")[0]; b_im = f("ssm_b_im")[0]; c_re = f("ssm_c_re")[0]; c_im = f("ssm_c_im")[0]
    G = 64
    gidx = (2 * np.arange(32)[None, :] + (np.arange(128)[:, None] // 64))
    pidx = np.broadcast_to((np.arange(128) % 64)[:, None], (128, 32))
    are_m = a_re[gidx, pidx]; aim_m = a_im[gidx, pidx]; ldt_m = ldt[gidx]
    ssmB = np.zeros((128, 4, 5, 8, 128), np.float32)
    ssmC = np.zeros((128, 4, 2, 8, 128), np.float32)
    for pair in range(32):
        qt, pp = divmod(pair, 8)
        for gp in range(2):
            g = 2 * pair + gp
            gl = g % 8
            ms = slice(gp * 64, gp * 64 + 64)
            ssmB[:, qt, 0, pp, ms] = a_re[g][None, :]
            ssmB[:, qt, 1, pp, ms] = a_im[g][None, :]
            ssmB[:, qt, 2, pp, ms] = ldt[g]
            ssmB[gl * 16:(gl + 1) * 16, qt, 3, pp, ms] = b_re[g].T
            ssmB[gl * 16:(gl + 1) * 16, qt, 4, pp, ms] = b_im[g].T
            ssmC[ms, qt, 0, pp, gl * 16:(gl + 1) * 16] = c_re[g].T
            ssmC[ms, qt, 1, pp, gl * 16:(gl + 1) * 16] = c_im[g].T
    ssmB = ssmB.reshape(128, 4, 5, 1024)
    ssmC = ssmC.reshape(128, 4, 2048)

    rb = f("att_rel_bias")[0]
    btd = np.zeros((128, BT_TOT), np.float32)
    kk = np.arange(128)[:, None]; qq = np.arange(64)[None, :]
    types = [(0, 3), (0, 4), (1, 0), (1, 3), (1, 4)]
    bt = np.zeros((128, 16, 5, 64), np.float32)
    for t_, (par, b) in enumerate(types):
        rel = 64 * par + 128 * (4 - b) + qq - kk
        dcn = -par - 8 + 2 * b + kk // 64 + 0 * qq
        ok = (dcn >= -8) & (dcn <= 0)
        idx = np.clip(rel, -128, 128) + 128
        for hh in range(16):
            bt[:, hh, t_, :] = np.where(ok, rb[hh][idx], np.float32(NEG))
    btd[:, 0:BTW] = bt.reshape(128, BTW)
    q16 = np.arange(16)[None, :]
    rel3 = 128 + q16 - kk
    s3 = np.stack([rb[hh][np.clip(rel3, -128, 128) + 128] for hh in range(16)], 1)
    k16 = np.arange(16)[:, None]
    reln = q16 - k16
    sn = np.zeros((128, 16, 16), np.float32)
    sn[0:16] = np.stack([rb[hh][np.clip(reln, -128, 128) + 128] for hh in range(16)], 1)
    btd[:, SBT_OFF:SBT_OFF + 256] = s3.reshape(128, 256)
    btd[:, SBT_OFF + 256:SBT_OFF + 512] = sn.reshape(128, 256)

    def spack(s0=None):
        a = np.zeros((128, SPW), np.float32)

        def put(name, v):
            o, w_ = SPC[name]
            a[:, o:o + w_] = v
        put("g_f1pre", _gcol(f("ffn1_norm_pre")[0], 16)); put("g_f1post", _gcol(f("ffn1_norm_post")[0], 16))
        put("g_mpre", _gcol(f("mix_norm_pre")[0], 16)); put("g_mpost", _gcol(f("mix_norm_post")[0], 16))
        put("g_f2pre", _gcol(f("ffn2_norm_pre")[0], 16)); put("g_f2post", _gcol(f("ffn2_norm_post")[0], 16))
        put("g_mem", _gcol(f("mem_norm")[0], 16))
        put("dcol", _gcol(f("ssm_d")[0].reshape(-1), 8)); put("bglu", _gcol(f("ssm_b_glu")[0], 8))
        put("ch", np.broadcast_to(rb[:, 256][None, :], (128, 16)))
        put("are", are_m); put("aim", aim_m); put("ldt", ldt_m)
        put("jidx", np.broadcast_to(np.arange(1, 65, dtype=np.float32)[None, :], (128, 64)))
        if s0 is not None:
            put("s0", s0)
        return a

    xp = f("x_prompt"); xs = f("x_sample"); mp = f("mem_prompt")
    ck = f("cache_att_k")[0]; cvv = f("cache_att_v")[0]; cmk = f("cache_mem_k")[0]; cmvv = f("cache_mem_v")[0]
    sre = f("state_ssm_re")[0]; sim = f("state_ssm_im")[0]
    in_maps = []
    for c in range(8):
        s0 = np.zeros((128, 2, 2, 32), np.float32)
        for b2 in range(2):
            s0[:, b2, 0, :] = sre[2 * c + b2][gidx, pidx]
            s0[:, b2, 1, :] = sim[2 * c + b2][gidx, pidx]
        in_maps.append({
            "xT": np.ascontiguousarray(xp[c].T),
            "xsT": np.ascontiguousarray(xs[2 * c:2 * c + 2].reshape(TS, D).T),
            "memT": np.ascontiguousarray(mp[c].T),
            "ckT": np.ascontiguousarray(ck[2 * c:2 * c + 2].reshape(2, 512, 1024).transpose(0, 2, 1)),
            "cv": np.ascontiguousarray(cvv[2 * c:2 * c + 2].reshape(2, 512, 1024)),
            "cmkT": np.ascontiguousarray(cmk[2 * c:2 * c + 2].reshape(2, 256, 1024).transpose(0, 2, 1)),
            "cmv": np.ascontiguousarray(cmvv[2 * c:2 * c + 2].reshape(2, 256, 1024)),
            "wall": wall, "spk": spack(s0.reshape(128, 128)), "ssmB": ssmB, "ssmC": ssmC, "btd": btd,
        })
    if "nc" not in _NC_CACHE:
        _NC_CACHE["nc"] = build_nc()
    if os.environ.get('KTRACE'):
        res = run_bass_kernel_spmd(_NC_CACHE["nc"], in_maps[:NCORES], core_ids=list(range(NCORES)), trace=True)
        print("EXEC_TIME_NS", res.exec_time_ns)
    else:
        res = run_bass_kernel_spmd(_NC_CACHE["nc"], in_maps[:NCORES], core_ids=list(range(NCORES)))
    R_ = list(res.results)
    while len(R_) < 8:
        R_.append({k: np.zeros_like(v) for k, v in R_[0].items()})

    def unstate(a):
        o = np.zeros((64, 64), np.float32)
        o[gidx, pidx] = a
        return o
    y_p = np.stack([R_[c]["yT"].T for c in range(8)])
    y_s = np.concatenate([R_[c]["ysT"].T.reshape(2, 16, D) for c in range(8)])
    akp = np.stack([R_[c]["okT"].T.reshape(512, 16, 64) for c in range(8)])[None]
    avp = np.stack([R_[c]["ov"].reshape(512, 16, 64) for c in range(8)])[None]
    mkp = np.stack([R_[c]["omkT"].T.reshape(256, 4, 256) for c in range(8)])[None]
    mvp = np.stack([R_[c]["omv"].reshape(256, 4, 256) for c in range(8)])[None]
    srp = np.stack([unstate(R_[c]["ost"][:, 0, :]) for c in range(8)])[None]
    sip = np.stack([unstate(R_[c]["ost"][:, 1, :]) for c in range(8)])[None]
    aks = np.concatenate([R_[c]["oskT"].T.reshape(2, 16, 16, 64) for c in range(8)])[None]
    avs = np.concatenate([R_[c]["osv"].reshape(2, 16, 16, 64) for c in range(8)])[None]
    srs = np.stack([unstate(R_[c]["osst"][:, b2, 0, :]) for c in range(8) for b2 in range(2)])[None]
    sis = np.stack([unstate(R_[c]["osst"][:, b2, 1, :]) for c in range(8) for b2 in range(2)])[None]
    outs = (y_p, y_s, akp, avp, mkp, mvp, srp, sip, aks, avs, srs, sis)
    return tuple(np.ascontiguousarray(o, dtype=np.float32) for o in outs)
```

```python
import contextlib
import math
import os
import numpy as np
import concourse.bass as bass
import concourse.mybir as mybir
from concourse.bass_utils import run_bass_kernel_spmd

F32 = mybir.dt.float32
BF16 = mybir.dt.bfloat16
I32 = mybir.dt.int32
AF = mybir.ActivationFunctionType
ALU = mybir.AluOpType
AX = mybir.AxisListType

D = 2048
DC = 16
FC = 44
T = 256
NT = 16
SEQ = 4096
TS = 32
EPS = 1e-6
NEG = -1e30
NDS = 12
NSLOT = 8
SAME_SYNC = True
EPOCH_TILES = 3
STOP = int(os.environ.get('KSTOP', '99'))
NTILES = int(os.environ.get('KTILES', '16'))
NCORES = int(os.environ.get('KCORES', '8'))
KSUB = int(os.environ.get('KSUB', '99'))
KSUB2 = int(os.environ.get('KSUB2', '99'))
TSTOP = int(os.environ.get('KTSTOP', '99'))

WTAB = [("f1g", 16, 44), ("f1u", 16, 44), ("f1d", 44, 16), ("win", 16, 88), ("glu", 8, 8),
        ("bs", 8, 16), ("ba", 8, 16), ("bm", 8, 16), ("wo", 16, 16),
        ("f2g", 16, 44), ("f2u", 16, 44), ("f2d", 44, 16), ("mk", 16, 8), ("mv", 16, 8)]
WOFF = {}
_o = 0
for _n, _kc, _nb in WTAB:
    WOFF[_n] = (_o, _kc, _nb)
    _o += _kc * _nb * 128
WX = _o
CH = 2048
NCHUNK = WX // CH

SPC = {}
_o = 0
for _n, _w in [("g_f1pre", 16), ("g_f1post", 16), ("g_mpre", 16), ("g_mpost", 16), ("g_f2pre", 16),
               ("g_f2post", 16), ("g_mem", 16), ("dcol", 8), ("bglu", 8), ("ch", 16),
               ("are", 32), ("aim", 32), ("ldt", 32), ("s0", 128), ("jidx", 64)]:
    SPC[_n] = (_o, _w)
    _o += _w
SPW = _o
BTW = 16 * 5 * 64
SBT_OFF = BTW
BT_TOT = BTW + 16 * 16 + 16 * 16


class Op:
    __slots__ = ("eng", "fn", "deps", "idx", "sig", "sigval", "epoch", "dsem")


class Prog:
    ENG = ("pe", "act", "dve", "pool", "sp")

    def __init__(self):
        self.ops = {e: [] for e in self.ENG}
        self.lastw = {}
        self.rd = {}
        self.const = set()
        self.epoch = 0

    def op(self, eng, fn, reads=(), writes=()):
        o = Op()
        o.eng = eng
        o.fn = fn
        o.sig = False
        o.epoch = self.epoch
        deps = {}
        ex = [k for k in reads if k[0] in ("psL", "psB")]
        if ex:
            writes = list(writes) + ex
            reads = [k for k in reads if k[0] not in ("psL", "psB")]

        def add(d):
            if d is None:
                return
            if d.eng == "sp":
                deps[("sp", d.idx)] = d
            else:
                k = d.eng
                if k not in deps or deps[k].idx < d.idx:
                    deps[k] = d

        for k in reads:
            add(self.lastw.get(k))
        for k in writes:
            add(self.lastw.get(k))
            for r in self.rd.get(k, ()):
                add(r)
        if eng == "pe":
            deps.pop("pe", None)
        elif eng != "sp" and not SAME_SYNC:
            deps.pop(eng, None)
        o.deps = list(deps.values())
        for d in o.deps:
            d.sig = True
        o.idx = len(self.ops[eng])
        self.ops[eng].append(o)
        for k in writes:
            self.lastw[k] = o
            self.rd[k] = []
        for k in reads:
            if k not in self.const:
                self.rd.setdefault(k, []).append(o)
        return o

    def assign(self):
        self.nepoch = self.epoch + 1
        for e in self.ENG:
            if e == "sp":
                for o in self.ops[e]:
                    o.dsem = o.idx % NDS
                    o.sigval = 16 * (o.idx // NDS + 1)
            else:
                cnt = {}
                for o in self.ops[e]:
                    if o.sig:
                        cnt[o.epoch] = cnt.get(o.epoch, 0) + 1
                        o.sigval = cnt[o.epoch]

    def emit(self, e, eng, sems, dsems):
        waited = {}

        def w(key, sem, val):
            if waited.get(key, 0) < val:
                eng.wait_ge(sem, val)
                waited[key] = val

        for o in self.ops[e]:
            if e == "sp" and o.idx >= NDS:
                p = self.ops["sp"][o.idx - NDS]
                w(("d", p.dsem), dsems[p.dsem], p.sigval)
            for d in o.deps:
                if d.eng == "sp":
                    w(("d", d.dsem), dsems[d.dsem], d.sigval)
                else:
                    w((d.eng, d.epoch), sems[d.eng][d.epoch], d.sigval)
            inst = o.fn(eng)
            if e == "sp":
                inst.then_inc(dsems[o.dsem], 16)
            elif o.sig:
                inst.then_inc(sems[e][o.epoch], 1)
        if e == "sp":
            n = len(self.ops["sp"])
            for i in range(max(0, n - NDS), n):
                p = self.ops["sp"][i]
                w(("d", p.dsem), dsems[p.dsem], p.sigval)


def build_nc():
    nc = bass.Bass("TRN2", target_bir_lowering=False)
    P = Prog()

    def din(name, shape, dt=F32):
        return nc.dram_tensor(name, list(shape), dt, kind="ExternalInput").ap()

    def dout(name, shape, dt=F32):
        return nc.dram_tensor(name, list(shape), dt, kind="ExternalOutput").ap()

    xT = din("xT", [D, SEQ])
    xsT = din("xsT", [D, TS])
    memT = din("memT", [D, 256])
    ckT = din("ckT", [2, 1024, 512])
    cv = din("cv", [2, 512, 1024])
    cmkT = din("cmkT", [2, 1024, 256])
    cmv = din("cmv", [2, 256, 1024])
    wall = din("wall", [128, WX])
    spk = din("spk", [128, SPW])
    ssmB = din("ssmB", [128, 4, 5, 1024])
    ssmC = din("ssmC", [128, 4, 2048])
    btd = din("btd", [128, BT_TOT])
    yT = dout("yT", [D, SEQ])
    ysT = dout("ysT", [D, TS])
    okT = dout("okT", [1024, 512])
    ov = dout("ov", [512, 1024])
    omkT = dout("omkT", [1024, 256])
    omv = dout("omv", [256, 1024])
    ost = dout("ost", [128, 2, 32])
    oskT = dout("oskT", [1024, TS])
    osv = dout("osv", [2, 16, 1024])
    osst = dout("osst", [128, 2, 2, 32])
    wbf = nc.dram_tensor("wbf", [128, WX], BF16, kind="Internal").ap()
    ssmw = nc.dram_tensor("ssmw", [128, 4, 2, 2048], BF16, kind="Internal").ap()
    tabd = nc.dram_tensor("tabd", [128, 3, 32, 64], F32, kind="Internal").ap()

    es = contextlib.ExitStack()
    with es:
        def sb(name, shape, dt):
            return es.enter_context(nc.sbuf_tensor(name, list(shape), dt))

        h = sb("h", [128, DC, T], F32)
        xn = sb("xn", [128, DC, T], BF16)
        yb = sb("yb", [128, DC, T], F32)
        R = sb("R", [128, 20480], BF16)
        kring = sb("kring", [128, 8, 6, 128], BF16)
        vring = sb("vring", [128, 6, 1024], BF16)
        mkT = sb("mkT", [128, 8, 256], BF16)
        mvb = sb("mvb", [128, 2, 1024], BF16)
        btl = sb("btl", [128, 2, 2, 5, 64], F32)
        sbt = sb("sbt", [128, 2, 16, 16], F32)
        tabq = sb("tabq", [128, 3, 16, 64], F32)
        rt16 = sb("rt16", [128, 16, 16], F32)
        sw = sb("sw", [128, 6, 1024], F32)
        ssb = sb("ssb", [128, 16, 2, 64], BF16)
        wring = sb("wring", [128, NSLOT, 2048], BF16)
        pT = sb("pT", [128, 20, 64], BF16)
        ones_f = sb("ones_f", [128, 128], F32)
        ones_b = sb("ones_b", [128, 128], BF16)
        spt = sb("spt", [128, SPW], F32)
        sm = sb("sm", [128, 8, T], F32)
        rsb = sb("rsb", [128, 2, T], F32)
        stt = sb("stt", [128, 2, 2, 32], F32)
        rth = sb("rth", [128, 3, 32], F32)
        ps_t = es.enter_context(nc.psum_tensor("ps", [128, 8, 512], F32))

        if os.environ.get('KVERB'):
            print("SBUF bytes remaining per partition:", nc.sbuf_bytes_remaining)
        def rview(off_bytes, dt, n, pat=None, **kw):
            eb = 2
            a = R[:, off_bytes // eb: off_bytes // eb + (n * (4 if dt == F32 else 2)) // eb]
            if dt == F32:
                a = a.bitcast(F32)
            if pat:
                a = a.rearrange(pat, **kw)
            return a

        act = rview(0, BF16, FC * T, "p (c t) -> p c t", c=FC)
        sqb = rview(0, F32, DC * T, "p (c t) -> p c t", c=DC)
        qT = rview(0, BF16, 8 * T, "p (c t) -> p c t", c=8)
        usf = rview(4096, F32, 8 * T, "p (c t) -> p c t", c=8)
        usb = rview(12288, BF16, 8 * T, "p (c t) -> p c t", c=8)
        oa = rview(16384, BF16, 8 * T, "p (c t) -> p c t", c=8)
        om = rview(20480, BF16, 8 * T, "p (c t) -> p c t", c=8)
        os_ = rview(24576, BF16, 8 * T, "p (c t) -> p c t", c=8)
        mg = rview(28672, BF16, DC * T, "p (c t) -> p c t", c=DC)
        NSF, NSB = 5, 4
        stg_f = [rview(i * 8192, F32, CH) for i in range(NSF)]
        ybflat = yb[:, :, :].rearrange("p c t -> p (c t)").bitcast(BF16)
        stg_b = [ybflat[:, i * CH:(i + 1) * CH] for i in range(NSB)]
        pq = [rview(i * 4096, F32, 1024) for i in range(10)]

        def spc(name):
            o, w_ = SPC[name]
            return spt[:, o:o + w_]

        def dma(out, in_, reads, writes):
            return P.op("sp", lambda e: e.dma_start(out=out, in_=in_), reads, writes)

        def act_fn(out, in_, func, reads, writes, bias=None, scale=None):
            kw = {}
            if bias is not None:
                kw["bias"] = bias
            if scale is not None:
                kw["scale"] = scale
            return P.op("act", lambda e: e.activation(out=out, in_=in_, func=func, **kw), reads, writes)

        def tt(eng, out, a, b, op, reads, writes):
            return P.op(eng, lambda e: e.tensor_tensor(out=out, in0=a, in1=b, op=op), reads, writes)

        def ts(eng, out, a, s1, op0, reads, writes, s2=None, op1=None):
            if op1 is None:
                return P.op(eng, lambda e: e.tensor_scalar(out=out, in0=a, scalar1=s1, scalar2=None, op0=op0), reads, writes)
            return P.op(eng, lambda e: e.tensor_scalar(out=out, in0=a, scalar1=s1, scalar2=s2, op0=op0, op1=op1), reads, writes)

        def stt_(out, a, s, b, op0, op1, reads, writes):
            return P.op("dve", lambda e: e.scalar_tensor_tensor(out=out, in0=a, scalar=s, in1=b, op0=op0, op1=op1), reads, writes)

        def cp(eng, out, in_, reads, writes):
            if eng == "act":
                return P.op("act", lambda e: e.copy(out=out, in_=in_), reads, writes)
            return P.op(eng, lambda e: e.tensor_copy(out=out, in_=in_), reads, writes)

        def mmg(out, pairs, reads, writes):
            def fn(e):
                n = len(pairs)
                inst = None
                for i, (l, r) in enumerate(pairs):
                    inst = e.matmul(out, lhsT=l, rhs=r, start=(i == 0), stop=(i == n - 1))
                return inst
            return P.op("pe", fn, reads, writes)

        def mms(items, reads, writes):
            def fn(e):
                inst = None
                for (o, l, r) in items:
                    inst = e.matmul(o, lhsT=l, rhs=r, start=True, stop=True)
                return inst
            return P.op("pe", fn, reads, writes)

        class Pool_:
            def __init__(self, name, n):
                self.name, self.n, self.i, self.base = name, n, 0, 0

            def next(self):
                k = self.base + self.i % self.n
                self.i += 1
                return k

        plL = Pool_("pb", 4)
        plL.base = 4
        plS = Pool_("pS", 2)
        plO = Pool_("pO", 2)
        plO.base = 2
        plW = Pool_("w", NSLOT)
        plPG = Pool_("pTg", 4)
        plM = Pool_("sm", 8)
        plR = Pool_("rsb", 2)

        def bk(i):
            return ps_t[:, i, :]

        def pbk(i):
            return ("pb", i)

        def psL(i, n):
            return ps_t[:, i, 0:n]

        def wload(name, j, k0=0, kn=None):
            off, kc, nb = WOFF[name]
            if kn is None:
                kn = kc
            s = plW.next()
            c0 = off + j * kc * 128 + k0 * 128
            c1 = c0 + kn * 128
            rk = [("wbf", ci) for ci in range(c0 // CH, (c1 - 1) // CH + 1)]
            dma(wring[:, s, 0:kn * 128], wbf[:, c0:c1], rk, [("w", s)])
            return wring[:, s, 0:kn * 128].rearrange("p (k m) -> p k m", m=128), ("w", s)

        def lin(name, j, rhs, rkeys, n, pi=None, col=0):
            off, kc, nb = WOFF[name]
            wv, wk = wload(name, j)
            if pi is None:
                pi = plL.next()
            mmg(ps_t[:, pi, col:col + n], [(wv[:, k, :], rhs[:, k, 0:n]) for k in range(kc)], [wk] + rkeys, [("psL", pi)])
            return pi

        def rstd_of(src, skeys, n, nchunks=DC):
            sq = sqb[:, 0:nchunks, 0:n]
            act_fn(sq, src, AF.Square, skeys, [("sqb",)])
            s1 = plM.next()
            P.op("dve", lambda e: e.tensor_reduce(out=sm[:, s1, 0:n], in_=sq.rearrange("p c t -> p t c"),
                                                   axis=AX.X, op=ALU.add), [("sqb",)], [("sm", s1)])
            s3 = plM.next()
            hi = sm[:, s3, 0:n].bitcast(BF16)[:, 0:n]
            lo = sm[:, s3, 0:n].bitcast(BF16)[:, n:2 * n]
            cp("dve", hi, sm[:, s1, 0:n], [("sm", s1)], [("sm", s3)])
            tt("dve", sm[:, s1, 0:n], sm[:, s1, 0:n], hi, ALU.subtract, [("sm", s1), ("sm", s3)], [("sm", s1)])
            cp("dve", lo, sm[:, s1, 0:n], [("sm", s1), ("sm", s3)], [("sm", s3)])
            pi = plL.next()
            mmg(psL(pi, n), [(ones_b[:, :], hi), (ones_b[:, :], lo)], [("sm", s3), ("ones",)], [("psL", pi)])
            s2 = plR.next()
            ts("dve", rsb[:, s2, 0:n], psL(pi, n), 1.0 / D, ALU.mult, [("psL", pi)], [("rsb", s2)], s2=EPS, op1=ALU.add)
            act_fn(rsb[:, s2, 0:n], rsb[:, s2, 0:n], AF.Sqrt, [("rsb", s2)], [("rsb", s2)])
            P.op("dve", lambda e: e.reciprocal(out=rsb[:, s2, 0:n], in_=rsb[:, s2, 0:n]), [("rsb", s2)], [("rsb", s2)])
            return s2

        def prenorm(src, srckey, gname, n):
            keys = [(srckey, c) for c in range(DC)]
            s2 = rstd_of(src[:, :, 0:n], keys, n)
            g = spc(gname)
            for c in range(DC):
                stt_(xn[:, c, 0:n], src[:, c, 0:n], g[:, c:c + 1], rsb[:, s2, 0:n], ALU.mult, ALU.mult,
                     [(srckey, c), ("rsb", s2), ("spt",)], [("xn", c)])

        def postnorm_res(gname, factor, n):
            keys = [("yb", c) for c in range(DC)]
            s2 = rstd_of(yb[:, :, 0:n], keys, n)
            g = spc(gname)
            for c in range(DC):
                stt_(yb[:, c, 0:n], yb[:, c, 0:n], g[:, c:c + 1], rsb[:, s2, 0:n], ALU.mult, ALU.mult,
                     [("yb", c), ("rsb", s2), ("spt",)], [("yb", c)])
            for c in range(DC):
                stt_(h[:, c, 0:n], yb[:, c, 0:n], float(factor), h[:, c, 0:n], ALU.mult, ALU.add,
                     [("yb", c), ("h", c)], [("h", c)])

        def ffn(pre, post, wg, wu, wd, n):
            prenorm(h, "h", pre, n)
            xk = [("xn", c) for c in range(DC)]
            for j in range(FC):
                pg = lin(wg, j, xn, xk, n)
                lin(wu, j, xn, xk, n, pi=pg, col=256)
                s1 = plM.next()
                act_fn(sm[:, s1, 0:n], psL(pg, n), AF.Silu, [("psL", pg)], [("sm", s1)])
                tt("dve", act[:, j, 0:n], sm[:, s1, 0:n], ps_t[:, pg, 256:256 + n], ALU.mult,
                   [("sm", s1), ("psL", pg)], [("act", j)])
            for j in range(DC):
                pi = plL.next()
                parts = [(0, 16), (16, 16), (32, 12)]
                for pidx, (k0, kn) in enumerate(parts):
                    wv, wk = wload(wd, j, k0, kn)

                    def fn(e, wv=wv, k0=k0, kn=kn, pidx=pidx, pi=pi):
                        inst = None
                        for k in range(kn):
                            inst = e.matmul(psL(pi, n), lhsT=wv[:, k, :], rhs=act[:, k0 + k, 0:n],
                                            start=(pidx == 0 and k == 0), stop=(pidx == 2 and k == kn - 1))
                        return inst
                    P.op("pe", fn, [wk] + [("act", k0 + k) for k in range(kn)] + ([("psL", pi)] if pidx else []),
                         [("psL", pi)])
                cp("act", yb[:, j, 0:n], psL(pi, n), [("psL", pi)], [("yb", j)])
            postnorm_res(post, 0.5, n)

        C1 = 6.28125
        C2 = 2.0 * math.pi - 6.28125

        def sin_of(dst, src, shift, tmpA, tmpI, skeys, dkey, akey, ikey):
            ts("dve", tmpA, src, float(shift), ALU.add, skeys, [akey], s2=1.0 / (2.0 * math.pi), op1=ALU.mult)
            cp("dve", tmpI, tmpA, [akey], [ikey])
            cp("dve", tmpA, tmpI, [ikey], [akey])
            ts("dve", dst, src, float(shift), ALU.add, skeys, [dkey])
            stt_(dst, tmpA, -C1, dst, ALU.mult, ALU.add, [akey, dkey], [dkey])
            stt_(dst, tmpA, -C2, dst, ALU.mult, ALU.add, [akey, dkey], [dkey])
            ts("dve", dst, dst, -math.pi, ALU.max, [dkey], [dkey], s2=math.pi, op1=ALU.min)
            act_fn(dst, dst, AF.Sin, [dkey], [dkey])

        P.op("dve", lambda e: e.memset(ones_f[:, :], 1.0), [], [("ones",)])
        P.op("dve", lambda e: e.memset(ones_b[:, :], 1.0), [], [("ones",)])
        P.op("pool", lambda e: e.memset(stt[:, :, :, :], 0.0), [], [("stt", 0), ("stt", 1)])
        dma(spt[:, :], spk, [], [("spt",)])
        dma(sbt[:, :, :, :], btd[:, SBT_OFF:SBT_OFF + 512].rearrange("p (a h q) -> p a h q", a=2, h=16), [], [("sbt",)])
        P.const.add(("ones",))
        P.const.add(("spt",))
        P.const.add(("sbt",))

        cengs = ["act", "dve", "pool"]

        def conv_in(ci):
            dma(stg_f[ci % NSF], wall[:, ci * CH:(ci + 1) * CH], [], [("stgf", ci % NSF)])
        for ci in range(min(NSF, NCHUNK)):
            conv_in(ci)
        for ci in range(NCHUNK):
            cp(cengs[ci % 3], stg_b[ci % NSB], stg_f[ci % NSF], [("stgf", ci % NSF)], [("stgb", ci % NSB)])
            dma(wbf[:, ci * CH:(ci + 1) * CH], stg_b[ci % NSB], [("stgb", ci % NSB)], [("wbf", ci)])
            if ci + NSF < NCHUNK:
                conv_in(ci + NSF)
        for ci in range(NCHUNK):
            P.const.add(("wbf", ci))
        P.op("act", lambda e: e.copy(out=sm[:, 1, 0:1], in_=ones_f[:, 0:1]), [("ones",)],
             [("stgf", i) for i in range(NSF)] + [("stgb", i) for i in range(NSB)] + [("pq", i) for i in range(10)]
             + [("yb", i) for i in range(DC)] + [("sm", 1)])

        if STOP >= 1:
            PI = math.pi
            ts("dve", rth[:, 0, :], spc("ldt"), 0.0, ALU.add, [("spt",)], [("rth",)])
            act_fn(rth[:, 0, :], rth[:, 0, :], AF.Exp, [("rth",)], [("rth",)])
            tt("dve", rth[:, 1, :], spc("are"), rth[:, 0, :], ALU.mult, [("rth",), ("spt",)], [("rth",)])
            act_fn(rth[:, 1, :], rth[:, 1, :], AF.Exp, [("rth",)], [("rth",)])
            tt("dve", rth[:, 2, :], spc("aim"), rth[:, 0, :], ALU.mult, [("rth",), ("spt",)], [("rth",)])
            jx = spc("jidx")
            for half in range(2):
                ph = pq[0].rearrange("p (a j) -> p a j", j=64)
                cs = pq[1].rearrange("p (a j) -> p a j", j=64)
                sn = pq[2].rearrange("p (a j) -> p a j", j=64)
                rr = pq[3].rearrange("p (a j) -> p a j", j=64)
                for a in range(16):
                    pr = half * 16 + a
                    ts("dve", ph[:, a, :], jx, rth[:, 2, pr:pr + 1], ALU.mult, [("rth",), ("spt",)], [("pq", 0)])
                sin_of(cs, ph, 0.5 * PI, pq[4].rearrange("p (a j) -> p a j", j=64), pq[5].bitcast(I32).rearrange("p (a j) -> p a j", j=64),
                       [("pq", 0)], ("pq", 1), ("pq", 4), ("pq", 5))
                sin_of(sn, ph, 0.0, pq[4].rearrange("p (a j) -> p a j", j=64), pq[5].bitcast(I32).rearrange("p (a j) -> p a j", j=64),
                       [("pq", 0)], ("pq", 2), ("pq", 4), ("pq", 5))
                cp("pool", rr, rth[:, 1, half * 16:(half + 1) * 16].unsqueeze(2).to_broadcast([128, 16, 64]),
                   [("rth",)], [("pq", 3)])
                P.op("pool", lambda e, rr=rr: e.memset(rr[:, :, 0:1], 0.0), [("pq", 3)], [("pq", 3)])
                dma(tabd[:, 0, half * 16:(half + 1) * 16, :], cs, [("pq", 1)], [("tabd",)])
                dma(tabd[:, 1, half * 16:(half + 1) * 16, :], sn, [("pq", 2)], [("tabd",)])
                dma(tabd[:, 2, half * 16:(half + 1) * 16, :], rr, [("pq", 3)], [("tabd",)])
            for qt in range(4):
                A_re, A_im, LDT, b_re, b_im = pq[0], pq[1], pq[2], pq[3], pq[4]
                t0, t1, t2, t3, t4 = pq[5], pq[6], pq[7], pq[8], pq[9]
                for i, dst in enumerate([A_re, A_im, LDT, b_re, b_im]):
                    dma(dst, ssmB[:, qt, i, :], [], [("pq", i)])
                K = lambda *ix: [("pq", i) for i in ix]
                act_fn(LDT, LDT, AF.Exp, K(2), K(2))
                tt("dve", t1, A_im, LDT, ALU.mult, K(1, 2), K(6))
                sin_of(t2, t1, 0.5 * PI, t0, t4.bitcast(I32), K(6), ("pq", 7), ("pq", 5), ("pq", 9))
                sin_of(t3, t1, 0.0, t0, t4.bitcast(I32), K(6), ("pq", 8), ("pq", 5), ("pq", 9))
                tt("dve", t0, A_re, LDT, ALU.mult, K(0, 2), K(5))
                act_fn(t0, t0, AF.Exp, K(5), K(5))
                tt("dve", t2, t2, t0, ALU.mult, K(7, 5), K(7))
                tt("dve", t3, t3, t0, ALU.mult, K(8, 5), K(8))
                ts("dve", t2, t2, -1.0, ALU.add, K(7), K(7))
                tt("dve", t0, A_re, A_re, ALU.mult, K(0), K(5))
                tt("dve", t1, A_im, A_im, ALU.mult, K(1), K(6))
                tt("dve", t0, t0, t1, ALU.add, K(5, 6), K(5))
                P.op("dve", lambda e, t0=t0: e.reciprocal(out=t0, in_=t0), K(5), K(5))
                tt("dve", t1, t2, A_re, ALU.mult, K(7, 0), K(6))
                tt("dve", t4, t3, A_im, ALU.mult, K(8, 1), K(9))
                tt("dve", t1, t1, t4, ALU.add, K(6, 9), K(6))
                tt("dve", t1, t1, t0, ALU.mult, K(6, 5), K(6))
                tt("dve", t4, t3, A_re, ALU.mult, K(8, 0), K(9))
                tt("dve", t3, t2, A_im, ALU.mult, K(7, 1), K(8))
                tt("dve", t4, t4, t3, ALU.subtract, K(9, 8), K(9))
                tt("dve", t4, t4, t0, ALU.mult, K(9, 5), K(9))
                tt("dve", t0, t1, b_re, ALU.mult, K(6, 3), K(5))
                tt("dve", t2, t4, b_im, ALU.mult, K(9, 4), K(7))
                tt("dve", t0, t0, t2, ALU.subtract, K(5, 7), K(5))
                tt("dve", t2, t1, b_im, ALU.mult, K(6, 4), K(7))
                tt("dve", t3, t4, b_re, ALU.mult, K(9, 3), K(8))
                tt("dve", t2, t2, t3, ALU.add, K(7, 8), K(7))
                bst = pq[1].bitcast(BF16)[:, 0:2048].rearrange("p (a r m) -> p a r m", a=8, r=2)
                cp("pool", bst[:, :, 0, :], t0.rearrange("p (a m) -> p a m", a=8), K(5, 1), K(1))
                cp("pool", bst[:, :, 1, :], t2.rearrange("p (a m) -> p a m", a=8), K(7, 1), K(1))
                dma(ssmw[:, qt, 0, :], bst.rearrange("p a r m -> p (a r m)"), K(1), [("ssmw", qt, 0)])
                cst_f = pq[3].bitcast(F32)
                cf = R[:, (3 * 4096) // 2:(5 * 4096) // 2].bitcast(F32)
                dma(cf, ssmC[:, qt, :], K(5, 7), K(3, 4))
                cfv = cf.rearrange("p (r a m) -> p r a m", r=2, a=8)
                cbt = pq[6].bitcast(BF16)[:, 0:2048].rearrange("p (a r m) -> p a r m", a=8, r=2)
                cp("pool", cbt[:, :, 0, :], cfv[:, 0, :, :], K(3, 4, 6), K(6))
                ts("dve", cbt[:, :, 1, :], cfv[:, 1, :, :], -1.0, ALU.mult, K(3, 4, 6), K(6))
                dma(ssmw[:, qt, 1, :], cbt.rearrange("p a r m -> p (a r m)"), K(6), [("ssmw", qt, 1)])
            for qt in range(4):
                P.const.add(("ssmw", qt, 0))
                P.const.add(("ssmw", qt, 1))
            P.const.add(("tabd",))
        def ssm(n, L, nseg, st_views, final_out, fillers, nfill):
            plL.n = 2
            plL.i = 0
            plW.base, plW.n = 4, NSLOT - 4
            psbk = [("psL", 6), ("psL", 7)]
            nsteps = 2 * nseg
            per_pt = (nfill + 4 * nsteps - 1) // (4 * nsteps)

            def fill():
                for _ in range(per_pt):
                    next(fillers, None)
            for hf in range(2):
                p16 = slice(hf * 16, hf * 16 + 16)
                Bv, Cv, wkB, wkC = [], [], [], []
                for qi in range(2):
                    qt = 2 * hf + qi
                    sB = 2 * qi
                    dma(wring[:, sB, :], ssmw[:, qt, 0, :], [("ssmw", qt, 0)], [("w", sB)])
                    sC = 2 * qi + 1
                    dma(wring[:, sC, :], ssmw[:, qt, 1, :], [("ssmw", qt, 1)], [("w", sC)])
                    Bv.append(wring[:, sB, :].rearrange("p (a m) -> p a m", m=128))
                    Cv.append(wring[:, sC, :].rearrange("p (a m) -> p a m", m=128))
                    wkB.append(("w", sB))
                    wkC.append(("w", sC))
                dma(tabq[:, :, :, :], tabd[:, :, p16, :], [("tabd",)], [("tabq",)])
                if L != 64:
                    cp("pool", rt16[:, :, :], rth[:, 1, p16].unsqueeze(2).to_broadcast([128, 16, L]), [("rth",)], [("rt16",)])
                    P.op("pool", lambda e: e.memset(rt16[:, :, 0:1], 0.0), [("rt16",)], [("rt16",)])
                Ct = tabq[:, 0, :, 0:L]
                St = tabq[:, 1, :, 0:L]
                if L == 64:
                    Rt2 = tabq[:, 2, :, :].rearrange("p a j -> p (a j)")
                    rtk = ("tabq",)
                else:
                    Rt2 = rt16[:, :, :].rearrange("p a j -> p (a j)")
                    rtk = ("rt16",)
                for seg in range(nseg):
                    tok = slice(seg * L, (seg + 1) * L)
                    zre, zim, zk = st_views[seg]
                    tv = [sw[:, i, 0:16 * L].rearrange("p (a j) -> p a j", a=16) for i in range(6)]
                    for sbt_ in range(2):
                        p8 = slice(8 * sbt_, 8 * sbt_ + 8)
                        psb = ps_t[:, 6:8, :].rearrange("p a b -> p (a b)")[:, 0:16 * L]
                        psb4 = psb.rearrange("p (a r j) -> p a r j", a=8, r=2)
                        items = []
                        for pq_ in range(8):
                            pp = 8 * sbt_ + pq_
                            ch = 4 * hf + pp // 4
                            for ri in range(2):
                                items.append((psb4[:, pq_, ri, :], Bv[pp // 8][:, (pp % 8) * 2 + ri, :], usb[:, ch, tok]))
                        mms(items, wkB + [("usb", 4 * hf + i) for i in range(4)], psbk)
                        bre = psb4[:, :, 0, :]
                        bim = psb4[:, :, 1, :]
                        sk = lambda i: ("sw", i, sbt_)
                        tt("dve", tv[0][:, p8, :], Ct[:, p8, :], bre, ALU.mult, [("tabq",)] + psbk, [sk(0)])
                        tt("dve", tv[1][:, p8, :], St[:, p8, :], bim, ALU.mult, [("tabq",)] + psbk, [sk(1)])
                        tt("dve", tv[2][:, p8, :], Ct[:, p8, :], bim, ALU.mult, [("tabq",)] + psbk, [sk(2)])
                        tt("dve", tv[3][:, p8, :], St[:, p8, :], bre, ALU.mult, [("tabq",)] + psbk, [sk(3)])
                    SW = lambda i: [("sw", i, 0), ("sw", i, 1)]
                    tt("pool", tv[4], tv[0], tv[1], ALU.add, SW(0) + SW(1), SW(4))
                    tt("pool", tv[5], tv[2], tv[3], ALU.subtract, SW(2) + SW(3), SW(5))
                    fill()
                    s1 = plM.next()
                    tt("dve", sm[:, s1, 0:16], rth[:, 1, p16], zre[:, p16], ALU.mult, [("rth",), zk], [("sm", s1)])
                    tt("dve", sm[:, s1, 16:32], rth[:, 1, p16], zim[:, p16], ALU.mult, [("rth",), zk], [("sm", s1)])
                    tt("dve", tv[4][:, :, 0:1], tv[4][:, :, 0:1], sm[:, s1, 0:16].unsqueeze(2), ALU.add,
                       SW(4) + [("sm", s1)], SW(4))
                    tt("dve", tv[5][:, :, 0:1], tv[5][:, :, 0:1], sm[:, s1, 16:32].unsqueeze(2), ALU.add,
                       SW(5) + [("sm", s1)], SW(5))
                    w4 = sw[:, 4, 0:16 * L]
                    w5 = sw[:, 5, 0:16 * L]
                    P.op("dve", lambda e, w4=w4, Rt2=Rt2: e.tensor_tensor_scan(out=w4, data0=Rt2, data1=w4, initial=0.0,
                                                                           op0=ALU.mult, op1=ALU.add),
                         SW(4) + [rtk], SW(4))
                    P.op("dve", lambda e, w5=w5, Rt2=Rt2: e.tensor_tensor_scan(out=w5, data0=Rt2, data1=w5, initial=0.0,
                                                                           op0=ALU.mult, op1=ALU.add),
                         SW(5) + [rtk], SW(5))
                    fill()
                    tt("dve", tv[0], Ct, tv[4], ALU.mult, [("tabq",)] + SW(4), SW(0))
                    tt("dve", tv[1], St, tv[5], ALU.mult, [("tabq",)] + SW(5), SW(1))
                    tt("pool", tv[2], Ct, tv[5], ALU.mult, [("tabq",)] + SW(5), SW(2))
                    tt("pool", tv[3], St, tv[4], ALU.mult, [("tabq",)] + SW(4), SW(3))
                    sre = ssb[:, :, 0, 0:L]
                    sim = ssb[:, :, 1, 0:L]
                    tt("dve", sre, tv[0], tv[1], ALU.subtract, SW(0) + SW(1), [("ssb",)])
                    tt("pool", sim, tv[2], tv[3], ALU.add, SW(2) + SW(3), [("ssb",)])
                    tt("dve", zre[:, p16].unsqueeze(2), tv[0][:, :, L - 1:L], tv[1][:, :, L - 1:L], ALU.subtract,
                       SW(0) + SW(1), [zk])
                    tt("dve", zim[:, p16].unsqueeze(2), tv[2][:, :, L - 1:L], tv[3][:, :, L - 1:L], ALU.add,
                       SW(2) + SW(3), [zk])
                    fill()
                    dc = spc("dcol")
                    for fc in range(4):
                        ch = 4 * hf + fc
                        pc = plL.next()
                        prs = []
                        for pp in range(4 * fc, 4 * fc + 4):
                            for ri in range(2):
                                prs.append((Cv[pp // 8][:, (pp % 8) * 2 + ri, :], ssb[:, pp, ri, 0:L]))
                        mmg(ps_t[:, pc, 0:L], prs, wkC + [("ssb",)], [("psL", pc)])
                        stt_(usf[:, ch, tok], usf[:, ch, tok], dc[:, ch:ch + 1], ps_t[:, pc, 0:L], ALU.mult, ALU.add,
                             [("usf", ch), ("psL", pc), ("spt",)], [("usf", ch)])
                    fill()
            for _ in fillers:
                pass
            plL.n = 4
            plW.base, plW.n = 0, NSLOT
            if final_out is not None:
                final_out()
            for c in range(8):
                s1 = plM.next()
                a = sm[:, s1, 0:n]
                y = usf[:, c, 0:n]
                tt("dve", a, y, y, ALU.mult, [("usf", c)], [("sm", s1)])
                ts("dve", a, a, 0.044715, ALU.mult, [("sm", s1)], [("sm", s1)], s2=1.0, op1=ALU.add)
                tt("dve", a, a, y, ALU.mult, [("sm", s1), ("usf", c)], [("sm", s1)])
                act_fn(a, a, AF.Sigmoid, [("sm", s1)], [("sm", s1)], scale=2.0 * math.sqrt(2.0 / math.pi))
                tt("dve", y, y, a, ALU.mult, [("sm", s1), ("usf", c)], [("usf", c)])
                cp("pool", usb[:, c, 0:n], y, [("usf", c)], [("usb", c)])
            bg = spc("bglu")
            for j in range(8):
                pi = lin("glu", j, usb, [("usb", c) for c in range(8)], n)
                s1 = plM.next()
                act_fn(sm[:, s1, 0:n], psL(pi, n), AF.Sigmoid, [("psL", pi), ("spt",)], [("sm", s1)], bias=bg[:, j:j + 1])
                tt("dve", os_[:, j, 0:n], usf[:, j, 0:n], sm[:, s1, 0:n], ALU.mult, [("usf", j), ("sm", s1)], [("os", j)])

        def memattn(n, tok):
            for _ in memattn_gen(n, tok):
                pass

        def memattn_gen(n, tok):
            for hm in range(4):
                yield
                pms = []
                for mb in range(2):
                    pi = plL.next()
                    mmg(psL(pi, n), [(mkT[:, 2 * hm + dcc, mb * 128:(mb + 1) * 128], qT[:, 2 * hm + dcc, tok])
                                     for dcc in range(2)],
                        [("mkT",), ("q", 2 * hm), ("q", 2 * hm + 1)], [("psL", pi)])
                    s1 = plM.next()
                    pm = sm[:, s1, 0:n].bitcast(BF16)[:, 0:n]
                    act_fn(pm, psL(pi, n), AF.Exp, [("psL", pi)], [("sm", s1)])
                    pms.append((pm, s1))
                pd = plL.next()
                mmg(psL(pd, n), [(ones_b[:, :], pm) for pm, _ in pms], [("sm", s) for _, s in pms] + [("ones",)], [("psL", pd)])
                s2 = plM.next()
                P.op("dve", lambda e, s2=s2, pd=pd: e.reciprocal(out=sm[:, s2, 0:n], in_=psL(pd, n)), [("psL", pd)], [("sm", s2)])
                for dcc in range(2):
                    po = plL.next()
                    c = 2 * hm + dcc
                    mmg(psL(po, n), [(mvb[:, mb, c * 128:(c + 1) * 128], pms[mb][0]) for mb in range(2)],
                        [("mvb",)] + [("sm", s) for _, s in pms], [("psL", po)])
                    tt("dve", om[:, c, tok], psL(po, n), sm[:, s2, 0:n], ALU.mult, [("psL", po), ("sm", s2)], [("om", c)])

        def attn_S(c, e2, qcols, nq, blocks):
            prow = slice(e2 * 64, e2 * 64 + 64)
            sbk = plS.next()
            items = []
            allk = []
            for bi_, (kfn, vl, nk, bfn, keys) in enumerate(blocks):
                items.append((ps_t[0:nk, sbk, bi_ * 64:bi_ * 64 + nq], kfn(prow), qT[prow, c, qcols]))
                allk += keys
            mms(items, allk + [("q", c)], [("psL", sbk)])
            return sbk

        def attn_rest(c, e2, qcols, nq, blocks, okey, sbk):
            chv = spc("ch")
            hh = 2 * c + e2
            prow = slice(e2 * 64, e2 * 64 + 64)
            pg = plPG.next()
            nb_ = len(blocks)
            merge_ok = (nq == 64) and all(bl[2] == 128 for bl in blocks)
            runs = []
            for bi_, (kfn, vl, nk, bfn, keys) in enumerate(blocks):
                bt = bfn(e2) if bfn is not None else None
                typ = None if bt is None else (bt[2] if len(bt) > 2 else -100 - bi_)
                if runs and merge_ok:
                    r = runs[-1]
                    if (r["typ0"] is None and typ is None) or \
                       (r["typ0"] is not None and typ is not None and typ == r["typ0"] + r["n"] and typ >= 0):
                        r["n"] += 1
                        continue
                runs.append({"b0": bi_, "n": 1, "typ0": typ, "bt": bt, "nk": nk})
            pts = []
            for r in runs:
                b0, nr, nk = r["b0"], r["n"], r["nk"]
                if merge_ok:
                    ps = ps_t[:, sbk, b0 * 64:(b0 + nr) * 64]
                    pv = pT[:, pg * 5 + b0:pg * 5 + b0 + nr, :].rearrange("p a q -> p (a q)")
                else:
                    ps = ps_t[0:nk, sbk, b0 * 64:b0 * 64 + nq]
                    pv = pT[0:nk, pg * 5 + b0, 0:nq]
                if r["typ0"] is None:
                    act_fn(pv, ps, AF.Exp, [("psL", sbk), ("spt",)], [("pTg", pg)], bias=chv[0:nk, hh:hh + 1])
                else:
                    bap, bkey = r["bt"][0], r["bt"][1]
                    s1 = plM.next()
                    if merge_ok:
                        bap = r["bt"][3](r["typ0"], nr)
                        tmpv = sm[:, s1, 0:nr * 64]
                    else:
                        tmpv = sm[0:nk, s1, 0:nq]
                    tt("dve", tmpv, ps, bap, ALU.add, [("psL", sbk), bkey], [("sm", s1)])
                    act_fn(pv, tmpv, AF.Exp, [("sm", s1)], [("pTg", pg)])
            for bi_, (kfn, vl, nk, bfn, keys) in enumerate(blocks):
                pts.append((pT[0:nk, pg * 5 + bi_, 0:nq], vl, nk, keys))
            ob = plO.next()
            mmg(ps_t[:, ob, 0:nq], [(vl, pv) for (pv, vl, nk, keys) in pts],
                [("pTg", pg)] + sum([k for (_, _, _, k) in pts], []), [("psL", ob)])
            mmg(ps_t[:, ob, 64:64 + nq], [(ones_b[0:nk, :], pv) for (pv, vl, nk, keys) in pts],
                [("pTg", pg), ("ones",)], [("psL", ob)])
            return (ob, prow, c, qcols, nq, okey)

        def attn_norm(st):
            ob, prow, c, qcols, nq, okey = st
            s2 = plM.next()
            P.op("dve", lambda e, s2=s2, ob=ob, prow=prow: e.reciprocal(out=sm[prow, s2, 0:nq], in_=ps_t[prow, ob, 64:64 + nq]),
                 [("psL", ob)], [("sm", s2)])
            tt("dve", oa[prow, c, qcols], ps_t[prow, ob, 0:nq], sm[prow, s2, 0:nq], ALU.mult,
               [("psL", ob), ("sm", s2)], [okey])

        LA = 1

        def attn_gen(work, pre=None):
            sbs = {}
            nxt = 0
            pend = None
            for k in range(len(work)):
                if pre is not None:
                    pre(k)
                while nxt < len(work) and nxt <= k + LA:
                    sbs[nxt] = attn_S(*work[nxt][0:5])
                    nxt += 1
                st_new = attn_rest(*work[k], sbs.pop(k))
                if pend is not None:
                    attn_norm(pend)
                pend = st_new
                if k == len(work) - 1:
                    attn_norm(pend)
                    pend = None
                yield

        def attn_run(work, pre=None):
            for _ in attn_gen(work, pre):
                pass

        def tile_body(n, mode, ti):
            xk = [("xn", c) for c in range(DC)]
            ffn("g_f1pre", "g_f1post", "f1g", "f1u", "f1d", n)
            if TSTOP < 4:
                return
            prenorm(h, "h", "g_mpre", n)
            for c in range(8):
                pi = lin("win", c, xn, xk, n)
                cp("act", usf[:, c, 0:n], psL(pi, n), [("psL", pi)], [("usf", c)])
                cp("dve", usb[:, c, 0:n], psL(pi, n), [("psL", pi)], [("usb", c)])

            def pre_gen():
                need_kv_out = (mode == "sample") or ti >= NT - 2
                kf = yb[:, 0:8, 0:n]
                vf = yb[:, 8:16, :].rearrange("p c t -> p (c t)")
                if mode == "prompt":
                    kslots = [(2 * ti) % 6, (2 * ti + 1) % 6]
                for c in range(8):
                    pi = lin("win", 16 + c, xn, xk, n)
                    if mode == "prompt":
                        for tb in range(2):
                            cp("act", kring[:, c, kslots[tb], :], psL(pi, n)[:, tb * 128:(tb + 1) * 128], [("psL", pi)],
                               [("kr", c, kslots[tb])])
                    else:
                        cp("act", kring[:, c, 4, 0:16], psL(pi, n)[:, 0:16], [("psL", pi)], [("kr", c, 4)])
                        cp("act", kring[:, c, 5, 0:16], psL(pi, n)[:, 16:32], [("psL", pi)], [("kr", c, 5)])
                    if need_kv_out:
                        cp("dve", kf[:, c, :], psL(pi, n), [("psL", pi)], [("yb", c)])
                    yield
                if need_kv_out:
                    if mode == "prompt":
                        t0 = (ti - (NT - 2)) * T
                        dma(okT.rearrange("(c p) t -> p c t", p=128)[:, :, t0:t0 + T], kf, [("yb", c) for c in range(8)], [("okT", ti)])
                    else:
                        dma(oskT.rearrange("(c p) t -> p c t", p=128), kf, [("yb", c) for c in range(8)], [("oskT",)])
                for c in range(8):
                    wv, wk = wload("win", 24 + c)
                    if mode == "prompt":
                        for tb in range(2):
                            pi = plL.next()
                            mmg(psL(pi, 128), [(xn[:, k, tb * 128:(tb + 1) * 128], wv[:, k, :]) for k in range(16)], [wk] + xk, [("psL", pi)])
                            cp("act", vring[:, kslots[tb], c * 128:(c + 1) * 128], psL(pi, 128), [("psL", pi)], [("vr", kslots[tb], c)])
                            if need_kv_out:
                                cp("dve", vf[:, tb * 1024 + c * 128: tb * 1024 + (c + 1) * 128], psL(pi, 128), [("psL", pi)], [("yb", 8 + tb * 4 + c // 2)])
                    else:
                        for b2 in range(2):
                            pi = plL.next()
                            mmg(psL(pi, 128)[0:16, :], [(xn[:, k, b2 * 16:(b2 + 1) * 16], wv[:, k, :]) for k in range(16)], [wk] + xk, [("psL", pi)])
                            cp("act", vring[0:16, 4 + b2, c * 128:(c + 1) * 128], psL(pi, 128)[0:16, :], [("psL", pi)], [("vr", 4 + b2, c)])
                            cp("dve", vf[0:16, b2 * 1024 + c * 128: b2 * 1024 + (c + 1) * 128], psL(pi, 128)[0:16, :], [("psL", pi)], [("yb", 8 + b2 * 4 + c // 2)])
                    yield
                if need_kv_out:
                    vkeys = [("yb", 8 + i) for i in range(8)]
                    if mode == "prompt":
                        t0 = (ti - (NT - 2)) * T
                        dma(ov[t0:t0 + T, :].rearrange("(b p) f -> p b f", p=128), vf.rearrange("p (b f) -> p b f", b=2), vkeys, [("ov", ti)])
                    else:
                        dma(osv.rearrange("b p f -> p b f"), vf[0:16, :].rearrange("p (b f) -> p b f", b=2), vkeys, [("osv",)])
                for c in range(8):
                    pi = lin("win", 8 + c, xn, xk, n)
                    ts("dve", qT[:, c, 0:n], psL(pi, n), 0.125, ALU.mult, [("psL", pi)], [("q", c)])
                    yield
                if mode == "prompt":
                    work = []
                    for c in range(8):
                        bb = c % 2
                        for qi in range(4):
                            qc = 4 * ti + qi
                            par = qc % 2
                            blocks = []
                            for b in range(5):
                                gb = qc // 2 - 4 + b
                                if gb < 0:
                                    continue
                                sl = gb % 6
                                typ = {(0, 3): 0, (0, 4): 1, (1, 0): 2, (1, 3): 3, (1, 4): 4}.get((par, b))
                                kfn = (lambda prow, sl=sl, c=c: kring[prow, c, sl, :])
                                vl = vring[:, sl, c * 128:(c + 1) * 128]
                                if typ is None:
                                    bfn = None
                                else:
                                    bfn = (lambda e2, typ=typ, bb=bb: (btl[:, bb, e2, typ, :], ("btl", bb), typ,
                                                                      (lambda t0, nr, e2=e2, bb=bb: btl[:, bb, e2, t0:t0 + nr, :].rearrange("p a q -> p (a q)"))))
                                blocks.append((kfn, vl, 128, bfn, [("kr", c, sl), ("vr", sl, c)]))
                            for e2 in range(2):
                                work.append((c, e2, slice(qi * 64, qi * 64 + 64), 64, blocks, ("oa", c)))
                    def pre(k):
                        if k % 8 == 0:
                            c_ = k // 8
                            bb_ = c_ % 2
                            dma(btl[:, bb_, :, :, :],
                                btd[:, (2 * c_) * 320:(2 * c_ + 2) * 320].rearrange("p (h t q) -> p h t q", h=2, t=5),
                                [], [("btl", bb_)])
                    yield from attn_gen(work, pre)
                else:
                    for b2 in range(2):
                        st = yb[:, :, :].rearrange("p c t -> p (c t)")
                        ybk = [("yb", i) for i in range(16)]
                        dma(st.rearrange("p (c t) -> p c t", c=8), ckT[b2].rearrange("(c p) t -> p c t", p=128), [], ybk)
                        for c in range(8):
                            cp("pool", kring[:, c, 0:4, :], st[:, c * 512:(c + 1) * 512].rearrange("p (s k) -> p s k", s=4), ybk,
                               [("kr", c, s) for s in range(4)])
                        dma(st.rearrange("p (s f) -> p s f", s=4), cv[b2].rearrange("(s p) f -> p s f", p=128), [], ybk)
                        for s in range(4):
                            cp("pool", vring[:, s, :], st[:, s * 1024:(s + 1) * 1024], ybk, [("vr", s, c) for c in range(8)])
                        swork = []
                        for c in range(8):
                            blocks = []
                            for b in range(4):
                                kfn = (lambda prow, b=b, c=c: kring[prow, c, b, :])
                                bfn = None if b < 3 else (lambda e2, c=c: (sbt[:, 0, 2 * c + e2, :], ("sbt",)))
                                blocks.append((kfn, vring[:, b, c * 128:(c + 1) * 128], 128, bfn, [("kr", c, b), ("vr", b, c)]))
                            kfn = (lambda prow, b2=b2, c=c: kring[prow, c, 4 + b2, 0:16])
                            blocks.append((kfn, vring[0:16, 4 + b2, c * 128:(c + 1) * 128], 16,
                                           (lambda e2, c=c: (sbt[0:16, 1, 2 * c + e2, :], ("sbt",))), [("kr", c, 4 + b2), ("vr", 4 + b2, c)]))
                            swork.append((c, 0, slice(b2 * 16, b2 * 16 + 16), 16, blocks, ("oa", c)))
                            swork.append((c, 1, slice(b2 * 16, b2 * 16 + 16), 16, blocks, ("oa", c)))
                        attn_run(swork)
                for c in range(8):
                    pi = lin("win", 32 + c, xn, xk, n)
                    ts("dve", qT[:, c, 0:n], psL(pi, n), 0.0625, ALU.mult, [("psL", pi)], [("q", c)])
                    yield
                if mode == "prompt":
                    yield from memattn_gen(n, slice(0, n))
                else:
                    for b2 in range(2):
                        st = yb[:, :, :].rearrange("p c t -> p (c t)")
                        ybk = [("yb", i) for i in range(16)]
                        dma(st[:, 0:2048].rearrange("p (c t) -> p c t", c=8), cmkT[b2].rearrange("(c p) t -> p c t", p=128), [], ybk)
                        cp("pool", mkT[:, :, :], st[:, 0:2048].rearrange("p (c t) -> p c t", c=8), ybk, [("mkT",)])
                        dma(st[:, 2048:4096].rearrange("p (s f) -> p s f", s=2), cmv[b2].rearrange("(s p) f -> p s f", p=128), [], ybk)
                        cp("pool", mvb[:, :, :], st[:, 2048:4096].rearrange("p (s f) -> p s f", s=2), ybk, [("mvb",)])
                        memattn(16, slice(b2 * 16, b2 * 16 + 16))
            def mk_unit(j):
                def unit():
                    tms = []
                    for bi, wn, src, skey in ((1, "ba", oa, "oa"), (2, "bm", om, "om")):
                        pb_ = lin(wn, j, src, [(skey, c) for c in range(8)], n)
                        lin("win", 40 + 16 * bi + j, xn, xk, n, pi=pb_, col=256)
                        s1 = plM.next()
                        s2_ = plM.next()
                        act_fn(sm[:, s1, 0:n], ps_t[:, pb_, 256:256 + n], AF.Sigmoid, [("psL", pb_)], [("sm", s1)])
                        cp("act", sm[:, s2_, 0:n], psL(pb_, n), [("psL", pb_)], [("sm", s2_)])
                        tt("pool", sm[:, s1, 0:n], sm[:, s1, 0:n], sm[:, s2_, 0:n], ALU.mult, [("sm", s1), ("sm", s2_)], [("sm", s1)])
                        tms.append(s1)
                    tt("pool", yb[:, j, 0:n], sm[:, tms[0], 0:n], sm[:, tms[1], 0:n], ALU.add,
                       [("sm", tms[0]), ("sm", tms[1])], [("yb", j)])
                return unit
            def all_gen():
                yield from pre_gen()
                if TSTOP >= 7:
                    for j in range(DC):
                        mk_unit(j)()
                        yield
            fillers = all_gen()
            if mode != "prompt":
                for _ in fillers:
                    pass
            if mode == "prompt":
                stv = [(stt[:, 0, 0, :], stt[:, 0, 1, :], ("stt", 0))] * 4
                fo = None
                if ti == NT - 1:
                    fo = lambda: dma(ost, stt[:, 0, :, :], [("stt", 0)], [("ost",)])
                ssm(n, 64, 4, stv, fo, fillers, 116)
            else:
                s0 = spc("s0").rearrange("p (b r a) -> p b r a", b=2, r=2)
                cp("pool", stt[:, :, :, :], s0, [("spt",), ("stt", 0), ("stt", 1)], [("stt", 0), ("stt", 1)])
                stv = [(stt[:, b2, 0, :], stt[:, b2, 1, :], ("stt", b2)) for b2 in range(2)]
                ssm(n, 16, 2, stv, lambda: dma(osst, stt[:, :, :, :], [("stt", 0), ("stt", 1)], [("osst",)]), fillers, 0)
            if TSTOP < 7:
                return
            for j in range(DC):
                pb_ = lin("bs", j, os_, [("os", c) for c in range(8)], n)
                lin("win", 40 + j, xn, xk, n, pi=pb_, col=256)
                s1 = plM.next()
                act_fn(sm[:, s1, 0:n], ps_t[:, pb_, 256:256 + n], AF.Sigmoid, [("psL", pb_)], [("sm", s1)])
                tt("dve", sm[:, s1, 0:n], sm[:, s1, 0:n], psL(pb_, n), ALU.mult, [("sm", s1), ("psL", pb_)], [("sm", s1)])
                tt("pool", mg[:, j, 0:n], sm[:, s1, 0:n], yb[:, j, 0:n], ALU.add, [("sm", s1), ("yb", j)], [("mg", j)])
            for j in range(DC):
                pi = lin("wo", j, mg, [("mg", c) for c in range(DC)], n)
                cp("act", yb[:, j, 0:n], psL(pi, n), [("psL", pi)], [("yb", j)])
            postnorm_res("g_mpost", 1.0, n)
            if TSTOP < 8:
                return
            ffn("g_f2pre", "g_f2post", "f2g", "f2u", "f2d", n)

        P.const.add(("rth",))
        P.op("act", lambda e: e.copy(out=sm[:, 0, 0:1], in_=ones_f[:, 0:1]), [("ones",)],
             [("pq", i) for i in range(10)] + [("stgf", i) for i in range(NSF)] + [("stgb", i) for i in range(NSB)]
             + [("yb", i) for i in range(DC)] + [("sm", 0), ("sqb",)])
        hk = [("h", c) for c in range(DC)]
        if STOP >= 2:
            dma(h[:, :, :], memT.rearrange("(c p) t -> p c t", p=128), [], hk)
            prenorm(h, "h", "g_mem", 256)
            xk_ = [("xn", c) for c in range(DC)]
            for c in range(8 if KSUB >= 1 else 0):
                pi = lin("mk", c, xn, xk_, 256)
                if KSUB2 >= 1:
                    cp("act", mkT[:, c, :], psL(pi, 256), [("psL", pi)], [("mkTc", c)])
                if KSUB2 >= 2:
                    cp("dve", yb[:, c, :], psL(pi, 256), [("psL", pi)], [("yb", c)])
            if KSUB2 >= 3:
                dma(omkT.rearrange("(c p) t -> p c t", p=128), yb[:, 0:8, :], [("yb", c) for c in range(8)], [("omkT",)])
            vfm = yb[:, 8:16, :].rearrange("p c t -> p (c t)")
            for c in range(8 if KSUB >= 2 else 0):
                wv, wk = wload("mv", c)
                for mb in range(2):
                    pi = plL.next()
                    mmg(psL(pi, 128), [(xn[:, k, mb * 128:(mb + 1) * 128], wv[:, k, :]) for k in range(16)], [wk] + xk_, [("psL", pi)])
                    cp("act", mvb[:, mb, c * 128:(c + 1) * 128], psL(pi, 128), [("psL", pi)], [("mvb",)])
                    cp("dve", vfm[:, mb * 1024 + c * 128: mb * 1024 + (c + 1) * 128], psL(pi, 128), [("psL", pi)], [("yb", 8 + mb * 4 + c // 2)])
            dma(omv.rearrange("(b p) f -> p b f", p=128), vfm.rearrange("p (b f) -> p b f", b=2), [("yb", 8 + i) for i in range(8)], [("omv",)])

        xTv = xT.rearrange("(c p) t -> p c t", p=128)
        yTv = yT.rearrange("(c p) t -> p c t", p=128)
        for ti in range(min(NT, NTILES) if STOP >= 3 else 0):
            P.epoch = 1 + ti // EPOCH_TILES
            dma(h[:, :, :], xTv[:, :, ti * T:(ti + 1) * T], [], hk)
            tile_body(T, "prompt", ti)
            dma(yTv[:, :, ti * T:(ti + 1) * T], h[:, :, :], hk, [("yT", ti)])
        P.epoch += 1
        if STOP >= 10:
            dma(h[:, :, 0:TS], xsT.rearrange("(c p) t -> p c t", p=128), [], hk)
            tile_body(TS, "sample", 0)
            dma(ysT.rearrange("(c p) t -> p c t", p=128), h[:, :, 0:TS], hk, [("ysT",)])

        P.assign()
        semnames = {}
        sems = {}
        for e in ("pe", "act", "dve", "pool"):
            sems[e] = [es.enter_context(nc.semaphore(f"s_{e}_{i}")) for i in range(P.nepoch)]
        dsems = [es.enter_context(nc.semaphore(f"s_d_{i}")) for i in range(NDS)]
        block = es.enter_context(nc.Block())

        @block.sync
        def _(e):
            P.emit("sp", e, sems, dsems)

        @block.tensor
        def _(e):
            P.emit("pe", e, sems, dsems)

        @block.scalar
        def _(e):
            P.emit("act", e, sems, dsems)

        @block.vector
        def _(e):
            P.emit("dve", e, sems, dsems)

        @block.gpsimd
        def _(e):
            P.emit("pool", e, sems, dsems)
    return nc


def _tile_w(w, kc):
    K, N = w.shape
    nb = N // 128
    a = w.reshape(kc, 128, nb, 128).transpose(1, 2, 0, 3)
    return a.reshape(128, nb * kc * 128)


def _gcol(g, nch):
    return np.ascontiguousarray(g.reshape(nch, 128).T)


_NC_CACHE = {}


def kernel(**inp):
    f = lambda k: np.asarray(inp[k], dtype=np.float32)
    ws = {"f1g": f("ffn1_w_gate")[0], "f1u": f("ffn1_w_up")[0], "f1d": f("ffn1_w_down")[0], "win": f("w_in")[0],
          "glu": f("ssm_w_glu")[0], "bs": f("w_branch_ssm")[0], "ba": f("w_branch_att")[0], "bm": f("w_branch_mem")[0],
          "wo": f("w_out")[0], "f2g": f("ffn2_w_gate")[0], "f2u": f("ffn2_w_up")[0], "f2d": f("ffn2_w_down")[0],
          "mk": f("w_mem_k")[0], "mv": f("w_mem_v")[0]}
    wall = np.empty((128, WX), np.float32)
    for n, kc, nb in WTAB:
        o = WOFF[n][0]
        wall[:, o:o + kc * nb * 128] = _tile_w(ws[n], kc)

    a_re = f("ssm_a_re")[0]; a_im = f("ssm_a_im")[0]; ldt = f("ssm_log_dt")[0]
    b_re = f("ssm_b_re")[0]; b_im = f("ssm_b_im")[0]; c_re = f("ssm_c_re")[0]; c_im = f("ssm_c_im")[0]
    G = 64
    gidx = (2 * np.arange(32)[None, :] + (np.arange(128)[:, None] // 64))
    pidx = np.broadcast_to((np.arange(128) % 64)[:, None], (128, 32))
    are_m = a_re[gidx, pidx]; aim_m = a_im[gidx, pidx]; ldt_m = ldt[gidx]
    ssmB = np.zeros((128, 4, 5, 8, 128), np.float32)
    ssmC = np.zeros((128, 4, 2, 8, 128), np.float32)
    for pair in range(32):
        qt, pp = divmod(pair, 8)
        for gp in range(2):
            g = 2 * pair + gp
            gl = g % 8
            ms = slice(gp * 64, gp * 64 + 64)
            ssmB[:, qt, 0, pp, ms] = a_re[g][None, :]
            ssmB[:, qt, 1, pp, ms] = a_im[g][None, :]
            ssmB[:, qt, 2, pp, ms] = ldt[g]
            ssmB[gl * 16:(gl + 1) * 16, qt, 3, pp, ms] = b_re[g].T
            ssmB[gl * 16:(gl + 1) * 16, qt, 4, pp, ms] = b_im[g].T
            ssmC[ms, qt, 0, pp, gl * 16:(gl + 1) * 16] = c_re[g].T
            ssmC[ms, qt, 1, pp, gl * 16:(gl + 1) * 16] = c_im[g].T
    ssmB = ssmB.reshape(128, 4, 5, 1024)
    ssmC = ssmC.reshape(128, 4, 2048)

    rb = f("att_rel_bias")[0]
    btd = np.zeros((128, BT_TOT), np.float32)
    kk = np.arange(128)[:, None]; qq = np.arange(64)[None, :]
    types = [(0, 3), (0, 4), (1, 0), (1, 3), (1, 4)]
    bt = np.zeros((128, 16, 5, 64), np.float32)
    for t_, (par, b) in enumerate(types):
        rel = 64 * par + 128 * (4 - b) + qq - kk
        dcn = -par - 8 + 2 * b + kk // 64 + 0 * qq
        ok = (dcn >= -8) & (dcn <= 0)
        idx = np.clip(rel, -128, 128) + 128
        for hh in range(16):
            bt[:, hh, t_, :] = np.where(ok, rb[hh][idx], np.float32(NEG))
    btd[:, 0:BTW] = bt.reshape(128, BTW)
    q16 = np.arange(16)[None, :]
    rel3 = 128 + q16 - kk
    s3 = np.stack([rb[hh][np.clip(rel3, -128, 128) + 128] for hh in range(16)], 1)
    k16 = np.arange(16)[:, None]
    reln = q16 - k16
    sn = np.zeros((128, 16, 16), np.float32)
    sn[0:16] = np.stack([rb[hh][np.clip(reln, -128, 128) + 128] for hh in range(16)], 1)
    btd[:, SBT_OFF:SBT_OFF + 256] = s3.reshape(128, 256)
    btd[:, SBT_OFF + 256:SBT_OFF + 512] = sn.reshape(128, 256)

    def spack(s0=None):
        a = np.zeros((128, SPW), np.float32)

        def put(name, v):
            o, w_ = SPC[name]
            a[:, o:o + w_] = v
        put("g_f1pre", _gcol(f("ffn1_norm_pre")[0], 16)); put("g_f1post", _gcol(f("ffn1_norm_post")[0], 16))
        put("g_mpre", _gcol(f("mix_norm_pre")[0], 16)); put("g_mpost", _gcol(f("mix_norm_post")[0], 16))
        put("g_f2pre", _gcol(f("ffn2_norm_pre")[0], 16)); put("g_f2post", _gcol(f("ffn2_norm_post")[0], 16))
        put("g_mem", _gcol(f("mem_norm")[0], 16))
        put("dcol", _gcol(f("ssm_d")[0].reshape(-1), 8)); put("bglu", _gcol(f("ssm_b_glu")[0], 8))
        put("ch", np.broadcast_to(rb[:, 256][None, :], (128, 16)))
        put("are", are_m); put("aim", aim_m); put("ldt", ldt_m)
        put("jidx", np.broadcast_to(np.arange(1, 65, dtype=np.float32)[None, :], (128, 64)))
        if s0 is not None:
            put("s0", s0)
        return a

    xp = f("x_prompt"); xs = f("x_sample"); mp = f("mem_prompt")
    ck = f("cache_att_k")[0]; cvv = f("cache_att_v")[0]; cmk = f("cache_mem_k")[0]; cmvv = f("cache_mem_v")[0]
    sre = f("state_ssm_re")[0]; sim = f("state_ssm_im")[0]
    in_maps = []
    for c in range(8):
        s0 = np.zeros((128, 2, 2, 32), np.float32)
        for b2 in range(2):
            s0[:, b2, 0, :] = sre[2 * c + b2][gidx, pidx]
            s0[:, b2, 1, :] = sim[2 * c + b2][gidx, pidx]
        in_maps.append({
            "xT": np.ascontiguousarray(xp[c].T),
            "xsT": np.ascontiguousarray(xs[2 * c:2 * c + 2].reshape(TS, D).T),
            "memT": np.ascontiguousarray(mp[c].T),
            "ckT": np.ascontiguousarray(ck[2 * c:2 * c + 2].reshape(2, 512, 1024).transpose(0, 2, 1)),
            "cv": np.ascontiguousarray(cvv[2 * c:2 * c + 2].reshape(2, 512, 1024)),
            "cmkT": np.ascontiguousarray(cmk[2 * c:2 * c + 2].reshape(2, 256, 1024).transpose(0, 2, 1)),
            "cmv": np.ascontiguousarray(cmvv[2 * c:2 * c + 2].reshape(2, 256, 1024)),
            "wall": wall, "spk": spack(s0.reshape(128, 128)), "ssmB": ssmB, "ssmC": ssmC, "btd": btd,
        })
    if "nc" not in _NC_CACHE:
        _NC_CACHE["nc"] = build_nc()
    if os.environ.get('KTRACE'):
        res = run_bass_kernel_spmd(_NC_CACHE["nc"], in_maps[:NCORES], core_ids=list(range(NCORES)), trace=True)
        print("EXEC_TIME_NS", res.exec_time_ns)
    else:
        res = run_bass_kernel_spmd(_NC_CACHE["nc"], in_maps[:NCORES], core_ids=list(range(NCORES)))
    R_ = list(res.results)
    while len(R_) < 8:
        R_.append({k: np.zeros_like(v) for k, v in R_[0].items()})

    def unstate(a):
        o = np.zeros((64, 64), np.float32)
        o[gidx, pidx] = a
        return o
    y_p = np.stack([R_[c]["yT"].T for c in range(8)])
    y_s = np.concatenate([R_[c]["ysT"].T.reshape(2, 16, D) for c in range(8)])
    akp = np.stack([R_[c]["okT"].T.reshape(512, 16, 64) for c in range(8)])[None]
    avp = np.stack([R_[c]["ov"].reshape(512, 16, 64) for c in range(8)])[None]
    mkp = np.stack([R_[c]["omkT"].T.reshape(256, 4, 256) for c in range(8)])[None]
    mvp = np.stack([R_[c]["omv"].reshape(256, 4, 256) for c in range(8)])[None]
    srp = np.stack([unstate(R_[c]["ost"][:, 0, :]) for c in range(8)])[None]
    sip = np.stack([unstate(R_[c]["ost"][:, 1, :]) for c in range(8)])[None]
    aks = np.concatenate([R_[c]["oskT"].T.reshape(2, 16, 16, 64) for c in range(8)])[None]
    avs = np.concatenate([R_[c]["osv"].reshape(2, 16, 16, 64) for c in range(8)])[None]
    srs = np.stack([unstate(R_[c]["osst"][:, b2, 0, :]) for c in range(8) for b2 in range(2)])[None]
    sis = np.stack([unstate(R_[c]["osst"][:, b2, 1, :]) for c in range(8) for b2 in range(2)])[None]
    outs = (y_p, y_s, akp, avp, mkp, mvp, srp, sip, aks, avs, srs, sis)
    return tuple(np.ascontiguousarray(o, dtype=np.float32) for o in outs)
```

```python
import contextlib
import math
import os
import numpy as np
import concourse.bass as bass
import concourse.mybir as mybir
from concourse.bass_utils import run_bass_kernel_spmd

F32 = mybir.dt.float32
BF16 = mybir.dt.bfloat16
I32 = mybir.dt.int32
AF = mybir.ActivationFunctionType
ALU = mybir.AluOpType
AX = mybir.AxisListType

D = 2048
DC = 16
FC = 44
T = 256
NT = 16
SEQ = 4096
TS = 32
EPS = 1e-6
NEG = -1e30
NDS = 12
NSLOT = 8
SAME_SYNC = True
EPOCH_TILES = 3
STOP = int(os.environ.get('KSTOP', '99'))
NTILES = int(os.environ.get('KTILES', '16'))
NCORES = int(os.environ.get('KCORES', '8'))
KSUB = int(os.environ.get('KSUB', '99'))
KSUB2 = int(os.environ.get('KSUB2', '99'))
TSTOP = int(os.environ.get('KTSTOP', '99'))

WTAB = [("f1g", 16, 44), ("f1u", 16, 44), ("f1d", 44, 16), ("win", 16, 88), ("glu", 8, 8),
        ("bs", 8, 16), ("ba", 8, 16), ("bm", 8, 16), ("wo", 16, 16),
        ("f2g", 16, 44), ("f2u", 16, 44), ("f2d", 44, 16), ("mk", 16, 8), ("mv", 16, 8)]
WOFF = {}
_o = 0
for _n, _kc, _nb in WTAB:
    WOFF[_n] = (_o, _kc, _nb)
    _o += _kc * _nb * 128
WX = _o
CH = 2048
NCHUNK = WX // CH

SPC = {}
_o = 0
for _n, _w in [("g_f1pre", 16), ("g_f1post", 16), ("g_mpre", 16), ("g_mpost", 16), ("g_f2pre", 16),
               ("g_f2post", 16), ("g_mem", 16), ("dcol", 8), ("bglu", 8), ("ch", 16),
               ("are", 32), ("aim", 32), ("ldt", 32), ("s0", 128), ("jidx", 64)]:
    SPC[_n] = (_o, _w)
    _o += _w
SPW = _o
BTW = 16 * 5 * 64
SBT_OFF = BTW
BT_TOT = BTW + 16 * 16 + 16 * 16


class Op:
    __slots__ = ("eng", "fn", "deps", "idx", "sig", "sigval", "epoch", "dsem")


class Prog:
    ENG = ("pe", "act", "dve", "pool", "sp")

    def __init__(self):
        self.ops = {e: [] for e in self.ENG}
        self.lastw = {}
        self.rd = {}
        self.const = set()
        self.epoch = 0

    def op(self, eng, fn, reads=(), writes=()):
        o = Op()
        o.eng = eng
        o.fn = fn
        o.sig = False
        o.epoch = self.epoch
        deps = {}
        ex = [k for k in reads if k[0] in ("psL", "psB")]
        if ex:
            writes = list(writes) + ex
            reads = [k for k in reads if k[0] not in ("psL", "psB")]

        def add(d):
            if d is None:
                return
            if d.eng == "sp":
                deps[("sp", d.idx)] = d
            else:
                k = d.eng
                if k not in deps or deps[k].idx < d.idx:
                    deps[k] = d

        for k in reads:
            add(self.lastw.get(k))
        for k in writes:
            add(self.lastw.get(k))
            for r in self.rd.get(k, ()):
                add(r)
        if eng == "pe":
            deps.pop("pe", None)
        elif eng != "sp" and not SAME_SYNC:
            deps.pop(eng, None)
        o.deps = list(deps.values())
        for d in o.deps:
            d.sig = True
        o.idx = len(self.ops[eng])
        self.ops[eng].append(o)
        for k in writes:
            self.lastw[k] = o
            self.rd[k] = []
        for k in reads:
            if k not in self.const:
                self.rd.setdefault(k, []).append(o)
        return o

    def assign(self):
        self.nepoch = self.epoch + 1
        for e in self.ENG:
            if e == "sp":
                for o in self.ops[e]:
                    o.dsem = o.idx % NDS
                    o.sigval = 16 * (o.idx // NDS + 1)
            else:
                cnt = {}
                for o in self.ops[e]:
                    if o.sig:
                        cnt[o.epoch] = cnt.get(o.epoch, 0) + 1
                        o.sigval = cnt[o.epoch]

    def emit(self, e, eng, sems, dsems):
        waited = {}

        def w(key, sem, val):
            if waited.get(key, 0) < val:
                eng.wait_ge(sem, val)
                waited[key] = val

        for o in self.ops[e]:
            if e == "sp" and o.idx >= NDS:
                p = self.ops["sp"][o.idx - NDS]
                w(("d", p.dsem), dsems[p.dsem], p.sigval)
            for d in o.deps:
                if d.eng == "sp":
                    w(("d", d.dsem), dsems[d.dsem], d.sigval)
                else:
                    w((d.eng, d.epoch), sems[d.eng][d.epoch], d.sigval)
            inst = o.fn(eng)
            if e == "sp":
                inst.then_inc(dsems[o.dsem], 16)
            elif o.sig:
                inst.then_inc(sems[e][o.epoch], 1)
        if e == "sp":
            n = len(self.ops["sp"])
            for i in range(max(0, n - NDS), n):
                p = self.ops["sp"][i]
                w(("d", p.dsem), dsems[p.dsem], p.sigval)


def build_nc():
    nc = bass.Bass("TRN2", target_bir_lowering=False)
    P = Prog()

    def din(name, shape, dt=F32):
        return nc.dram_tensor(name, list(shape), dt, kind="ExternalInput").ap()

    def dout(name, shape, dt=F32):
        return nc.dram_tensor(name, list(shape), dt, kind="ExternalOutput").ap()

    xT = din("xT", [D, SEQ])
    xsT = din("xsT", [D, TS])
    memT = din("memT", [D, 256])
    ckT = din("ckT", [2, 1024, 512])
    cv = din("cv", [2, 512, 1024])
    cmkT = din("cmkT", [2, 1024, 256])
    cmv = din("cmv", [2, 256, 1024])
    wall = din("wall", [128, WX])
    spk = din("spk", [128, SPW])
    ssmB = din("ssmB", [128, 4, 5, 1024])
    ssmC = din("ssmC", [128, 4, 2048])
    btd = din("btd", [128, BT_TOT])
    yT = dout("yT", [D, SEQ])
    ysT = dout("ysT", [D, TS])
    okT = dout("okT", [1024, 512])
    ov = dout("ov", [512, 1024])
    omkT = dout("omkT", [1024, 256])
    omv = dout("omv", [256, 1024])
    ost = dout("ost", [128, 2, 32])
    oskT = dout("oskT", [1024, TS])
    osv = dout("osv", [2, 16, 1024])
    osst = dout("osst", [128, 2, 2, 32])
    wbf = nc.dram_tensor("wbf", [128, WX], BF16, kind="Internal").ap()
    ssmw = nc.dram_tensor("ssmw", [128, 4, 2, 2048], BF16, kind="Internal").ap()
    tabd = nc.dram_tensor("tabd", [128, 3, 32, 64], F32, kind="Internal").ap()

    es = contextlib.ExitStack()
    with es:
        def sb(name, shape, dt):
            return es.enter_context(nc.sbuf_tensor(name, list(shape), dt))

        h = sb("h", [128, DC, T], F32)
        xn = sb("xn", [128, DC, T], BF16)
        yb = sb("yb", [128, DC, T], F32)
        R = sb("R", [128, 20480], BF16)
        kring = sb("kring", [128, 8, 6, 128], BF16)
        vring = sb("vring", [128, 6, 1024], BF16)
        mkT = sb("mkT", [128, 8, 256], BF16)
        mvb = sb("mvb", [128, 2, 1024], BF16)
        btl = sb("btl", [128, 2, 2, 5, 64], F32)
        sbt = sb("sbt", [128, 2, 16, 16], F32)
        tabq = sb("tabq", [128, 3, 16, 64], F32)
        rt16 = sb("rt16", [128, 16, 16], F32)
        sw = sb("sw", [128, 6, 1024], F32)
        ssb = sb("ssb", [128, 16, 2, 64], BF16)
        wring = sb("wring", [128, NSLOT, 2048], BF16)
        pT = sb("pT", [128, 20, 64], BF16)
        ones_f = sb("ones_f", [128, 128], F32)
        ones_b = sb("ones_b", [128, 128], BF16)
        spt = sb("spt", [128, SPW], F32)
        sm = sb("sm", [128, 8, T], F32)
        rsb = sb("rsb", [128, 2, T], F32)
        stt = sb("stt", [128, 2, 2, 32], F32)
        rth = sb("rth", [128, 3, 32], F32)
        ps_t = es.enter_context(nc.psum_tensor("ps", [128, 8, 512], F32))

        if os.environ.get('KVERB'):
            print("SBUF bytes remaining per partition:", nc.sbuf_bytes_remaining)
        def rview(off_bytes, dt, n, pat=None, **kw):
            eb = 2
            a = R[:, off_bytes // eb: off_bytes // eb + (n * (4 if dt == F32 else 2)) // eb]
            if dt == F32:
                a = a.bitcast(F32)
            if pat:
                a = a.rearrange(pat, **kw)
            return a

        act = rview(0, BF16, FC * T, "p (c t) -> p c t", c=FC)
        sqb = rview(0, F32, DC * T, "p (c t) -> p c t", c=DC)
        qT = rview(0, BF16, 8 * T, "p (c t) -> p c t", c=8)
        usf = rview(4096, F32, 8 * T, "p (c t) -> p c t", c=8)
        usb = rview(12288, BF16, 8 * T, "p (c t) -> p c t", c=8)
        oa = rview(16384, BF16, 8 * T, "p (c t) -> p c t", c=8)
        om = rview(20480, BF16, 8 * T, "p (c t) -> p c t", c=8)
        os_ = rview(24576, BF16, 8 * T, "p (c t) -> p c t", c=8)
        mg = rview(28672, BF16, DC * T, "p (c t) -> p c t", c=DC)
        xpre = rview(24576, F32, DC * T, "p (c t) -> p c t", c=DC)
        XPK = [("os", c) for c in range(8)] + [("mg", c) for c in range(DC)]
        NSF, NSB = 5, 4
        stg_f = [rview(i * 8192, F32, CH) for i in range(NSF)]
        ybflat = yb[:, :, :].rearrange("p c t -> p (c t)").bitcast(BF16)
        stg_b = [ybflat[:, i * CH:(i + 1) * CH] for i in range(NSB)]
        pq = [rview(i * 4096, F32, 1024) for i in range(10)]

        def spc(name):
            o, w_ = SPC[name]
            return spt[:, o:o + w_]

        def dma(out, in_, reads, writes):
            return P.op("sp", lambda e: e.dma_start(out=out, in_=in_), reads, writes)

        def act_fn(out, in_, func, reads, writes, bias=None, scale=None):
            kw = {}
            if bias is not None:
                kw["bias"] = bias
            if scale is not None:
                kw["scale"] = scale
            return P.op("act", lambda e: e.activation(out=out, in_=in_, func=func, **kw), reads, writes)

        def tt(eng, out, a, b, op, reads, writes):
            return P.op(eng, lambda e: e.tensor_tensor(out=out, in0=a, in1=b, op=op), reads, writes)

        def ts(eng, out, a, s1, op0, reads, writes, s2=None, op1=None):
            if op1 is None:
                return P.op(eng, lambda e: e.tensor_scalar(out=out, in0=a, scalar1=s1, scalar2=None, op0=op0), reads, writes)
            return P.op(eng, lambda e: e.tensor_scalar(out=out, in0=a, scalar1=s1, scalar2=s2, op0=op0, op1=op1), reads, writes)

        def stt_(out, a, s, b, op0, op1, reads, writes):
            return P.op("dve", lambda e: e.scalar_tensor_tensor(out=out, in0=a, scalar=s, in1=b, op0=op0, op1=op1), reads, writes)

        def cp(eng, out, in_, reads, writes):
            if eng == "act":
                return P.op("act", lambda e: e.copy(out=out, in_=in_), reads, writes)
            return P.op(eng, lambda e: e.tensor_copy(out=out, in_=in_), reads, writes)

        def mmg(out, pairs, reads, writes):
            def fn(e):
                n = len(pairs)
                inst = None
                for i, (l, r) in enumerate(pairs):
                    inst = e.matmul(out, lhsT=l, rhs=r, start=(i == 0), stop=(i == n - 1))
                return inst
            return P.op("pe", fn, reads, writes)

        def mms(items, reads, writes):
            def fn(e):
                inst = None
                for (o, l, r) in items:
                    inst = e.matmul(o, lhsT=l, rhs=r, start=True, stop=True)
                return inst
            return P.op("pe", fn, reads, writes)

        class Pool_:
            def __init__(self, name, n):
                self.name, self.n, self.i, self.base = name, n, 0, 0

            def next(self):
                k = self.base + self.i % self.n
                self.i += 1
                return k

        plL = Pool_("pb", 4)
        plL.base = 4
        plS = Pool_("pS", 2)
        plO = Pool_("pO", 2)
        plO.base = 2
        plW = Pool_("w", NSLOT)
        plPG = Pool_("pTg", 4)
        plM = Pool_("sm", 8)
        plR = Pool_("rsb", 2)

        def bk(i):
            return ps_t[:, i, :]

        def pbk(i):
            return ("pb", i)

        def psL(i, n):
            return ps_t[:, i, 0:n]

        def wload(name, j, k0=0, kn=None):
            off, kc, nb = WOFF[name]
            if kn is None:
                kn = kc
            s = plW.next()
            c0 = off + j * kc * 128 + k0 * 128
            c1 = c0 + kn * 128
            rk = [("wbf", ci) for ci in range(c0 // CH, (c1 - 1) // CH + 1)]
            dma(wring[:, s, 0:kn * 128], wbf[:, c0:c1], rk, [("w", s)])
            return wring[:, s, 0:kn * 128].rearrange("p (k m) -> p k m", m=128), ("w", s)

        def lin(name, j, rhs, rkeys, n, pi=None, col=0):
            off, kc, nb = WOFF[name]
            wv, wk = wload(name, j)
            if pi is None:
                pi = plL.next()
            mmg(ps_t[:, pi, col:col + n], [(wv[:, k, :], rhs[:, k, 0:n]) for k in range(kc)], [wk] + rkeys, [("psL", pi)])
            return pi

        def rstd_of(src, skeys, n, nchunks=DC):
            sq = sqb[:, 0:nchunks, 0:n]
            act_fn(sq, src, AF.Square, skeys, [("sqb",)])
            s1 = plM.next()
            P.op("dve", lambda e: e.tensor_reduce(out=sm[:, s1, 0:n], in_=sq.rearrange("p c t -> p t c"),
                                                   axis=AX.X, op=ALU.add), [("sqb",)], [("sm", s1)])
            s3 = plM.next()
            hi = sm[:, s3, 0:n].bitcast(BF16)[:, 0:n]
            lo = sm[:, s3, 0:n].bitcast(BF16)[:, n:2 * n]
            cp("dve", hi, sm[:, s1, 0:n], [("sm", s1)], [("sm", s3)])
            tt("dve", sm[:, s1, 0:n], sm[:, s1, 0:n], hi, ALU.subtract, [("sm", s1), ("sm", s3)], [("sm", s1)])
            cp("dve", lo, sm[:, s1, 0:n], [("sm", s1), ("sm", s3)], [("sm", s3)])
            pi = plL.next()
            mmg(psL(pi, n), [(ones_b[:, :], hi), (ones_b[:, :], lo)], [("sm", s3), ("ones",)], [("psL", pi)])
            s2 = plR.next()
            ts("dve", rsb[:, s2, 0:n], psL(pi, n), 1.0 / D, ALU.mult, [("psL", pi)], [("rsb", s2)], s2=EPS, op1=ALU.add)
            act_fn(rsb[:, s2, 0:n], rsb[:, s2, 0:n], AF.Sqrt, [("rsb", s2)], [("rsb", s2)])
            P.op("dve", lambda e: e.reciprocal(out=rsb[:, s2, 0:n], in_=rsb[:, s2, 0:n]), [("rsb", s2)], [("rsb", s2)])
            return s2

        def prenorm(src, srckey, gname, n):
            keys = [(srckey, c) for c in range(DC)]
            s2 = rstd_of(src[:, :, 0:n], keys, n)
            g = spc(gname)
            for c in range(DC):
                stt_(xn[:, c, 0:n], src[:, c, 0:n], g[:, c:c + 1], rsb[:, s2, 0:n], ALU.mult, ALU.mult,
                     [(srckey, c), ("rsb", s2), ("spt",)], [("xn", c)])

        def postnorm_res(gname, factor, n):
            keys = [("yb", c) for c in range(DC)]
            s2 = rstd_of(yb[:, :, 0:n], keys, n)
            g = spc(gname)
            for c in range(DC):
                stt_(yb[:, c, 0:n], yb[:, c, 0:n], g[:, c:c + 1], rsb[:, s2, 0:n], ALU.mult, ALU.mult,
                     [("yb", c), ("rsb", s2), ("spt",)], [("yb", c)])
            for c in range(DC):
                stt_(h[:, c, 0:n], yb[:, c, 0:n], float(factor), h[:, c, 0:n], ALU.mult, ALU.add,
                     [("yb", c), ("h", c)], [("h", c)])

        def prenorm_next_stats(nn, src=None, kfn=None):
            if src is None:
                src = xpre
                kfn = lambda c: XPK
            a0, t1_, t2_ = 0, 1, 2
            tmps = [t1_, t2_]
            for c in range(DC):
                tsl = tmps[c % 2]
                dst = sm[:, a0, 0:nn] if c == 0 else sm[:, tsl, 0:nn]
                dk = ("sm", a0) if c == 0 else ("sm", tsl)
                act_fn(dst, src[:, c, 0:nn], AF.Square, kfn(c), [dk])
                if c > 0:
                    tt("dve", sm[:, a0, 0:nn], sm[:, a0, 0:nn], sm[:, tsl, 0:nn], ALU.add,
                       [("sm", a0), ("sm", tsl)], [("sm", a0)])
            hi = sm[:, t1_, 0:nn].bitcast(BF16)[:, 0:nn]
            lo = sm[:, t1_, 0:nn].bitcast(BF16)[:, nn:2 * nn]
            cp("dve", hi, sm[:, a0, 0:nn], [("sm", a0)], [("sm", t1_)])
            tt("dve", sm[:, a0, 0:nn], sm[:, a0, 0:nn], hi, ALU.subtract, [("sm", a0), ("sm", t1_)], [("sm", a0)])
            cp("dve", lo, sm[:, a0, 0:nn], [("sm", a0), ("sm", t1_)], [("sm", t1_)])
            return (hi, lo, t1_)

        def prenorm_next_rstd(st, nn):
            hi, lo, t1_ = st
            pi = plL.next()
            mmg(psL(pi, nn), [(ones_b[:, :], hi), (ones_b[:, :], lo)], [("sm", t1_), ("ones",)], [("psL", pi)])
            s2 = plR.next()
            ts("dve", rsb[:, s2, 0:nn], psL(pi, nn), 1.0 / D, ALU.mult, [("psL", pi)], [("rsb", s2)], s2=EPS, op1=ALU.add)
            act_fn(rsb[:, s2, 0:nn], rsb[:, s2, 0:nn], AF.Sqrt, [("rsb", s2)], [("rsb", s2)])
            P.op("dve", lambda e: e.reciprocal(out=rsb[:, s2, 0:nn], in_=rsb[:, s2, 0:nn]), [("rsb", s2)], [("rsb", s2)])
            return s2

        def prenorm_next_apply(s2, gname, nn):
            g = spc(gname)
            for c in range(DC):
                stt_(xn[:, c, 0:nn], xpre[:, c, 0:nn], g[:, c:c + 1], rsb[:, s2, 0:nn], ALU.mult, ALU.mult,
                     XPK + [("rsb", s2), ("spt",)], [("xn", c)])

        def ffn(pre, post, wg, wu, wd, n, skip_prenorm=False, hooks=None, defer_post=False):
            if not skip_prenorm:
                prenorm(h, "h", pre, n)
            xk = [("xn", c) for c in range(DC)]
            for j in range(FC):
                if hooks is not None and j in hooks:
                    hooks[j]()
                pg = lin(wg, j, xn, xk, n)
                lin(wu, j, xn, xk, n, pi=pg, col=256)
                s1 = plM.next()
                act_fn(sm[:, s1, 0:n], psL(pg, n), AF.Silu, [("psL", pg)], [("sm", s1)])
                tt("dve", act[:, j, 0:n], sm[:, s1, 0:n], ps_t[:, pg, 256:256 + n], ALU.mult,
                   [("sm", s1), ("psL", pg)], [("act", j)])
            if hooks is not None and "mid" in hooks:
                hooks["mid"]()
            for j in range(DC):
                pi = plL.next()
                parts = [(0, 16), (16, 16), (32, 12)]
                for pidx, (k0, kn) in enumerate(parts):
                    wv, wk = wload(wd, j, k0, kn)

                    def fn(e, wv=wv, k0=k0, kn=kn, pidx=pidx, pi=pi):
                        inst = None
                        for k in range(kn):
                            inst = e.matmul(psL(pi, n), lhsT=wv[:, k, :], rhs=act[:, k0 + k, 0:n],
                                            start=(pidx == 0 and k == 0), stop=(pidx == 2 and k == kn - 1))
                        return inst
                    P.op("pe", fn, [wk] + [("act", k0 + k) for k in range(kn)] + ([("psL", pi)] if pidx else []),
                         [("psL", pi)])
                cp("act", yb[:, j, 0:n], psL(pi, n), [("psL", pi)], [("yb", j)])
            if not defer_post:
                postnorm_res(post, 0.5, n)

        C1 = 6.28125
        C2 = 2.0 * math.pi - 6.28125

        def sin_of(dst, src, shift, tmpA, tmpI, skeys, dkey, akey, ikey):
            ts("dve", tmpA, src, float(shift), ALU.add, skeys, [akey], s2=1.0 / (2.0 * math.pi), op1=ALU.mult)
            cp("dve", tmpI, tmpA, [akey], [ikey])
            cp("dve", tmpA, tmpI, [ikey], [akey])
            ts("dve", dst, src, float(shift), ALU.add, skeys, [dkey])
            stt_(dst, tmpA, -C1, dst, ALU.mult, ALU.add, [akey, dkey], [dkey])
            stt_(dst, tmpA, -C2, dst, ALU.mult, ALU.add, [akey, dkey], [dkey])
            ts("dve", dst, dst, -math.pi, ALU.max, [dkey], [dkey], s2=math.pi, op1=ALU.min)
            act_fn(dst, dst, AF.Sin, [dkey], [dkey])

        P.op("dve", lambda e: e.memset(ones_f[:, :], 1.0), [], [("ones",)])
        P.op("dve", lambda e: e.memset(ones_b[:, :], 1.0), [], [("ones",)])
        P.op("pool", lambda e: e.memset(stt[:, :, :, :], 0.0), [], [("stt", 0), ("stt", 1)])
        dma(spt[:, :], spk, [], [("spt",)])
        dma(sbt[:, :, :, :], btd[:, SBT_OFF:SBT_OFF + 512].rearrange("p (a h q) -> p a h q", a=2, h=16), [], [("sbt",)])
        P.const.add(("ones",))
        P.const.add(("spt",))
        P.const.add(("sbt",))

        cengs = ["act", "dve", "pool"]

        def conv_in(ci):
            dma(stg_f[ci % NSF], wall[:, ci * CH:(ci + 1) * CH], [], [("stgf", ci % NSF)])
        for ci in range(min(NSF, NCHUNK)):
            conv_in(ci)
        for ci in range(NCHUNK):
            cp(cengs[ci % 3], stg_b[ci % NSB], stg_f[ci % NSF], [("stgf", ci % NSF)], [("stgb", ci % NSB)])
            dma(wbf[:, ci * CH:(ci + 1) * CH], stg_b[ci % NSB], [("stgb", ci % NSB)], [("wbf", ci)])
            if ci + NSF < NCHUNK:
                conv_in(ci + NSF)
        for ci in range(NCHUNK):
            P.const.add(("wbf", ci))
        P.op("act", lambda e: e.copy(out=sm[:, 1, 0:1], in_=ones_f[:, 0:1]), [("ones",)],
             [("stgf", i) for i in range(NSF)] + [("stgb", i) for i in range(NSB)] + [("pq", i) for i in range(10)]
             + [("yb", i) for i in range(DC)] + [("sm", 1)])

        if STOP >= 1:
            PI = math.pi
            ts("dve", rth[:, 0, :], spc("ldt"), 0.0, ALU.add, [("spt",)], [("rth",)])
            act_fn(rth[:, 0, :], rth[:, 0, :], AF.Exp, [("rth",)], [("rth",)])
            tt("dve", rth[:, 1, :], spc("are"), rth[:, 0, :], ALU.mult, [("rth",), ("spt",)], [("rth",)])
            act_fn(rth[:, 1, :], rth[:, 1, :], AF.Exp, [("rth",)], [("rth",)])
            tt("dve", rth[:, 2, :], spc("aim"), rth[:, 0, :], ALU.mult, [("rth",), ("spt",)], [("rth",)])
            jx = spc("jidx")
            for half in range(2):
                ph = pq[0].rearrange("p (a j) -> p a j", j=64)
                cs = pq[1].rearrange("p (a j) -> p a j", j=64)
                sn = pq[2].rearrange("p (a j) -> p a j", j=64)
                rr = pq[3].rearrange("p (a j) -> p a j", j=64)
                for a in range(16):
                    pr = half * 16 + a
                    ts("dve", ph[:, a, :], jx, rth[:, 2, pr:pr + 1], ALU.mult, [("rth",), ("spt",)], [("pq", 0)])
                sin_of(cs, ph, 0.5 * PI, pq[4].rearrange("p (a j) -> p a j", j=64), pq[5].bitcast(I32).rearrange("p (a j) -> p a j", j=64),
                       [("pq", 0)], ("pq", 1), ("pq", 4), ("pq", 5))
                sin_of(sn, ph, 0.0, pq[4].rearrange("p (a j) -> p a j", j=64), pq[5].bitcast(I32).rearrange("p (a j) -> p a j", j=64),
                       [("pq", 0)], ("pq", 2), ("pq", 4), ("pq", 5))
                cp("pool", rr, rth[:, 1, half * 16:(half + 1) * 16].unsqueeze(2).to_broadcast([128, 16, 64]),
                   [("rth",)], [("pq", 3)])
                P.op("pool", lambda e, rr=rr: e.memset(rr[:, :, 0:1], 0.0), [("pq", 3)], [("pq", 3)])
                dma(tabd[:, 0, half * 16:(half + 1) * 16, :], cs, [("pq", 1)], [("tabd",)])
                dma(tabd[:, 1, half * 16:(half + 1) * 16, :], sn, [("pq", 2)], [("tabd",)])
                dma(tabd[:, 2, half * 16:(half + 1) * 16, :], rr, [("pq", 3)], [("tabd",)])
            for qt in range(4):
                A_re, A_im, LDT, b_re, b_im = pq[0], pq[1], pq[2], pq[3], pq[4]
                t0, t1, t2, t3, t4 = pq[5], pq[6], pq[7], pq[8], pq[9]
                for i, dst in enumerate([A_re, A_im, LDT, b_re, b_im]):
                    dma(dst, ssmB[:, qt, i, :], [], [("pq", i)])
                K = lambda *ix: [("pq", i) for i in ix]
                act_fn(LDT, LDT, AF.Exp, K(2), K(2))
                tt("dve", t1, A_im, LDT, ALU.mult, K(1, 2), K(6))
                sin_of(t2, t1, 0.5 * PI, t0, t4.bitcast(I32), K(6), ("pq", 7), ("pq", 5), ("pq", 9))
                sin_of(t3, t1, 0.0, t0, t4.bitcast(I32), K(6), ("pq", 8), ("pq", 5), ("pq", 9))
                tt("dve", t0, A_re, LDT, ALU.mult, K(0, 2), K(5))
                act_fn(t0, t0, AF.Exp, K(5), K(5))
                tt("dve", t2, t2, t0, ALU.mult, K(7, 5), K(7))
                tt("dve", t3, t3, t0, ALU.mult, K(8, 5), K(8))
                ts("dve", t2, t2, -1.0, ALU.add, K(7), K(7))
                tt("dve", t0, A_re, A_re, ALU.mult, K(0), K(5))
                tt("dve", t1, A_im, A_im, ALU.mult, K(1), K(6))
                tt("dve", t0, t0, t1, ALU.add, K(5, 6), K(5))
                P.op("dve", lambda e, t0=t0: e.reciprocal(out=t0, in_=t0), K(5), K(5))
                tt("dve", t1, t2, A_re, ALU.mult, K(7, 0), K(6))
                tt("dve", t4, t3, A_im, ALU.mult, K(8, 1), K(9))
                tt("dve", t1, t1, t4, ALU.add, K(6, 9), K(6))
                tt("dve", t1, t1, t0, ALU.mult, K(6, 5), K(6))
                tt("dve", t4, t3, A_re, ALU.mult, K(8, 0), K(9))
                tt("dve", t3, t2, A_im, ALU.mult, K(7, 1), K(8))
                tt("dve", t4, t4, t3, ALU.subtract, K(9, 8), K(9))
                tt("dve", t4, t4, t0, ALU.mult, K(9, 5), K(9))
                tt("dve", t0, t1, b_re, ALU.mult, K(6, 3), K(5))
                tt("dve", t2, t4, b_im, ALU.mult, K(9, 4), K(7))
                tt("dve", t0, t0, t2, ALU.subtract, K(5, 7), K(5))
                tt("dve", t2, t1, b_im, ALU.mult, K(6, 4), K(7))
                tt("dve", t3, t4, b_re, ALU.mult, K(9, 3), K(8))
                tt("dve", t2, t2, t3, ALU.add, K(7, 8), K(7))
                bst = pq[1].bitcast(BF16)[:, 0:2048].rearrange("p (a r m) -> p a r m", a=8, r=2)
                cp("pool", bst[:, :, 0, :], t0.rearrange("p (a m) -> p a m", a=8), K(5, 1), K(1))
                cp("pool", bst[:, :, 1, :], t2.rearrange("p (a m) -> p a m", a=8), K(7, 1), K(1))
                dma(ssmw[:, qt, 0, :], bst.rearrange("p a r m -> p (a r m)"), K(1), [("ssmw", qt, 0)])
                cst_f = pq[3].bitcast(F32)
                cf = R[:, (3 * 4096) // 2:(5 * 4096) // 2].bitcast(F32)
                dma(cf, ssmC[:, qt, :], K(5, 7), K(3, 4))
                cfv = cf.rearrange("p (r a m) -> p r a m", r=2, a=8)
                cbt = pq[6].bitcast(BF16)[:, 0:2048].rearrange("p (a r m) -> p a r m", a=8, r=2)
                cp("pool", cbt[:, :, 0, :], cfv[:, 0, :, :], K(3, 4, 6), K(6))
                ts("dve", cbt[:, :, 1, :], cfv[:, 1, :, :], -1.0, ALU.mult, K(3, 4, 6), K(6))
                dma(ssmw[:, qt, 1, :], cbt.rearrange("p a r m -> p (a r m)"), K(6), [("ssmw", qt, 1)])
            for qt in range(4):
                P.const.add(("ssmw", qt, 0))
                P.const.add(("ssmw", qt, 1))
            P.const.add(("tabd",))
        def ssm(n, L, nseg, st_views, final_out, fillers, nfill):
            plL.n = 2
            plL.i = 0
            plW.base, plW.n = 4, NSLOT - 4
            psbk = [("psL", 6), ("psL", 7)]
            nsteps = 2 * nseg
            per_pt = (nfill + 4 * nsteps - 1) // (4 * nsteps)

            def fill():
                for _ in range(per_pt):
                    next(fillers, None)
            for hf in range(2):
                p16 = slice(hf * 16, hf * 16 + 16)
                Bv, Cv, wkB, wkC = [], [], [], []
                for qi in range(2):
                    qt = 2 * hf + qi
                    sB = 2 * qi
                    dma(wring[:, sB, :], ssmw[:, qt, 0, :], [("ssmw", qt, 0)], [("w", sB)])
                    sC = 2 * qi + 1
                    dma(wring[:, sC, :], ssmw[:, qt, 1, :], [("ssmw", qt, 1)], [("w", sC)])
                    Bv.append(wring[:, sB, :].rearrange("p (a m) -> p a m", m=128))
                    Cv.append(wring[:, sC, :].rearrange("p (a m) -> p a m", m=128))
                    wkB.append(("w", sB))
                    wkC.append(("w", sC))
                dma(tabq[:, :, :, :], tabd[:, :, p16, :], [("tabd",)], [("tabq",)])
                if L != 64:
                    cp("pool", rt16[:, :, :], rth[:, 1, p16].unsqueeze(2).to_broadcast([128, 16, L]), [("rth",)], [("rt16",)])
                    P.op("pool", lambda e: e.memset(rt16[:, :, 0:1], 0.0), [("rt16",)], [("rt16",)])
                Ct = tabq[:, 0, :, 0:L]
                St = tabq[:, 1, :, 0:L]
                if L == 64:
                    Rt2 = tabq[:, 2, :, :].rearrange("p a j -> p (a j)")
                    rtk = ("tabq",)
                else:
                    Rt2 = rt16[:, :, :].rearrange("p a j -> p (a j)")
                    rtk = ("rt16",)
                for seg in range(nseg):
                    tok = slice(seg * L, (seg + 1) * L)
                    zre, zim, zk = st_views[seg]
                    tv = [sw[:, i, 0:16 * L].rearrange("p (a j) -> p a j", a=16) for i in range(6)]
                    for sbt_ in range(2):
                        p8 = slice(8 * sbt_, 8 * sbt_ + 8)
                        psb = ps_t[:, 6:8, :].rearrange("p a b -> p (a b)")[:, 0:16 * L]
                        psb4 = psb.rearrange("p (a r j) -> p a r j", a=8, r=2)
                        items = []
                        for pq_ in range(8):
                            pp = 8 * sbt_ + pq_
                            ch = 4 * hf + pp // 4
                            for ri in range(2):
                                items.append((psb4[:, pq_, ri, :], Bv[pp // 8][:, (pp % 8) * 2 + ri, :], usb[:, ch, tok]))
                        mms(items, wkB + [("usb", 4 * hf + i) for i in range(4)], psbk)
                        bre = psb4[:, :, 0, :]
                        bim = psb4[:, :, 1, :]
                        sk = lambda i: ("sw", i, sbt_)
                        tt("dve", tv[0][:, p8, :], Ct[:, p8, :], bre, ALU.mult, [("tabq",)] + psbk, [sk(0)])
                        tt("dve", tv[1][:, p8, :], St[:, p8, :], bim, ALU.mult, [("tabq",)] + psbk, [sk(1)])
                        tt("dve", tv[2][:, p8, :], Ct[:, p8, :], bim, ALU.mult, [("tabq",)] + psbk, [sk(2)])
                        tt("dve", tv[3][:, p8, :], St[:, p8, :], bre, ALU.mult, [("tabq",)] + psbk, [sk(3)])
                    SW = lambda i: [("sw", i, 0), ("sw", i, 1)]
                    tt("pool", tv[4], tv[0], tv[1], ALU.add, SW(0) + SW(1), SW(4))
                    tt("pool", tv[5], tv[2], tv[3], ALU.subtract, SW(2) + SW(3), SW(5))
                    fill()
                    s1 = plM.next()
                    tt("dve", sm[:, s1, 0:16], rth[:, 1, p16], zre[:, p16], ALU.mult, [("rth",), zk], [("sm", s1)])
                    tt("dve", sm[:, s1, 16:32], rth[:, 1, p16], zim[:, p16], ALU.mult, [("rth",), zk], [("sm", s1)])
                    tt("dve", tv[4][:, :, 0:1], tv[4][:, :, 0:1], sm[:, s1, 0:16].unsqueeze(2), ALU.add,
                       SW(4) + [("sm", s1)], SW(4))
                    tt("dve", tv[5][:, :, 0:1], tv[5][:, :, 0:1], sm[:, s1, 16:32].unsqueeze(2), ALU.add,
                       SW(5) + [("sm", s1)], SW(5))
                    w4 = sw[:, 4, 0:16 * L]
                    w5 = sw[:, 5, 0:16 * L]
                    P.op("dve", lambda e, w4=w4, Rt2=Rt2: e.tensor_tensor_scan(out=w4, data0=Rt2, data1=w4, initial=0.0,
                                                                           op0=ALU.mult, op1=ALU.add),
                         SW(4) + [rtk], SW(4))
                    P.op("dve", lambda e, w5=w5, Rt2=Rt2: e.tensor_tensor_scan(out=w5, data0=Rt2, data1=w5, initial=0.0,
                                                                           op0=ALU.mult, op1=ALU.add),
                         SW(5) + [rtk], SW(5))
                    fill()
                    tt("dve", tv[0], Ct, tv[4], ALU.mult, [("tabq",)] + SW(4), SW(0))
                    tt("dve", tv[1], St, tv[5], ALU.mult, [("tabq",)] + SW(5), SW(1))
                    tt("pool", tv[2], Ct, tv[5], ALU.mult, [("tabq",)] + SW(5), SW(2))
                    tt("pool", tv[3], St, tv[4], ALU.mult, [("tabq",)] + SW(4), SW(3))
                    sre = ssb[:, :, 0, 0:L]
                    sim = ssb[:, :, 1, 0:L]
                    tt("dve", sre, tv[0], tv[1], ALU.subtract, SW(0) + SW(1), [("ssb",)])
                    tt("pool", sim, tv[2], tv[3], ALU.add, SW(2) + SW(3), [("ssb",)])
                    tt("dve", zre[:, p16].unsqueeze(2), tv[0][:, :, L - 1:L], tv[1][:, :, L - 1:L], ALU.subtract,
                       SW(0) + SW(1), [zk])
                    tt("dve", zim[:, p16].unsqueeze(2), tv[2][:, :, L - 1:L], tv[3][:, :, L - 1:L], ALU.add,
                       SW(2) + SW(3), [zk])
                    fill()
                    dc = spc("dcol")
                    for fc in range(4):
                        ch = 4 * hf + fc
                        pc = plL.next()
                        prs = []
                        for pp in range(4 * fc, 4 * fc + 4):
                            for ri in range(2):
                                prs.append((Cv[pp // 8][:, (pp % 8) * 2 + ri, :], ssb[:, pp, ri, 0:L]))
                        mmg(ps_t[:, pc, 0:L], prs, wkC + [("ssb",)], [("psL", pc)])
                        stt_(usf[:, ch, tok], usf[:, ch, tok], dc[:, ch:ch + 1], ps_t[:, pc, 0:L], ALU.mult, ALU.add,
                             [("usf", ch), ("psL", pc), ("spt",)], [("usf", ch)])
                    fill()
            for _ in fillers:
                pass
            plL.n = 4
            plW.base, plW.n = 0, NSLOT
            if final_out is not None:
                final_out()
            for c in range(8):
                s1 = plM.next()
                a = sm[:, s1, 0:n]
                y = usf[:, c, 0:n]
                tt("dve", a, y, y, ALU.mult, [("usf", c)], [("sm", s1)])
                ts("dve", a, a, 0.044715, ALU.mult, [("sm", s1)], [("sm", s1)], s2=1.0, op1=ALU.add)
                tt("dve", a, a, y, ALU.mult, [("sm", s1), ("usf", c)], [("sm", s1)])
                act_fn(a, a, AF.Sigmoid, [("sm", s1)], [("sm", s1)], scale=2.0 * math.sqrt(2.0 / math.pi))
                tt("dve", y, y, a, ALU.mult, [("sm", s1), ("usf", c)], [("usf", c)])
                cp("pool", usb[:, c, 0:n], y, [("usf", c)], [("usb", c)])
            bg = spc("bglu")
            for j in range(8):
                pi = lin("glu", j, usb, [("usb", c) for c in range(8)], n)
                s1 = plM.next()
                act_fn(sm[:, s1, 0:n], psL(pi, n), AF.Sigmoid, [("psL", pi), ("spt",)], [("sm", s1)], bias=bg[:, j:j + 1])
                tt("dve", os_[:, j, 0:n], usf[:, j, 0:n], sm[:, s1, 0:n], ALU.mult, [("usf", j), ("sm", s1)], [("os", j)])

        def memattn(n, tok):
            for _ in memattn_gen(n, tok):
                pass

        def memattn_gen(n, tok):
            for hm in range(4):
                yield
                pms = []
                for mb in range(2):
                    pi = plL.next()
                    mmg(psL(pi, n), [(mkT[:, 2 * hm + dcc, mb * 128:(mb + 1) * 128], qT[:, 2 * hm + dcc, tok])
                                     for dcc in range(2)],
                        [("mkT",), ("q", 2 * hm), ("q", 2 * hm + 1)], [("psL", pi)])
                    s1 = plM.next()
                    pm = sm[:, s1, 0:n].bitcast(BF16)[:, 0:n]
                    act_fn(pm, psL(pi, n), AF.Exp, [("psL", pi)], [("sm", s1)])
                    pms.append((pm, s1))
                pd = plL.next()
                mmg(psL(pd, n), [(ones_b[:, :], pm) for pm, _ in pms], [("sm", s) for _, s in pms] + [("ones",)], [("psL", pd)])
                s2 = plM.next()
                P.op("dve", lambda e, s2=s2, pd=pd: e.reciprocal(out=sm[:, s2, 0:n], in_=psL(pd, n)), [("psL", pd)], [("sm", s2)])
                for dcc in range(2):
                    po = plL.next()
                    c = 2 * hm + dcc
                    mmg(psL(po, n), [(mvb[:, mb, c * 128:(c + 1) * 128], pms[mb][0]) for mb in range(2)],
                        [("mvb",)] + [("sm", s) for _, s in pms], [("psL", po)])
                    tt("dve", om[:, c, tok], psL(po, n), sm[:, s2, 0:n], ALU.mult, [("psL", po), ("sm", s2)], [("om", c)])

        def attn_S(c, e2, qcols, nq, blocks):
            prow = slice(e2 * 64, e2 * 64 + 64)
            sbk = plS.next()
            items = []
            allk = []
            for bi_, (kfn, vl, nk, bfn, keys) in enumerate(blocks):
                items.append((ps_t[0:nk, sbk, bi_ * 64:bi_ * 64 + nq], kfn(prow), qT[prow, c, qcols]))
                allk += keys
            mms(items, allk + [("q", c)], [("psL", sbk)])
            return sbk

        def attn_rest(c, e2, qcols, nq, blocks, okey, sbk):
            chv = spc("ch")
            hh = 2 * c + e2
            prow = slice(e2 * 64, e2 * 64 + 64)
            pg = plPG.next()
            nb_ = len(blocks)
            merge_ok = (nq == 64) and all(bl[2] == 128 for bl in blocks)
            runs = []
            for bi_, (kfn, vl, nk, bfn, keys) in enumerate(blocks):
                bt = bfn(e2) if bfn is not None else None
                typ = None if bt is None else (bt[2] if len(bt) > 2 else -100 - bi_)
                if runs and merge_ok:
                    r = runs[-1]
                    if (r["typ0"] is None and typ is None) or \
                       (r["typ0"] is not None and typ is not None and typ == r["typ0"] + r["n"] and typ >= 0):
                        r["n"] += 1
                        continue
                runs.append({"b0": bi_, "n": 1, "typ0": typ, "bt": bt, "nk": nk})
            pts = []
            for r in runs:
                b0, nr, nk = r["b0"], r["n"], r["nk"]
                if merge_ok:
                    ps = ps_t[:, sbk, b0 * 64:(b0 + nr) * 64]
                    pv = pT[:, pg * 5 + b0:pg * 5 + b0 + nr, :].rearrange("p a q -> p (a q)")
                else:
                    ps = ps_t[0:nk, sbk, b0 * 64:b0 * 64 + nq]
                    pv = pT[0:nk, pg * 5 + b0, 0:nq]
                if r["typ0"] is None:
                    act_fn(pv, ps, AF.Exp, [("psL", sbk), ("spt",)], [("pTg", pg)], bias=chv[0:nk, hh:hh + 1])
                else:
                    bap, bkey = r["bt"][0], r["bt"][1]
                    s1 = plM.next()
                    if merge_ok:
                        bap = r["bt"][3](r["typ0"], nr)
                        tmpv = sm[:, s1, 0:nr * 64]
                    else:
                        tmpv = sm[0:nk, s1, 0:nq]
                    tt("dve", tmpv, ps, bap, ALU.add, [("psL", sbk), bkey], [("sm", s1)])
                    act_fn(pv, tmpv, AF.Exp, [("sm", s1)], [("pTg", pg)])
            for bi_, (kfn, vl, nk, bfn, keys) in enumerate(blocks):
                pts.append((pT[0:nk, pg * 5 + bi_, 0:nq], vl, nk, keys))
            ob = plO.next()
            mmg(ps_t[:, ob, 0:nq], [(vl, pv) for (pv, vl, nk, keys) in pts],
                [("pTg", pg)] + sum([k for (_, _, _, k) in pts], []), [("psL", ob)])
            mmg(ps_t[:, ob, 64:64 + nq], [(ones_b[0:nk, :], pv) for (pv, vl, nk, keys) in pts],
                [("pTg", pg), ("ones",)], [("psL", ob)])
            return (ob, prow, c, qcols, nq, okey)

        def attn_norm(st):
            ob, prow, c, qcols, nq, okey = st
            s2 = plM.next()
            P.op("dve", lambda e, s2=s2, ob=ob, prow=prow: e.reciprocal(out=sm[prow, s2, 0:nq], in_=ps_t[prow, ob, 64:64 + nq]),
                 [("psL", ob)], [("sm", s2)])
            tt("dve", oa[prow, c, qcols], ps_t[prow, ob, 0:nq], sm[prow, s2, 0:nq], ALU.mult,
               [("psL", ob), ("sm", s2)], [okey])

        LA = 1

        def attn_gen(work, pre=None):
            sbs = {}
            nxt = 0
            pend = None
            for k in range(len(work)):
                if pre is not None:
                    pre(k)
                while nxt < len(work) and nxt <= k + LA:
                    sbs[nxt] = attn_S(*work[nxt][0:5])
                    nxt += 1
                st_new = attn_rest(*work[k], sbs.pop(k))
                if pend is not None:
                    attn_norm(pend)
                pend = st_new
                if k == len(work) - 1:
                    attn_norm(pend)
                    pend = None
                yield

        def attn_run(work, pre=None):
            for _ in attn_gen(work, pre):
                pass

        def tile_body(n, mode, ti, pre_done=False, nxt=None, dhooks=None):
            xk = [("xn", c) for c in range(DC)]
            if dhooks is not None:
                plM.base, plM.n = 3, 5
            ffn("g_f1pre", "g_f1post", "f1g", "f1u", "f1d", n, skip_prenorm=pre_done, hooks=dhooks)
            if TSTOP < 4:
                return
            prenorm(h, "h", "g_mpre", n)
            for c in range(8):
                pi = lin("win", c, xn, xk, n)
                cp("act", usf[:, c, 0:n], psL(pi, n), [("psL", pi)], [("usf", c)])
                cp("dve", usb[:, c, 0:n], psL(pi, n), [("psL", pi)], [("usb", c)])

            def pre_gen():
                need_kv_out = (mode == "sample") or ti >= NT - 2
                kf = yb[:, 0:8, 0:n]
                vf = yb[:, 8:16, :].rearrange("p c t -> p (c t)")
                if mode == "prompt":
                    kslots = [(2 * ti) % 6, (2 * ti + 1) % 6]
                for c in range(8):
                    pi = lin("win", 16 + c, xn, xk, n)
                    if mode == "prompt":
                        for tb in range(2):
                            cp("act", kring[:, c, kslots[tb], :], psL(pi, n)[:, tb * 128:(tb + 1) * 128], [("psL", pi)],
                               [("kr", c, kslots[tb])])
                    else:
                        cp("act", kring[:, c, 4, 0:16], psL(pi, n)[:, 0:16], [("psL", pi)], [("kr", c, 4)])
                        cp("act", kring[:, c, 5, 0:16], psL(pi, n)[:, 16:32], [("psL", pi)], [("kr", c, 5)])
                    if need_kv_out:
                        cp("dve", kf[:, c, :], psL(pi, n), [("psL", pi)], [("yb", c)])
                    yield
                if need_kv_out:
                    if mode == "prompt":
                        t0 = (ti - (NT - 2)) * T
                        dma(okT.rearrange("(c p) t -> p c t", p=128)[:, :, t0:t0 + T], kf, [("yb", c) for c in range(8)], [("okT", ti)])
                    else:
                        dma(oskT.rearrange("(c p) t -> p c t", p=128), kf, [("yb", c) for c in range(8)], [("oskT",)])
                for c in range(8):
                    wv, wk = wload("win", 24 + c)
                    if mode == "prompt":
                        for tb in range(2):
                            pi = plL.next()
                            mmg(psL(pi, 128), [(xn[:, k, tb * 128:(tb + 1) * 128], wv[:, k, :]) for k in range(16)], [wk] + xk, [("psL", pi)])
                            cp("act", vring[:, kslots[tb], c * 128:(c + 1) * 128], psL(pi, 128), [("psL", pi)], [("vr", kslots[tb], c)])
                            if need_kv_out:
                                cp("dve", vf[:, tb * 1024 + c * 128: tb * 1024 + (c + 1) * 128], psL(pi, 128), [("psL", pi)], [("yb", 8 + tb * 4 + c // 2)])
                    else:
                        for b2 in range(2):
                            pi = plL.next()
                            mmg(psL(pi, 128)[0:16, :], [(xn[:, k, b2 * 16:(b2 + 1) * 16], wv[:, k, :]) for k in range(16)], [wk] + xk, [("psL", pi)])
                            cp("act", vring[0:16, 4 + b2, c * 128:(c + 1) * 128], psL(pi, 128)[0:16, :], [("psL", pi)], [("vr", 4 + b2, c)])
                            cp("dve", vf[0:16, b2 * 1024 + c * 128: b2 * 1024 + (c + 1) * 128], psL(pi, 128)[0:16, :], [("psL", pi)], [("yb", 8 + b2 * 4 + c // 2)])
                    yield
                if need_kv_out:
                    vkeys = [("yb", 8 + i) for i in range(8)]
                    if mode == "prompt":
                        t0 = (ti - (NT - 2)) * T
                        dma(ov[t0:t0 + T, :].rearrange("(b p) f -> p b f", p=128), vf.rearrange("p (b f) -> p b f", b=2), vkeys, [("ov", ti)])
                    else:
                        dma(osv.rearrange("b p f -> p b f"), vf[0:16, :].rearrange("p (b f) -> p b f", b=2), vkeys, [("osv",)])
                for c in range(8):
                    pi = lin("win", 8 + c, xn, xk, n)
                    ts("dve", qT[:, c, 0:n], psL(pi, n), 0.125, ALU.mult, [("psL", pi)], [("q", c)])
                    yield
                if mode == "prompt":
                    work = []
                    for c in range(8):
                        bb = c % 2
                        for qi in range(4):
                            qc = 4 * ti + qi
                            par = qc % 2
                            blocks = []
                            for b in range(5):
                                gb = qc // 2 - 4 + b
                                if gb < 0:
                                    continue
                                sl = gb % 6
                                typ = {(0, 3): 0, (0, 4): 1, (1, 0): 2, (1, 3): 3, (1, 4): 4}.get((par, b))
                                kfn = (lambda prow, sl=sl, c=c: kring[prow, c, sl, :])
                                vl = vring[:, sl, c * 128:(c + 1) * 128]
                                if typ is None:
                                    bfn = None
                                else:
                                    bfn = (lambda e2, typ=typ, bb=bb: (btl[:, bb, e2, typ, :], ("btl", bb), typ,
                                                                      (lambda t0, nr, e2=e2, bb=bb: btl[:, bb, e2, t0:t0 + nr, :].rearrange("p a q -> p (a q)"))))
                                blocks.append((kfn, vl, 128, bfn, [("kr", c, sl), ("vr", sl, c)]))
                            for e2 in range(2):
                                work.append((c, e2, slice(qi * 64, qi * 64 + 64), 64, blocks, ("oa", c)))
                    def pre(k):
                        if k % 8 == 0:
                            c_ = k // 8
                            bb_ = c_ % 2
                            dma(btl[:, bb_, :, :, :],
                                btd[:, (2 * c_) * 320:(2 * c_ + 2) * 320].rearrange("p (h t q) -> p h t q", h=2, t=5),
                                [], [("btl", bb_)])
                    yield from attn_gen(work, pre)
                else:
                    for b2 in range(2):
                        st = yb[:, :, :].rearrange("p c t -> p (c t)")
                        ybk = [("yb", i) for i in range(16)]
                        dma(st.rearrange("p (c t) -> p c t", c=8), ckT[b2].rearrange("(c p) t -> p c t", p=128), [], ybk)
                        for c in range(8):
                            cp("pool", kring[:, c, 0:4, :], st[:, c * 512:(c + 1) * 512].rearrange("p (s k) -> p s k", s=4), ybk,
                               [("kr", c, s) for s in range(4)])
                        dma(st.rearrange("p (s f) -> p s f", s=4), cv[b2].rearrange("(s p) f -> p s f", p=128), [], ybk)
                        for s in range(4):
                            cp("pool", vring[:, s, :], st[:, s * 1024:(s + 1) * 1024], ybk, [("vr", s, c) for c in range(8)])
                        swork = []
                        for c in range(8):
                            blocks = []
                            for b in range(4):
                                kfn = (lambda prow, b=b, c=c: kring[prow, c, b, :])
                                bfn = None if b < 3 else (lambda e2, c=c: (sbt[:, 0, 2 * c + e2, :], ("sbt",)))
                                blocks.append((kfn, vring[:, b, c * 128:(c + 1) * 128], 128, bfn, [("kr", c, b), ("vr", b, c)]))
                            kfn = (lambda prow, b2=b2, c=c: kring[prow, c, 4 + b2, 0:16])
                            blocks.append((kfn, vring[0:16, 4 + b2, c * 128:(c + 1) * 128], 16,
                                           (lambda e2, c=c: (sbt[0:16, 1, 2 * c + e2, :], ("sbt",))), [("kr", c, 4 + b2), ("vr", 4 + b2, c)]))
                            swork.append((c, 0, slice(b2 * 16, b2 * 16 + 16), 16, blocks, ("oa", c)))
                            swork.append((c, 1, slice(b2 * 16, b2 * 16 + 16), 16, blocks, ("oa", c)))
                        attn_run(swork)
                for c in range(8):
                    pi = lin("win", 32 + c, xn, xk, n)
                    ts("dve", qT[:, c, 0:n], psL(pi, n), 0.0625, ALU.mult, [("psL", pi)], [("q", c)])
                    yield
                if mode == "prompt":
                    yield from memattn_gen(n, slice(0, n))
                else:
                    for b2 in range(2):
                        st = yb[:, :, :].rearrange("p c t -> p (c t)")
                        ybk = [("yb", i) for i in range(16)]
                        dma(st[:, 0:2048].rearrange("p (c t) -> p c t", c=8), cmkT[b2].rearrange("(c p) t -> p c t", p=128), [], ybk)
                        cp("pool", mkT[:, :, :], st[:, 0:2048].rearrange("p (c t) -> p c t", c=8), ybk, [("mkT",)])
                        dma(st[:, 2048:4096].rearrange("p (s f) -> p s f", s=2), cmv[b2].rearrange("(s p) f -> p s f", p=128), [], ybk)
                        cp("pool", mvb[:, :, :], st[:, 2048:4096].rearrange("p (s f) -> p s f", s=2), ybk, [("mvb",)])
                        memattn(16, slice(b2 * 16, b2 * 16 + 16))
            def mk_unit(j):
                def unit():
                    tms = []
                    for bi, wn, src, skey in ((1, "ba", oa, "oa"), (2, "bm", om, "om")):
                        pb_ = lin(wn, j, src, [(skey, c) for c in range(8)], n)
                        lin("win", 40 + 16 * bi + j, xn, xk, n, pi=pb_, col=256)
                        s1 = plM.next()
                        s2_ = plM.next()
                        act_fn(sm[:, s1, 0:n], ps_t[:, pb_, 256:256 + n], AF.Sigmoid, [("psL", pb_)], [("sm", s1)])
                        cp("act", sm[:, s2_, 0:n], psL(pb_, n), [("psL", pb_)], [("sm", s2_)])
                        tt("pool", sm[:, s1, 0:n], sm[:, s1, 0:n], sm[:, s2_, 0:n], ALU.mult, [("sm", s1), ("sm", s2_)], [("sm", s1)])
                        tms.append(s1)
                    tt("pool", yb[:, j, 0:n], sm[:, tms[0], 0:n], sm[:, tms[1], 0:n], ALU.add,
                       [("sm", tms[0]), ("sm", tms[1])], [("yb", j)])
                return unit
            def all_gen():
                yield from pre_gen()
                if TSTOP >= 7:
                    for j in range(DC):
                        mk_unit(j)()
                        yield
            fillers = all_gen()
            if mode != "prompt":
                for _ in fillers:
                    pass
            if mode == "prompt":
                stv = [(stt[:, 0, 0, :], stt[:, 0, 1, :], ("stt", 0))] * 4
                fo = None
                if ti == NT - 1:
                    fo = lambda: dma(ost, stt[:, 0, :, :], [("stt", 0)], [("ost",)])
                ssm(n, 64, 4, stv, fo, fillers, 116)
            else:
                s0 = spc("s0").rearrange("p (b r a) -> p b r a", b=2, r=2)
                cp("pool", stt[:, :, :, :], s0, [("spt",), ("stt", 0), ("stt", 1)], [("stt", 0), ("stt", 1)])
                stv = [(stt[:, b2, 0, :], stt[:, b2, 1, :], ("stt", b2)) for b2 in range(2)]
                ssm(n, 16, 2, stv, lambda: dma(osst, stt[:, :, :, :], [("stt", 0), ("stt", 1)], [("osst",)]), fillers, 0)
            if TSTOP < 7:
                return
            for j in range(DC):
                pb_ = lin("bs", j, os_, [("os", c) for c in range(8)], n)
                lin("win", 40 + j, xn, xk, n, pi=pb_, col=256)
                s1 = plM.next()
                act_fn(sm[:, s1, 0:n], ps_t[:, pb_, 256:256 + n], AF.Sigmoid, [("psL", pb_)], [("sm", s1)])
                tt("dve", sm[:, s1, 0:n], sm[:, s1, 0:n], psL(pb_, n), ALU.mult, [("sm", s1), ("psL", pb_)], [("sm", s1)])
                tt("pool", mg[:, j, 0:n], sm[:, s1, 0:n], yb[:, j, 0:n], ALU.add, [("sm", s1), ("yb", j)], [("mg", j)])
            for j in range(DC):
                pi = lin("wo", j, mg, [("mg", c) for c in range(DC)], n)
                cp("act", yb[:, j, 0:n], psL(pi, n), [("psL", pi)], [("yb", j)])
            postnorm_res("g_mpost", 1.0, n)
            if TSTOP < 8:
                return
            hooks = None
            if nxt is not None:
                src_next, nn = nxt
                dma(xpre[:, :, 0:nn], src_next, [], XPK)
                stt_box = {}

                def h_stats():
                    stt_box["st"] = prenorm_next_stats(nn)

                def h_rstd():
                    stt_box["s2"] = prenorm_next_rstd(stt_box["st"], nn)
                    plM.base, plM.n = 0, 8

                def h_apply():
                    prenorm_next_apply(stt_box["s2"], "g_f1pre", nn)
                hooks = {8: h_stats, 30: h_rstd, "mid": h_apply}
                plM.base, plM.n = 3, 5
            ffn("g_f2pre", "g_f2post", "f2g", "f2u", "f2d", n, hooks=hooks, defer_post=(nxt is not None))

        P.const.add(("rth",))
        P.op("act", lambda e: e.copy(out=sm[:, 0, 0:1], in_=ones_f[:, 0:1]), [("ones",)],
             [("pq", i) for i in range(10)] + [("stgf", i) for i in range(NSF)] + [("stgb", i) for i in range(NSB)]
             + [("yb", i) for i in range(DC)] + [("sm", 0), ("sqb",)])
        hk = [("h", c) for c in range(DC)]
        if STOP >= 2:
            dma(h[:, :, :], memT.rearrange("(c p) t -> p c t", p=128), [], hk)
            prenorm(h, "h", "g_mem", 256)
            xk_ = [("xn", c) for c in range(DC)]
            for c in range(8 if KSUB >= 1 else 0):
                pi = lin("mk", c, xn, xk_, 256)
                if KSUB2 >= 1:
                    cp("act", mkT[:, c, :], psL(pi, 256), [("psL", pi)], [("mkTc", c)])
                if KSUB2 >= 2:
                    cp("dve", yb[:, c, :], psL(pi, 256), [("psL", pi)], [("yb", c)])
            if KSUB2 >= 3:
                dma(omkT.rearrange("(c p) t -> p c t", p=128), yb[:, 0:8, :], [("yb", c) for c in range(8)], [("omkT",)])
            vfm = yb[:, 8:16, :].rearrange("p c t -> p (c t)")
            for c in range(8 if KSUB >= 2 else 0):
                wv, wk = wload("mv", c)
                for mb in range(2):
                    pi = plL.next()
                    mmg(psL(pi, 128), [(xn[:, k, mb * 128:(mb + 1) * 128], wv[:, k, :]) for k in range(16)], [wk] + xk_, [("psL", pi)])
                    cp("act", mvb[:, mb, c * 128:(c + 1) * 128], psL(pi, 128), [("psL", pi)], [("mvb",)])
                    cp("dve", vfm[:, mb * 1024 + c * 128: mb * 1024 + (c + 1) * 128], psL(pi, 128), [("psL", pi)], [("yb", 8 + mb * 4 + c // 2)])
            dma(omv.rearrange("(b p) f -> p b f", p=128), vfm.rearrange("p (b f) -> p b f", b=2), [("yb", 8 + i) for i in range(8)], [("omv",)])

        xTv = xT.rearrange("(c p) t -> p c t", p=128)
        yTv = yT.rearrange("(c p) t -> p c t", p=128)
        ntl = min(NT, NTILES) if STOP >= 3 else 0
        xsv = xsT.rearrange("(c p) t -> p c t", p=128)
        do_sample = STOP >= 10
        pre_done = False
        dhooks = None

        def make_deferred(store_fn, n_prev, n_cur):
            box = {}

            def d_stats():
                box["st"] = prenorm_next_stats(n_prev, yb, lambda c: [("yb", c)])

            def d_rstd():
                box["s2"] = prenorm_next_rstd(box["st"], n_prev)
                plM.base, plM.n = 0, 8

            def d_apply():
                s2 = box["s2"]
                g = spc("g_f2post")
                for c in range(DC):
                    stt_(yb[:, c, 0:n_prev], yb[:, c, 0:n_prev], g[:, c:c + 1], rsb[:, s2, 0:n_prev], ALU.mult, ALU.mult,
                         [("yb", c), ("rsb", s2), ("spt",)], [("yb", c)])
                for c in range(DC):
                    stt_(h[:, c, 0:n_prev], yb[:, c, 0:n_prev], 0.5, h[:, c, 0:n_prev], ALU.mult, ALU.add,
                         [("yb", c), ("h", c)], [("h", c)])

            def d_store():
                store_fn()
                cp("pool", h[:, :, 0:n_cur], xpre[:, :, 0:n_cur], XPK, hk)
            return {2: d_stats, 14: d_rstd, 20: d_apply, "mid": d_store}

        for ti in range(ntl):
            P.epoch = 1 + ti // EPOCH_TILES
            if not pre_done:
                dma(h[:, :, :], xTv[:, :, ti * T:(ti + 1) * T], [], hk)
            if ti + 1 < ntl:
                nxt = (xTv[:, :, (ti + 1) * T:(ti + 2) * T], T)
            elif do_sample and TSTOP >= 99:
                nxt = (xsv, TS)
            else:
                nxt = None
            if TSTOP < 99:
                nxt = None
            tile_body(T, "prompt", ti, pre_done, nxt, dhooks)
            store_fn = (lambda ti=ti: dma(yTv[:, :, ti * T:(ti + 1) * T], h[:, :, :], hk, [("yT", ti)]))
            if nxt is not None:
                dhooks = make_deferred(store_fn, T, nxt[1])
                pre_done = True
            else:
                store_fn()
                dhooks = None
                pre_done = False
        P.epoch += 1
        if do_sample:
            if not pre_done:
                dma(h[:, :, 0:TS], xsv, [], hk)
            tile_body(TS, "sample", 0, pre_done, None, dhooks)
            dma(ysT.rearrange("(c p) t -> p c t", p=128), h[:, :, 0:TS], hk, [("ysT",)])
        elif dhooks is not None:
            raise RuntimeError("deferred work pending")

        P.assign()
        semnames = {}
        sems = {}
        for e in ("pe", "act", "dve", "pool"):
            sems[e] = [es.enter_context(nc.semaphore(f"s_{e}_{i}")) for i in range(P.nepoch)]
        dsems = [es.enter_context(nc.semaphore(f"s_d_{i}")) for i in range(NDS)]
        block = es.enter_context(nc.Block())

        @block.sync
        def _(e):
            P.emit("sp", e, sems, dsems)

        @block.tensor
        def _(e):
            P.emit("pe", e, sems, dsems)

        @block.scalar
        def _(e):
            P.emit("act", e, sems, dsems)

        @block.vector
        def _(e):
            P.emit("dve", e, sems, dsems)

        @block.gpsimd
        def _(e):
            P.emit("pool", e, sems, dsems)
    return nc


def _tile_w(w, kc):
    K, N = w.shape
    nb = N // 128
    a = w.reshape(kc, 128, nb, 128).transpose(1, 2, 0, 3)
    return a.reshape(128, nb * kc * 128)


def _gcol(g, nch):
    return np.ascontiguousarray(g.reshape(nch, 128).T)


_NC_CACHE = {}


def kernel(**inp):
    f = lambda k: np.asarray(inp[k], dtype=np.float32)
    ws = {"f1g": f("ffn1_w_gate")[0], "f1u": f("ffn1_w_up")[0], "f1d": f("ffn1_w_down")[0], "win": f("w_in")[0],
          "glu": f("ssm_w_glu")[0], "bs": f("w_branch_ssm")[0], "ba": f("w_branch_att")[0], "bm": f("w_branch_mem")[0],
          "wo": f("w_out")[0], "f2g": f("ffn2_w_gate")[0], "f2u": f("ffn2_w_up")[0], "f2d": f("ffn2_w_down")[0],
          "mk": f("w_mem_k")[0], "mv": f("w_mem_v")[0]}
    wall = np.empty((128, WX), np.float32)
    for n, kc, nb in WTAB:
        o = WOFF[n][0]
        wall[:, o:o + kc * nb * 128] = _tile_w(ws[n], kc)

    a_re = f("ssm_a_re")[0]; a_im = f("ssm_a_im")[0]; ldt = f("ssm_log_dt")[0]
    b_re = f("ssm_b_re")[0]; b_im = f("ssm_b_im")[0]; c_re = f("ssm_c_re")[0]; c_im = f("ssm_c_im")[0]
    G = 64
    gidx = (2 * np.arange(32)[None, :] + (np.arange(128)[:, None] // 64))
    pidx = np.broadcast_to((np.arange(128) % 64)[:, None], (128, 32))
    are_m = a_re[gidx, pidx]; aim_m = a_im[gidx, pidx]; ldt_m = ldt[gidx]
    ssmB = np.zeros((128, 4, 5, 8, 128), np.float32)
    ssmC = np.zeros((128, 4, 2, 8, 128), np.float32)
    for pair in range(32):
        qt, pp = divmod(pair, 8)
        for gp in range(2):
            g = 2 * pair + gp
            gl = g % 8
            ms = slice(gp * 64, gp * 64 + 64)
            ssmB[:, qt, 0, pp, ms] = a_re[g][None, :]
            ssmB[:, qt, 1, pp, ms] = a_im[g][None, :]
            ssmB[:, qt, 2, pp, ms] = ldt[g]
            ssmB[gl * 16:(gl + 1) * 16, qt, 3, pp, ms] = b_re[g].T
            ssmB[gl * 16:(gl + 1) * 16, qt, 4, pp, ms] = b_im[g].T
            ssmC[ms, qt, 0, pp, gl * 16:(gl + 1) * 16] = c_re[g].T
            ssmC[ms, qt, 1, pp, gl * 16:(gl + 1) * 16] = c_im[g].T
    ssmB = ssmB.reshape(128, 4, 5, 1024)
    ssmC = ssmC.reshape(128, 4, 2048)

    rb = f("att_rel_bias")[0]
    btd = np.zeros((128, BT_TOT), np.float32)
    kk = np.arange(128)[:, None]; qq = np.arange(64)[None, :]
    types = [(0, 3), (0, 4), (1, 0), (1, 3), (1, 4)]
    bt = np.zeros((128, 16, 5, 64), np.float32)
    for t_, (par, b) in enumerate(types):
        rel = 64 * par + 128 * (4 - b) + qq - kk
        dcn = -par - 8 + 2 * b + kk // 64 + 0 * qq
        ok = (dcn >= -8) & (dcn <= 0)
        idx = np.clip(rel, -128, 128) + 128
        for hh in range(16):
            bt[:, hh, t_, :] = np.where(ok, rb[hh][idx], np.float32(NEG))
    btd[:, 0:BTW] = bt.reshape(128, BTW)
    q16 = np.arange(16)[None, :]
    rel3 = 128 + q16 - kk
    s3 = np.stack([rb[hh][np.clip(rel3, -128, 128) + 128] for hh in range(16)], 1)
    k16 = np.arange(16)[:, None]
    reln = q16 - k16
    sn = np.zeros((128, 16, 16), np.float32)
    sn[0:16] = np.stack([rb[hh][np.clip(reln, -128, 128) + 128] for hh in range(16)], 1)
    btd[:, SBT_OFF:SBT_OFF + 256] = s3.reshape(128, 256)
    btd[:, SBT_OFF + 256:SBT_OFF + 512] = sn.reshape(128, 256)

    def spack(s0=None):
        a = np.zeros((128, SPW), np.float32)

        def put(name, v):
            o, w_ = SPC[name]
            a[:, o:o + w_] = v
        put("g_f1pre", _gcol(f("ffn1_norm_pre")[0], 16)); put("g_f1post", _gcol(f("ffn1_norm_post")[0], 16))
        put("g_mpre", _gcol(f("mix_norm_pre")[0], 16)); put("g_mpost", _gcol(f("mix_norm_post")[0], 16))
        put("g_f2pre", _gcol(f("ffn2_norm_pre")[0], 16)); put("g_f2post", _gcol(f("ffn2_norm_post")[0], 16))
        put("g_mem", _gcol(f("mem_norm")[0], 16))
        put("dcol", _gcol(f("ssm_d")[0].reshape(-1), 8)); put("bglu", _gcol(f("ssm_b_glu")[0], 8))
        put("ch", np.broadcast_to(rb[:, 256][None, :], (128, 16)))
        put("are", are_m); put("aim", aim_m); put("ldt", ldt_m)
        put("jidx", np.broadcast_to(np.arange(1, 65, dtype=np.float32)[None, :], (128, 64)))
        if s0 is not None:
            put("s0", s0)
        return a

    xp = f("x_prompt"); xs = f("x_sample"); mp = f("mem_prompt")
    ck = f("cache_att_k")[0]; cvv = f("cache_att_v")[0]; cmk = f("cache_mem_k")[0]; cmvv = f("cache_mem_v")[0]
    sre = f("state_ssm_re")[0]; sim = f("state_ssm_im")[0]
    in_maps = []
    for c in range(8):
        s0 = np.zeros((128, 2, 2, 32), np.float32)
        for b2 in range(2):
            s0[:, b2, 0, :] = sre[2 * c + b2][gidx, pidx]
            s0[:, b2, 1, :] = sim[2 * c + b2][gidx, pidx]
        in_maps.append({
            "xT": np.ascontiguousarray(xp[c].T),
            "xsT": np.ascontiguousarray(xs[2 * c:2 * c + 2].reshape(TS, D).T),
            "memT": np.ascontiguousarray(mp[c].T),
            "ckT": np.ascontiguousarray(ck[2 * c:2 * c + 2].reshape(2, 512, 1024).transpose(0, 2, 1)),
            "cv": np.ascontiguousarray(cvv[2 * c:2 * c + 2].reshape(2, 512, 1024)),
            "cmkT": np.ascontiguousarray(cmk[2 * c:2 * c + 2].reshape(2, 256, 1024).transpose(0, 2, 1)),
            "cmv": np.ascontiguousarray(cmvv[2 * c:2 * c + 2].reshape(2, 256, 1024)),
            "wall": wall, "spk": spack(s0.reshape(128, 128)), "ssmB": ssmB, "ssmC": ssmC, "btd": btd,
        })
    if "nc" not in _NC_CACHE:
        _NC_CACHE["nc"] = build_nc()
    if os.environ.get('KTRACE'):
        res = run_bass_kernel_spmd(_NC_CACHE["nc"], in_maps[:NCORES], core_ids=list(range(NCORES)), trace=True)
        print("EXEC_TIME_NS", res.exec_time_ns)
    else:
        res = run_bass_kernel_spmd(_NC_CACHE["nc"], in_maps[:NCORES], core_ids=list(range(NCORES)))
    R_ = list(res.results)
    while len(R_) < 8:
        R_.append({k: np.zeros_like(v) for k, v in R_[0].items()})

    def unstate(a):
        o = np.zeros((64, 64), np.float32)
        o[gidx, pidx] = a
        return o
    y_p = np.stack([R_[c]["yT"].T for c in range(8)])
    y_s = np.concatenate([R_[c]["ysT"].T.reshape(2, 16, D) for c in range(8)])
    akp = np.stack([R_[c]["okT"].T.reshape(512, 16, 64) for c in range(8)])[None]
    avp = np.stack([R_[c]["ov"].reshape(512, 16, 64) for c in range(8)])[None]
    mkp = np.stack([R_[c]["omkT"].T.reshape(256, 4, 256) for c in range(8)])[None]
    mvp = np.stack([R_[c]["omv"].reshape(256, 4, 256) for c in range(8)])[None]
    srp = np.stack([unstate(R_[c]["ost"][:, 0, :]) for c in range(8)])[None]
    sip = np.stack([unstate(R_[c]["ost"][:, 1, :]) for c in range(8)])[None]
    aks = np.concatenate([R_[c]["oskT"].T.reshape(2, 16, 16, 64) for c in range(8)])[None]
    avs = np.concatenate([R_[c]["osv"].reshape(2, 16, 16, 64) for c in range(8)])[None]
    srs = np.stack([unstate(R_[c]["osst"][:, b2, 0, :]) for c in range(8) for b2 in range(2)])[None]
    sis = np.stack([unstate(R_[c]["osst"][:, b2, 1, :]) for c in range(8) for b2 in range(2)])[None]
    outs = (y_p, y_s, akp, avp, mkp, mvp, srp, sip, aks, avs, srs, sis)
    return tuple(np.ascontiguousarray(o, dtype=np.float32) for o in outs)
```

```python
import contextlib
import math
import os
import numpy as np
import concourse.bass as bass
import concourse.mybir as mybir
from concourse.bass_utils import run_bass_kernel_spmd

F32 = mybir.dt.float32
BF16 = mybir.dt.bfloat16
I32 = mybir.dt.int32
AF = mybir.ActivationFunctionType
ALU = mybir.AluOpType
AX = mybir.AxisListType

D = 2048
DC = 16
FC = 44
T = 256
NT = 16
SEQ = 4096
TS = 32
EPS = 1e-6
NEG = -1e30
NDS = 12
NSLOT = 8
SAME_SYNC = True
EPOCH_TILES = 3
STOP = int(os.environ.get('KSTOP', '99'))
NTILES = int(os.environ.get('KTILES', '16'))
NCORES = int(os.environ.get('KCORES', '8'))
KSUB = int(os.environ.get('KSUB', '99'))
KSUB2 = int(os.environ.get('KSUB2', '99'))
TSTOP = int(os.environ.get('KTSTOP', '99'))

WTAB = [("f1g", 16, 44), ("f1u", 16, 44), ("f1d", 44, 16), ("win", 16, 88), ("glu", 8, 8),
        ("bs", 8, 16), ("ba", 8, 16), ("bm", 8, 16), ("wo", 16, 16),
        ("f2g", 16, 44), ("f2u", 16, 44), ("f2d", 44, 16), ("mk", 16, 8), ("mv", 16, 8)]
WOFF = {}
_o = 0
for _n, _kc, _nb in WTAB:
    WOFF[_n] = (_o, _kc, _nb)
    _o += _kc * _nb * 128
WX = _o
CH = 2048
NCHUNK = WX // CH

SPC = {}
_o = 0
for _n, _w in [("g_f1pre", 16), ("g_f1post", 16), ("g_mpre", 16), ("g_mpost", 16), ("g_f2pre", 16),
               ("g_f2post", 16), ("g_mem", 16), ("dcol", 8), ("bglu", 8), ("ch", 16),
               ("are", 32), ("aim", 32), ("ldt", 32), ("s0", 128), ("jidx", 64)]:
    SPC[_n] = (_o, _w)
    _o += _w
SPW = _o
BTW = 16 * 5 * 64
SBT_OFF = BTW
BT_TOT = BTW + 16 * 16 + 16 * 16


class Op:
    __slots__ = ("eng", "fn", "deps", "idx", "sig", "sigval", "epoch", "dsem")


class Prog:
    ENG = ("pe", "act", "dve", "pool", "sp")

    def __init__(self):
        self.ops = {e: [] for e in self.ENG}
        self.lastw = {}
        self.rd = {}
        self.const = set()
        self.epoch = 0

    def op(self, eng, fn, reads=(), writes=()):
        o = Op()
        o.eng = eng
        o.fn = fn
        o.sig = False
        o.epoch = self.epoch
        deps = {}
        ex = [k for k in reads if k[0] in ("psL", "psB")]
        if ex:
            writes = list(writes) + ex
            reads = [k for k in reads if k[0] not in ("psL", "psB")]

        def add(d):
            if d is None:
                return
            if d.eng == "sp":
                deps[("sp", d.idx)] = d
            else:
                k = d.eng
                if k not in deps or deps[k].idx < d.idx:
                    deps[k] = d

        for k in reads:
            add(self.lastw.get(k))
        for k in writes:
            add(self.lastw.get(k))
            for r in self.rd.get(k, ()):
                add(r)
        if eng == "pe":
            deps.pop("pe", None)
        elif eng != "sp" and not SAME_SYNC:
            deps.pop(eng, None)
        o.deps = list(deps.values())
        for d in o.deps:
            d.sig = True
        o.idx = len(self.ops[eng])
        self.ops[eng].append(o)
        for k in writes:
            self.lastw[k] = o
            self.rd[k] = []
        for k in reads:
            if k not in self.const:
                self.rd.setdefault(k, []).append(o)
        return o

    def assign(self):
        self.nepoch = self.epoch + 1
        for e in self.ENG:
            if e == "sp":
                for o in self.ops[e]:
                    o.dsem = o.idx % NDS
                    o.sigval = 16 * (o.idx // NDS + 1)
            else:
                cnt = {}
                for o in self.ops[e]:
                    if o.sig:
                        cnt[o.epoch] = cnt.get(o.epoch, 0) + 1
                        o.sigval = cnt[o.epoch]

    def emit(self, e, eng, sems, dsems):
        waited = {}

        def w(key, sem, val):
            if waited.get(key, 0) < val:
                eng.wait_ge(sem, val)
                waited[key] = val

        for o in self.ops[e]:
            if e == "sp" and o.idx >= NDS:
                p = self.ops["sp"][o.idx - NDS]
                w(("d", p.dsem), dsems[p.dsem], p.sigval)
            for d in o.deps:
                if d.eng == "sp":
                    w(("d", d.dsem), dsems[d.dsem], d.sigval)
                else:
                    w((d.eng, d.epoch), sems[d.eng][d.epoch], d.sigval)
            inst = o.fn(eng)
            if e == "sp":
                inst.then_inc(dsems[o.dsem], 16)
            elif o.sig:
                inst.then_inc(sems[e][o.epoch], 1)
        if e == "sp":
            n = len(self.ops["sp"])
            for i in range(max(0, n - NDS), n):
                p = self.ops["sp"][i]
                w(("d", p.dsem), dsems[p.dsem], p.sigval)


def build_nc():
    nc = bass.Bass("TRN2", target_bir_lowering=False)
    P = Prog()

    def din(name, shape, dt=F32):
        return nc.dram_tensor(name, list(shape), dt, kind="ExternalInput").ap()

    def dout(name, shape, dt=F32):
        return nc.dram_tensor(name, list(shape), dt, kind="ExternalOutput").ap()

    xT = din("xT", [D, SEQ])
    xsT = din("xsT", [D, TS])
    memT = din("memT", [D, 256])
    ckT = din("ckT", [2, 1024, 512])
    cv = din("cv", [2, 512, 1024])
    cmkT = din("cmkT", [2, 1024, 256])
    cmv = din("cmv", [2, 256, 1024])
    wall = din("wall", [128, WX])
    spk = din("spk", [128, SPW])
    ssmB = din("ssmB", [128, 4, 5, 1024])
    ssmC = din("ssmC", [128, 4, 2048])
    btd = din("btd", [128, BT_TOT])
    yT = dout("yT", [D, SEQ])
    ysT = dout("ysT", [D, TS])
    okT = dout("okT", [1024, 512])
    ov = dout("ov", [512, 1024])
    omkT = dout("omkT", [1024, 256])
    omv = dout("omv", [256, 1024])
    ost = dout("ost", [128, 2, 32])
    oskT = dout("oskT", [1024, TS])
    osv = dout("osv", [2, 16, 1024])
    osst = dout("osst", [128, 2, 2, 32])
    wbf = nc.dram_tensor("wbf", [128, WX], BF16, kind="Internal").ap()
    ssmw = nc.dram_tensor("ssmw", [128, 4, 2, 2048], BF16, kind="Internal").ap()
    tabd = nc.dram_tensor("tabd", [128, 3, 32, 64], F32, kind="Internal").ap()

    es = contextlib.ExitStack()
    with es:
        def sb(name, shape, dt):
            return es.enter_context(nc.sbuf_tensor(name, list(shape), dt))

        h = sb("h", [128, DC, T], F32)
        xn = sb("xn", [128, DC, T], BF16)
        yb = sb("yb", [128, DC, T], F32)
        R = sb("R", [128, 20480], BF16)
        kring = sb("kring", [128, 8, 6, 128], BF16)
        vring = sb("vring", [128, 6, 1024], BF16)
        mkT = sb("mkT", [128, 8, 256], BF16)
        mvb = sb("mvb", [128, 2, 1024], BF16)
        btl = sb("btl", [128, 2, 2, 5, 64], F32)
        sbt = sb("sbt", [128, 2, 16, 16], F32)
        tabq = sb("tabq", [128, 3, 16, 64], F32)
        rt16 = sb("rt16", [128, 16, 16], F32)
        sw = sb("sw", [128, 6, 1024], F32)
        ssb = sb("ssb", [128, 16, 2, 64], BF16)
        wring = sb("wring", [128, NSLOT, 2048], BF16)
        pT = sb("pT", [128, 20, 64], BF16)
        ones_f = sb("ones_f", [128, 128], F32)
        ones_b = sb("ones_b", [128, 128], BF16)
        spt = sb("spt", [128, SPW], F32)
        sm = sb("sm", [128, 8, T], F32)
        rsb = sb("rsb", [128, 2, T], F32)
        stt = sb("stt", [128, 2, 2, 32], F32)
        rth = sb("rth", [128, 3, 32], F32)
        ps_t = es.enter_context(nc.psum_tensor("ps", [128, 8, 512], F32))

        if os.environ.get('KVERB'):
            print("SBUF bytes remaining per partition:", nc.sbuf_bytes_remaining)
        def rview(off_bytes, dt, n, pat=None, **kw):
            eb = 2
            a = R[:, off_bytes // eb: off_bytes // eb + (n * (4 if dt == F32 else 2)) // eb]
            if dt == F32:
                a = a.bitcast(F32)
            if pat:
                a = a.rearrange(pat, **kw)
            return a

        act = rview(0, BF16, FC * T, "p (c t) -> p c t", c=FC)
        sqb = rview(0, F32, DC * T, "p (c t) -> p c t", c=DC)
        qT = rview(0, BF16, 8 * T, "p (c t) -> p c t", c=8)
        usf = rview(4096, F32, 8 * T, "p (c t) -> p c t", c=8)
        usb = rview(12288, BF16, 8 * T, "p (c t) -> p c t", c=8)
        oa = rview(16384, BF16, 8 * T, "p (c t) -> p c t", c=8)
        om = rview(20480, BF16, 8 * T, "p (c t) -> p c t", c=8)
        os_ = rview(24576, BF16, 8 * T, "p (c t) -> p c t", c=8)
        mg = rview(28672, BF16, DC * T, "p (c t) -> p c t", c=DC)
        xpre = rview(24576, F32, DC * T, "p (c t) -> p c t", c=DC)
        XPK = [("os", c) for c in range(8)] + [("mg", c) for c in range(DC)]
        NSF, NSB = 5, 4
        stg_f = [rview(i * 8192, F32, CH) for i in range(NSF)]
        ybflat = yb[:, :, :].rearrange("p c t -> p (c t)").bitcast(BF16)
        stg_b = [ybflat[:, i * CH:(i + 1) * CH] for i in range(NSB)]
        pq = [rview(i * 4096, F32, 1024) for i in range(10)]

        def spc(name):
            o, w_ = SPC[name]
            return spt[:, o:o + w_]

        def dma(out, in_, reads, writes):
            return P.op("sp", lambda e: e.dma_start(out=out, in_=in_), reads, writes)

        def act_fn(out, in_, func, reads, writes, bias=None, scale=None):
            kw = {}
            if bias is not None:
                kw["bias"] = bias
            if scale is not None:
                kw["scale"] = scale
            return P.op("act", lambda e: e.activation(out=out, in_=in_, func=func, **kw), reads, writes)

        def tt(eng, out, a, b, op, reads, writes):
            return P.op(eng, lambda e: e.tensor_tensor(out=out, in0=a, in1=b, op=op), reads, writes)

        def ts(eng, out, a, s1, op0, reads, writes, s2=None, op1=None):
            if op1 is None:
                return P.op(eng, lambda e: e.tensor_scalar(out=out, in0=a, scalar1=s1, scalar2=None, op0=op0), reads, writes)
            return P.op(eng, lambda e: e.tensor_scalar(out=out, in0=a, scalar1=s1, scalar2=s2, op0=op0, op1=op1), reads, writes)

        def stt_(out, a, s, b, op0, op1, reads, writes):
            return P.op("dve", lambda e: e.scalar_tensor_tensor(out=out, in0=a, scalar=s, in1=b, op0=op0, op1=op1), reads, writes)

        def cp(eng, out, in_, reads, writes):
            if eng == "act":
                return P.op("act", lambda e: e.copy(out=out, in_=in_), reads, writes)
            return P.op(eng, lambda e: e.tensor_copy(out=out, in_=in_), reads, writes)

        def mmg(out, pairs, reads, writes):
            def fn(e):
                n = len(pairs)
                inst = None
                for i, (l, r) in enumerate(pairs):
                    inst = e.matmul(out, lhsT=l, rhs=r, start=(i == 0), stop=(i == n - 1))
                return inst
            return P.op("pe", fn, reads, writes)

        def mms(items, reads, writes):
            def fn(e):
                inst = None
                for (o, l, r) in items:
                    inst = e.matmul(o, lhsT=l, rhs=r, start=True, stop=True)
                return inst
            return P.op("pe", fn, reads, writes)

        class Pool_:
            def __init__(self, name, n):
                self.name, self.n, self.i, self.base = name, n, 0, 0

            def next(self):
                k = self.base + self.i % self.n
                self.i += 1
                return k

        plL = Pool_("pb", 4)
        plL.base = 4
        plS = Pool_("pS", 2)
        plO = Pool_("pO", 2)
        plO.base = 2
        plW = Pool_("w", NSLOT)
        plPG = Pool_("pTg", 4)
        plM = Pool_("sm", 8)
        plR = Pool_("rsb", 2)

        def bk(i):
            return ps_t[:, i, :]

        def pbk(i):
            return ("pb", i)

        def psL(i, n):
            return ps_t[:, i, 0:n]

        def wload(name, j, k0=0, kn=None):
            off, kc, nb = WOFF[name]
            if kn is None:
                kn = kc
            s = plW.next()
            c0 = off + j * kc * 128 + k0 * 128
            c1 = c0 + kn * 128
            rk = [("wbf", ci) for ci in range(c0 // CH, (c1 - 1) // CH + 1)]
            dma(wring[:, s, 0:kn * 128], wbf[:, c0:c1], rk, [("w", s)])
            return wring[:, s, 0:kn * 128].rearrange("p (k m) -> p k m", m=128), ("w", s)

        def lin(name, j, rhs, rkeys, n, pi=None, col=0):
            off, kc, nb = WOFF[name]
            wv, wk = wload(name, j)
            if pi is None:
                pi = plL.next()
            mmg(ps_t[:, pi, col:col + n], [(wv[:, k, :], rhs[:, k, 0:n]) for k in range(kc)], [wk] + rkeys, [("psL", pi)])
            return pi

        def rstd_of(src, skeys, n, nchunks=DC):
            sq = sqb[:, 0:nchunks, 0:n]
            act_fn(sq, src, AF.Square, skeys, [("sqb",)])
            s1 = plM.next()
            P.op("dve", lambda e: e.tensor_reduce(out=sm[:, s1, 0:n], in_=sq.rearrange("p c t -> p t c"),
                                                   axis=AX.X, op=ALU.add), [("sqb",)], [("sm", s1)])
            s3 = plM.next()
            hi = sm[:, s3, 0:n].bitcast(BF16)[:, 0:n]
            lo = sm[:, s3, 0:n].bitcast(BF16)[:, n:2 * n]
            cp("dve", hi, sm[:, s1, 0:n], [("sm", s1)], [("sm", s3)])
            tt("dve", sm[:, s1, 0:n], sm[:, s1, 0:n], hi, ALU.subtract, [("sm", s1), ("sm", s3)], [("sm", s1)])
            cp("dve", lo, sm[:, s1, 0:n], [("sm", s1), ("sm", s3)], [("sm", s3)])
            pi = plL.next()
            mmg(psL(pi, n), [(ones_b[:, :], hi), (ones_b[:, :], lo)], [("sm", s3), ("ones",)], [("psL", pi)])
            s2 = plR.next()
            ts("dve", rsb[:, s2, 0:n], psL(pi, n), 1.0 / D, ALU.mult, [("psL", pi)], [("rsb", s2)], s2=EPS, op1=ALU.add)
            act_fn(rsb[:, s2, 0:n], rsb[:, s2, 0:n], AF.Sqrt, [("rsb", s2)], [("rsb", s2)])
            P.op("dve", lambda e: e.reciprocal(out=rsb[:, s2, 0:n], in_=rsb[:, s2, 0:n]), [("rsb", s2)], [("rsb", s2)])
            return s2

        def prenorm(src, srckey, gname, n):
            keys = [(srckey, c) for c in range(DC)]
            s2 = rstd_of(src[:, :, 0:n], keys, n)
            g = spc(gname)
            for c in range(DC):
                stt_(xn[:, c, 0:n], src[:, c, 0:n], g[:, c:c + 1], rsb[:, s2, 0:n], ALU.mult, ALU.mult,
                     [(srckey, c), ("rsb", s2), ("spt",)], [("xn", c)])

        def postnorm_res(gname, factor, n):
            keys = [("yb", c) for c in range(DC)]
            s2 = rstd_of(yb[:, :, 0:n], keys, n)
            g = spc(gname)
            for c in range(DC):
                stt_(yb[:, c, 0:n], yb[:, c, 0:n], g[:, c:c + 1], rsb[:, s2, 0:n], ALU.mult, ALU.mult,
                     [("yb", c), ("rsb", s2), ("spt",)], [("yb", c)])
            for c in range(DC):
                stt_(h[:, c, 0:n], yb[:, c, 0:n], float(factor), h[:, c, 0:n], ALU.mult, ALU.add,
                     [("yb", c), ("h", c)], [("h", c)])

        def prenorm_next_stats(nn, src=None, kfn=None):
            if src is None:
                src = xpre
                kfn = lambda c: XPK
            a0, t1_, t2_ = 0, 1, 2
            tmps = [t1_, t2_]
            for c in range(DC):
                tsl = tmps[c % 2]
                dst = sm[:, a0, 0:nn] if c == 0 else sm[:, tsl, 0:nn]
                dk = ("sm", a0) if c == 0 else ("sm", tsl)
                act_fn(dst, src[:, c, 0:nn], AF.Square, kfn(c), [dk])
                if c > 0:
                    tt("dve", sm[:, a0, 0:nn], sm[:, a0, 0:nn], sm[:, tsl, 0:nn], ALU.add,
                       [("sm", a0), ("sm", tsl)], [("sm", a0)])
            hi = sm[:, t1_, 0:nn].bitcast(BF16)[:, 0:nn]
            lo = sm[:, t1_, 0:nn].bitcast(BF16)[:, nn:2 * nn]
            cp("dve", hi, sm[:, a0, 0:nn], [("sm", a0)], [("sm", t1_)])
            tt("dve", sm[:, a0, 0:nn], sm[:, a0, 0:nn], hi, ALU.subtract, [("sm", a0), ("sm", t1_)], [("sm", a0)])
            cp("dve", lo, sm[:, a0, 0:nn], [("sm", a0), ("sm", t1_)], [("sm", t1_)])
            return (hi, lo, t1_)

        def prenorm_next_rstd(st, nn):
            hi, lo, t1_ = st
            pi = plL.next()
            mmg(psL(pi, nn), [(ones_b[:, :], hi), (ones_b[:, :], lo)], [("sm", t1_), ("ones",)], [("psL", pi)])
            s2 = plR.next()
            ts("dve", rsb[:, s2, 0:nn], psL(pi, nn), 1.0 / D, ALU.mult, [("psL", pi)], [("rsb", s2)], s2=EPS, op1=ALU.add)
            act_fn(rsb[:, s2, 0:nn], rsb[:, s2, 0:nn], AF.Sqrt, [("rsb", s2)], [("rsb", s2)])
            P.op("dve", lambda e: e.reciprocal(out=rsb[:, s2, 0:nn], in_=rsb[:, s2, 0:nn]), [("rsb", s2)], [("rsb", s2)])
            return s2

        def prenorm_next_apply(s2, gname, nn):
            g = spc(gname)
            for c in range(DC):
                stt_(xn[:, c, 0:nn], xpre[:, c, 0:nn], g[:, c:c + 1], rsb[:, s2, 0:nn], ALU.mult, ALU.mult,
                     XPK + [("rsb", s2), ("spt",)], [("xn", c)])

        def ffn(pre, post, wg, wu, wd, n, skip_prenorm=False, hooks=None, defer_post=False):
            if not skip_prenorm:
                prenorm(h, "h", pre, n)
            xk = [("xn", c) for c in range(DC)]
            for j in range(FC):
                if hooks is not None and j in hooks:
                    hooks[j]()
                pg = lin(wg, j, xn, xk, n)
                lin(wu, j, xn, xk, n, pi=pg, col=256)
                s1 = plM.next()
                act_fn(sm[:, s1, 0:n], psL(pg, n), AF.Silu, [("psL", pg)], [("sm", s1)])
                tt("dve", act[:, j, 0:n], sm[:, s1, 0:n], ps_t[:, pg, 256:256 + n], ALU.mult,
                   [("sm", s1), ("psL", pg)], [("act", j)])
            if hooks is not None and "mid" in hooks:
                hooks["mid"]()
            for j in range(DC):
                pi = plL.next()
                parts = [(0, 16), (16, 16), (32, 12)]
                for pidx, (k0, kn) in enumerate(parts):
                    wv, wk = wload(wd, j, k0, kn)

                    def fn(e, wv=wv, k0=k0, kn=kn, pidx=pidx, pi=pi):
                        inst = None
                        for k in range(kn):
                            inst = e.matmul(psL(pi, n), lhsT=wv[:, k, :], rhs=act[:, k0 + k, 0:n],
                                            start=(pidx == 0 and k == 0), stop=(pidx == 2 and k == kn - 1))
                        return inst
                    P.op("pe", fn, [wk] + [("act", k0 + k) for k in range(kn)] + ([("psL", pi)] if pidx else []),
                         [("psL", pi)])
                cp("act", yb[:, j, 0:n], psL(pi, n), [("psL", pi)], [("yb", j)])
            if not defer_post:
                postnorm_res(post, 0.5, n)

        C1 = 6.28125
        C2 = 2.0 * math.pi - 6.28125

        def sin_of(dst, src, shift, tmpA, tmpI, skeys, dkey, akey, ikey):
            ts("dve", tmpA, src, float(shift), ALU.add, skeys, [akey], s2=1.0 / (2.0 * math.pi), op1=ALU.mult)
            cp("dve", tmpI, tmpA, [akey], [ikey])
            cp("dve", tmpA, tmpI, [ikey], [akey])
            ts("dve", dst, src, float(shift), ALU.add, skeys, [dkey])
            stt_(dst, tmpA, -C1, dst, ALU.mult, ALU.add, [akey, dkey], [dkey])
            stt_(dst, tmpA, -C2, dst, ALU.mult, ALU.add, [akey, dkey], [dkey])
            ts("dve", dst, dst, -math.pi, ALU.max, [dkey], [dkey], s2=math.pi, op1=ALU.min)
            act_fn(dst, dst, AF.Sin, [dkey], [dkey])

        P.op("dve", lambda e: e.memset(ones_f[:, :], 1.0), [], [("ones",)])
        P.op("dve", lambda e: e.memset(ones_b[:, :], 1.0), [], [("ones",)])
        P.op("pool", lambda e: e.memset(stt[:, :, :, :], 0.0), [], [("stt", 0), ("stt", 1)])
        dma(spt[:, :], spk, [], [("spt",)])
        dma(sbt[:, :, :, :], btd[:, SBT_OFF:SBT_OFF + 512].rearrange("p (a h q) -> p a h q", a=2, h=16), [], [("sbt",)])
        P.const.add(("ones",))
        P.const.add(("spt",))
        P.const.add(("sbt",))

        cengs = ["act", "dve", "pool"]

        def conv_in(ci):
            dma(stg_f[ci % NSF], wall[:, ci * CH:(ci + 1) * CH], [], [("stgf", ci % NSF)])
        for ci in range(min(NSF, NCHUNK)):
            conv_in(ci)
        for ci in range(NCHUNK):
            cp(cengs[ci % 3], stg_b[ci % NSB], stg_f[ci % NSF], [("stgf", ci % NSF)], [("stgb", ci % NSB)])
            dma(wbf[:, ci * CH:(ci + 1) * CH], stg_b[ci % NSB], [("stgb", ci % NSB)], [("wbf", ci)])
            if ci + NSF < NCHUNK:
                conv_in(ci + NSF)
        for ci in range(NCHUNK):
            P.const.add(("wbf", ci))
        P.op("act", lambda e: e.copy(out=sm[:, 1, 0:1], in_=ones_f[:, 0:1]), [("ones",)],
             [("stgf", i) for i in range(NSF)] + [("stgb", i) for i in range(NSB)] + [("pq", i) for i in range(10)]
             + [("yb", i) for i in range(DC)] + [("sm", 1)])

        if STOP >= 1:
            PI = math.pi
            ts("dve", rth[:, 0, :], spc("ldt"), 0.0, ALU.add, [("spt",)], [("rth",)])
            act_fn(rth[:, 0, :], rth[:, 0, :], AF.Exp, [("rth",)], [("rth",)])
            tt("dve", rth[:, 1, :], spc("are"), rth[:, 0, :], ALU.mult, [("rth",), ("spt",)], [("rth",)])
            act_fn(rth[:, 1, :], rth[:, 1, :], AF.Exp, [("rth",)], [("rth",)])
            tt("dve", rth[:, 2, :], spc("aim"), rth[:, 0, :], ALU.mult, [("rth",), ("spt",)], [("rth",)])
            jx = spc("jidx")
            for half in range(2):
                ph = pq[0].rearrange("p (a j) -> p a j", j=64)
                cs = pq[1].rearrange("p (a j) -> p a j", j=64)
                sn = pq[2].rearrange("p (a j) -> p a j", j=64)
                rr = pq[3].rearrange("p (a j) -> p a j", j=64)
                for a in range(16):
                    pr = half * 16 + a
                    ts("dve", ph[:, a, :], jx, rth[:, 2, pr:pr + 1], ALU.mult, [("rth",), ("spt",)], [("pq", 0)])
                sin_of(cs, ph, 0.5 * PI, pq[4].rearrange("p (a j) -> p a j", j=64), pq[5].bitcast(I32).rearrange("p (a j) -> p a j", j=64),
                       [("pq", 0)], ("pq", 1), ("pq", 4), ("pq", 5))
                sin_of(sn, ph, 0.0, pq[4].rearrange("p (a j) -> p a j", j=64), pq[5].bitcast(I32).rearrange("p (a j) -> p a j", j=64),
                       [("pq", 0)], ("pq", 2), ("pq", 4), ("pq", 5))
                cp("pool", rr, rth[:, 1, half * 16:(half + 1) * 16].unsqueeze(2).to_broadcast([128, 16, 64]),
                   [("rth",)], [("pq", 3)])
                P.op("pool", lambda e, rr=rr: e.memset(rr[:, :, 0:1], 0.0), [("pq", 3)], [("pq", 3)])
                dma(tabd[:, 0, half * 16:(half + 1) * 16, :], cs, [("pq", 1)], [("tabd",)])
                dma(tabd[:, 1, half * 16:(half + 1) * 16, :], sn, [("pq", 2)], [("tabd",)])
                dma(tabd[:, 2, half * 16:(half + 1) * 16, :], rr, [("pq", 3)], [("tabd",)])
            for qt in range(4):
                A_re, A_im, LDT, b_re, b_im = pq[0], pq[1], pq[2], pq[3], pq[4]
                t0, t1, t2, t3, t4 = pq[5], pq[6], pq[7], pq[8], pq[9]
                for i, dst in enumerate([A_re, A_im, LDT, b_re, b_im]):
                    dma(dst, ssmB[:, qt, i, :], [], [("pq", i)])
                K = lambda *ix: [("pq", i) for i in ix]
                act_fn(LDT, LDT, AF.Exp, K(2), K(2))
                tt("dve", t1, A_im, LDT, ALU.mult, K(1, 2), K(6))
                sin_of(t2, t1, 0.5 * PI, t0, t4.bitcast(I32), K(6), ("pq", 7), ("pq", 5), ("pq", 9))
                sin_of(t3, t1, 0.0, t0, t4.bitcast(I32), K(6), ("pq", 8), ("pq", 5), ("pq", 9))
                tt("dve", t0, A_re, LDT, ALU.mult, K(0, 2), K(5))
                act_fn(t0, t0, AF.Exp, K(5), K(5))
                tt("dve", t2, t2, t0, ALU.mult, K(7, 5), K(7))
                tt("dve", t3, t3, t0, ALU.mult, K(8, 5), K(8))
                ts("dve", t2, t2, -1.0, ALU.add, K(7), K(7))
                tt("dve", t0, A_re, A_re, ALU.mult, K(0), K(5))
                tt("dve", t1, A_im, A_im, ALU.mult, K(1), K(6))
                tt("dve", t0, t0, t1, ALU.add, K(5, 6), K(5))
                P.op("dve", lambda e, t0=t0: e.reciprocal(out=t0, in_=t0), K(5), K(5))
                tt("dve", t1, t2, A_re, ALU.mult, K(7, 0), K(6))
                tt("dve", t4, t3, A_im, ALU.mult, K(8, 1), K(9))
                tt("dve", t1, t1, t4, ALU.add, K(6, 9), K(6))
                tt("dve", t1, t1, t0, ALU.mult, K(6, 5), K(6))
                tt("dve", t4, t3, A_re, ALU.mult, K(8, 0), K(9))
                tt("dve", t3, t2, A_im, ALU.mult, K(7, 1), K(8))
                tt("dve", t4, t4, t3, ALU.subtract, K(9, 8), K(9))
                tt("dve", t4, t4, t0, ALU.mult, K(9, 5), K(9))
                tt("dve", t0, t1, b_re, ALU.mult, K(6, 3), K(5))
                tt("dve", t2, t4, b_im, ALU.mult, K(9, 4), K(7))
                tt("dve", t0, t0, t2, ALU.subtract, K(5, 7), K(5))
                tt("dve", t2, t1, b_im, ALU.mult, K(6, 4), K(7))
                tt("dve", t3, t4, b_re, ALU.mult, K(9, 3), K(8))
                tt("dve", t2, t2, t3, ALU.add, K(7, 8), K(7))
                bst = pq[1].bitcast(BF16)[:, 0:2048].rearrange("p (a r m) -> p a r m", a=8, r=2)
                cp("pool", bst[:, :, 0, :], t0.rearrange("p (a m) -> p a m", a=8), K(5, 1), K(1))
                cp("pool", bst[:, :, 1, :], t2.rearrange("p (a m) -> p a m", a=8), K(7, 1), K(1))
                dma(ssmw[:, qt, 0, :], bst.rearrange("p a r m -> p (a r m)"), K(1), [("ssmw", qt, 0)])
                cst_f = pq[3].bitcast(F32)
                cf = R[:, (3 * 4096) // 2:(5 * 4096) // 2].bitcast(F32)
                dma(cf, ssmC[:, qt, :], K(5, 7), K(3, 4))
                cfv = cf.rearrange("p (r a m) -> p r a m", r=2, a=8)
                cbt = pq[6].bitcast(BF16)[:, 0:2048].rearrange("p (a r m) -> p a r m", a=8, r=2)
                cp("pool", cbt[:, :, 0, :], cfv[:, 0, :, :], K(3, 4, 6), K(6))
                ts("dve", cbt[:, :, 1, :], cfv[:, 1, :, :], -1.0, ALU.mult, K(3, 4, 6), K(6))
                dma(ssmw[:, qt, 1, :], cbt.rearrange("p a r m -> p (a r m)"), K(6), [("ssmw", qt, 1)])
            for qt in range(4):
                P.const.add(("ssmw", qt, 0))
                P.const.add(("ssmw", qt, 1))
            P.const.add(("tabd",))
        def ssm(n, L, nseg, st_views, final_out, fillers, nfill):
            plL.n = 2
            plL.i = 0
            plW.base, plW.n = 4, NSLOT - 4
            psbk = [("psL", 6), ("psL", 7)]
            nsteps = 2 * nseg
            per_pt = (nfill + 4 * nsteps - 1) // (4 * nsteps)

            def fill():
                for _ in range(per_pt):
                    next(fillers, None)
            for hf in range(2):
                p16 = slice(hf * 16, hf * 16 + 16)
                Bv, Cv, wkB, wkC = [], [], [], []
                for qi in range(2):
                    qt = 2 * hf + qi
                    sB = 2 * qi
                    dma(wring[:, sB, :], ssmw[:, qt, 0, :], [("ssmw", qt, 0)], [("w", sB)])
                    sC = 2 * qi + 1
                    dma(wring[:, sC, :], ssmw[:, qt, 1, :], [("ssmw", qt, 1)], [("w", sC)])
                    Bv.append(wring[:, sB, :].rearrange("p (a m) -> p a m", m=128))
                    Cv.append(wring[:, sC, :].rearrange("p (a m) -> p a m", m=128))
                    wkB.append(("w", sB))
                    wkC.append(("w", sC))
                dma(tabq[:, :, :, :], tabd[:, :, p16, :], [("tabd",)], [("tabq",)])
                if L != 64:
                    cp("pool", rt16[:, :, :], rth[:, 1, p16].unsqueeze(2).to_broadcast([128, 16, L]), [("rth",)], [("rt16",)])
                    P.op("pool", lambda e: e.memset(rt16[:, :, 0:1], 0.0), [("rt16",)], [("rt16",)])
                Ct = tabq[:, 0, :, 0:L]
                St = tabq[:, 1, :, 0:L]
                if L == 64:
                    Rt2 = tabq[:, 2, :, :].rearrange("p a j -> p (a j)")
                    rtk = ("tabq",)
                else:
                    Rt2 = rt16[:, :, :].rearrange("p a j -> p (a j)")
                    rtk = ("rt16",)
                for seg in range(nseg):
                    tok = slice(seg * L, (seg + 1) * L)
                    zre, zim, zk = st_views[seg]
                    tv = [sw[:, i, 0:16 * L].rearrange("p (a j) -> p a j", a=16) for i in range(6)]
                    for sbt_ in range(2):
                        p8 = slice(8 * sbt_, 8 * sbt_ + 8)
                        psb = ps_t[:, 6:8, :].rearrange("p a b -> p (a b)")[:, 0:16 * L]
                        psb4 = psb.rearrange("p (a r j) -> p a r j", a=8, r=2)
                        items = []
                        for pq_ in range(8):
                            pp = 8 * sbt_ + pq_
                            ch = 4 * hf + pp // 4
                            for ri in range(2):
                                items.append((psb4[:, pq_, ri, :], Bv[pp // 8][:, (pp % 8) * 2 + ri, :], usb[:, ch, tok]))
                        mms(items, wkB + [("usb", 4 * hf + i) for i in range(4)], psbk)
                        bre = psb4[:, :, 0, :]
                        bim = psb4[:, :, 1, :]
                        sk = lambda i: ("sw", i, sbt_)
                        tt("dve", tv[0][:, p8, :], Ct[:, p8, :], bre, ALU.mult, [("tabq",)] + psbk, [sk(0)])
                        tt("dve", tv[1][:, p8, :], St[:, p8, :], bim, ALU.mult, [("tabq",)] + psbk, [sk(1)])
                        tt("dve", tv[2][:, p8, :], Ct[:, p8, :], bim, ALU.mult, [("tabq",)] + psbk, [sk(2)])
                        tt("dve", tv[3][:, p8, :], St[:, p8, :], bre, ALU.mult, [("tabq",)] + psbk, [sk(3)])
                    SW = lambda i: [("sw", i, 0), ("sw", i, 1)]
                    tt("pool", tv[4], tv[0], tv[1], ALU.add, SW(0) + SW(1), SW(4))
                    tt("pool", tv[5], tv[2], tv[3], ALU.subtract, SW(2) + SW(3), SW(5))
                    fill()
                    s1 = plM.next()
                    tt("dve", sm[:, s1, 0:16], rth[:, 1, p16], zre[:, p16], ALU.mult, [("rth",), zk], [("sm", s1)])
                    tt("dve", sm[:, s1, 16:32], rth[:, 1, p16], zim[:, p16], ALU.mult, [("rth",), zk], [("sm", s1)])
                    tt("dve", tv[4][:, :, 0:1], tv[4][:, :, 0:1], sm[:, s1, 0:16].unsqueeze(2), ALU.add,
                       SW(4) + [("sm", s1)], SW(4))
                    tt("dve", tv[5][:, :, 0:1], tv[5][:, :, 0:1], sm[:, s1, 16:32].unsqueeze(2), ALU.add,
                       SW(5) + [("sm", s1)], SW(5))
                    w4 = sw[:, 4, 0:16 * L]
                    w5 = sw[:, 5, 0:16 * L]
                    P.op("dve", lambda e, w4=w4, Rt2=Rt2: e.tensor_tensor_scan(out=w4, data0=Rt2, data1=w4, initial=0.0,
                                                                           op0=ALU.mult, op1=ALU.add),
                         SW(4) + [rtk], SW(4))
                    P.op("dve", lambda e, w5=w5, Rt2=Rt2: e.tensor_tensor_scan(out=w5, data0=Rt2, data1=w5, initial=0.0,
                                                                           op0=ALU.mult, op1=ALU.add),
                         SW(5) + [rtk], SW(5))
                    fill()
                    tt("dve", tv[0], Ct, tv[4], ALU.mult, [("tabq",)] + SW(4), SW(0))
                    tt("dve", tv[1], St, tv[5], ALU.mult, [("tabq",)] + SW(5), SW(1))
                    tt("pool", tv[2], Ct, tv[5], ALU.mult, [("tabq",)] + SW(5), SW(2))
                    tt("pool", tv[3], St, tv[4], ALU.mult, [("tabq",)] + SW(4), SW(3))
                    sre = ssb[:, :, 0, 0:L]
                    sim = ssb[:, :, 1, 0:L]
                    tt("dve", sre, tv[0], tv[1], ALU.subtract, SW(0) + SW(1), [("ssb",)])
                    tt("pool", sim, tv[2], tv[3], ALU.add, SW(2) + SW(3), [("ssb",)])
                    tt("dve", zre[:, p16].unsqueeze(2), tv[0][:, :, L - 1:L], tv[1][:, :, L - 1:L], ALU.subtract,
                       SW(0) + SW(1), [zk])
                    tt("dve", zim[:, p16].unsqueeze(2), tv[2][:, :, L - 1:L], tv[3][:, :, L - 1:L], ALU.add,
                       SW(2) + SW(3), [zk])
                    fill()
                    dc = spc("dcol")
                    for fc in range(4):
                        ch = 4 * hf + fc
                        pc = plL.next()
                        prs = []
                        for pp in range(4 * fc, 4 * fc + 4):
                            for ri in range(2):
                                prs.append((Cv[pp // 8][:, (pp % 8) * 2 + ri, :], ssb[:, pp, ri, 0:L]))
                        mmg(ps_t[:, pc, 0:L], prs, wkC + [("ssb",)], [("psL", pc)])
                        stt_(usf[:, ch, tok], usf[:, ch, tok], dc[:, ch:ch + 1], ps_t[:, pc, 0:L], ALU.mult, ALU.add,
                             [("usf", ch), ("psL", pc), ("spt",)], [("usf", ch)])
                    fill()
            for _ in fillers:
                pass
            plL.n = 4
            plW.base, plW.n = 0, NSLOT
            if final_out is not None:
                final_out()
            for c in range(8):
                s1 = plM.next()
                a = sm[:, s1, 0:n]
                y = usf[:, c, 0:n]
                tt("dve", a, y, y, ALU.mult, [("usf", c)], [("sm", s1)])
                ts("dve", a, a, 0.044715, ALU.mult, [("sm", s1)], [("sm", s1)], s2=1.0, op1=ALU.add)
                tt("dve", a, a, y, ALU.mult, [("sm", s1), ("usf", c)], [("sm", s1)])
                act_fn(a, a, AF.Sigmoid, [("sm", s1)], [("sm", s1)], scale=2.0 * math.sqrt(2.0 / math.pi))
                tt("dve", y, y, a, ALU.mult, [("sm", s1), ("usf", c)], [("usf", c)])
                cp("pool", usb[:, c, 0:n], y, [("usf", c)], [("usb", c)])
            bg = spc("bglu")
            for j in range(8):
                pi = lin("glu", j, usb, [("usb", c) for c in range(8)], n)
                s1 = plM.next()
                act_fn(sm[:, s1, 0:n], psL(pi, n), AF.Sigmoid, [("psL", pi), ("spt",)], [("sm", s1)], bias=bg[:, j:j + 1])
                tt("dve", os_[:, j, 0:n], usf[:, j, 0:n], sm[:, s1, 0:n], ALU.mult, [("usf", j), ("sm", s1)], [("os", j)])

        def memattn(n, tok):
            for _ in memattn_gen(n, tok):
                pass

        def memattn_gen(n, tok):
            for hm in range(4):
                yield
                pms = []
                for mb in range(2):
                    pi = plL.next()
                    mmg(psL(pi, n), [(mkT[:, 2 * hm + dcc, mb * 128:(mb + 1) * 128], qT[:, 2 * hm + dcc, tok])
                                     for dcc in range(2)],
                        [("mkT",), ("q", 2 * hm), ("q", 2 * hm + 1)], [("psL", pi)])
                    s1 = plM.next()
                    pm = sm[:, s1, 0:n].bitcast(BF16)[:, 0:n]
                    act_fn(pm, psL(pi, n), AF.Exp, [("psL", pi)], [("sm", s1)])
                    pms.append((pm, s1))
                pd = plL.next()
                mmg(psL(pd, n), [(ones_b[:, :], pm) for pm, _ in pms], [("sm", s) for _, s in pms] + [("ones",)], [("psL", pd)])
                s2 = plM.next()
                P.op("dve", lambda e, s2=s2, pd=pd: e.reciprocal(out=sm[:, s2, 0:n], in_=psL(pd, n)), [("psL", pd)], [("sm", s2)])
                for dcc in range(2):
                    po = plL.next()
                    c = 2 * hm + dcc
                    mmg(psL(po, n), [(mvb[:, mb, c * 128:(c + 1) * 128], pms[mb][0]) for mb in range(2)],
                        [("mvb",)] + [("sm", s) for _, s in pms], [("psL", po)])
                    tt("dve", om[:, c, tok], psL(po, n), sm[:, s2, 0:n], ALU.mult, [("psL", po), ("sm", s2)], [("om", c)])

        def attn_S(c, e2, qcols, nq, blocks):
            prow = slice(e2 * 64, e2 * 64 + 64)
            sbk = plS.next()
            items = []
            allk = []
            for bi_, (kfn, vl, nk, bfn, keys) in enumerate(blocks):
                items.append((ps_t[0:nk, sbk, bi_ * 64:bi_ * 64 + nq], kfn(prow), qT[prow, c, qcols]))
                allk += keys
            mms(items, allk + [("q", c)], [("psL", sbk)])
            return sbk

        def attn_rest(c, e2, qcols, nq, blocks, okey, sbk):
            chv = spc("ch")
            hh = 2 * c + e2
            prow = slice(e2 * 64, e2 * 64 + 64)
            pg = plPG.next()
            nb_ = len(blocks)
            merge_ok = (nq == 64) and all(bl[2] == 128 for bl in blocks)
            runs = []
            for bi_, (kfn, vl, nk, bfn, keys) in enumerate(blocks):
                bt = bfn(e2) if bfn is not None else None
                typ = None if bt is None else (bt[2] if len(bt) > 2 else -100 - bi_)
                if runs and merge_ok:
                    r = runs[-1]
                    if (r["typ0"] is None and typ is None) or \
                       (r["typ0"] is not None and typ is not None and typ == r["typ0"] + r["n"] and typ >= 0):
                        r["n"] += 1
                        continue
                runs.append({"b0": bi_, "n": 1, "typ0": typ, "bt": bt, "nk": nk})
            pts = []
            for r in runs:
                b0, nr, nk = r["b0"], r["n"], r["nk"]
                if merge_ok:
                    ps = ps_t[:, sbk, b0 * 64:(b0 + nr) * 64]
                    pv = pT[:, pg * 5 + b0:pg * 5 + b0 + nr, :].rearrange("p a q -> p (a q)")
                else:
                    ps = ps_t[0:nk, sbk, b0 * 64:b0 * 64 + nq]
                    pv = pT[0:nk, pg * 5 + b0, 0:nq]
                if r["typ0"] is None:
                    act_fn(pv, ps, AF.Exp, [("psL", sbk), ("spt",)], [("pTg", pg)], bias=chv[0:nk, hh:hh + 1])
                else:
                    bap, bkey = r["bt"][0], r["bt"][1]
                    s1 = plM.next()
                    if merge_ok:
                        bap = r["bt"][3](r["typ0"], nr)
                        tmpv = sm[:, s1, 0:nr * 64]
                    else:
                        tmpv = sm[0:nk, s1, 0:nq]
                    tt("dve", tmpv, ps, bap, ALU.add, [("psL", sbk), bkey], [("sm", s1)])
                    act_fn(pv, tmpv, AF.Exp, [("sm", s1)], [("pTg", pg)])
            for bi_, (kfn, vl, nk, bfn, keys) in enumerate(blocks):
                pts.append((pT[0:nk, pg * 5 + bi_, 0:nq], vl, nk, keys))
            ob = plO.next()
            mmg(ps_t[:, ob, 0:nq], [(vl, pv) for (pv, vl, nk, keys) in pts],
                [("pTg", pg)] + sum([k for (_, _, _, k) in pts], []), [("psL", ob)])
            mmg(ps_t[:, ob, 64:64 + nq], [(ones_b[0:nk, :], pv) for (pv, vl, nk, keys) in pts],
                [("pTg", pg), ("ones",)], [("psL", ob)])
            return (ob, prow, c, qcols, nq, okey)

        def attn_norm(st):
            ob, prow, c, qcols, nq, okey = st
            s2 = plM.next()
            P.op("dve", lambda e, s2=s2, ob=ob, prow=prow: e.reciprocal(out=sm[prow, s2, 0:nq], in_=ps_t[prow, ob, 64:64 + nq]),
                 [("psL", ob)], [("sm", s2)])
            tt("dve", oa[prow, c, qcols], ps_t[prow, ob, 0:nq], sm[prow, s2, 0:nq], ALU.mult,
               [("psL", ob), ("sm", s2)], [okey])

        LA = 1

        def attn_gen(work, pre=None):
            sbs = {}
            nxt = 0
            pend = None
            for k in range(len(work)):
                if pre is not None:
                    pre(k)
                while nxt < len(work) and nxt <= k + LA:
                    sbs[nxt] = attn_S(*work[nxt][0:5])
                    nxt += 1
                st_new = attn_rest(*work[k], sbs.pop(k))
                if pend is not None:
                    attn_norm(pend)
                pend = st_new
                if k == len(work) - 1:
                    attn_norm(pend)
                    pend = None
                yield

        def attn_run(work, pre=None):
            for _ in attn_gen(work, pre):
                pass

        def tile_body(n, mode, ti, pre_done=False, nxt=None, dhooks=None):
            xk = [("xn", c) for c in range(DC)]
            if dhooks is not None:
                plM.base, plM.n = 3, 5
            ffn("g_f1pre", "g_f1post", "f1g", "f1u", "f1d", n, skip_prenorm=pre_done, hooks=dhooks)
            if TSTOP < 4:
                return
            prenorm(h, "h", "g_mpre", n)
            for c in range(8):
                pi = lin("win", c, xn, xk, n)
                cp("act", usf[:, c, 0:n], psL(pi, n), [("psL", pi)], [("usf", c)])
                cp("act", usb[:, c, 0:n], psL(pi, n), [("psL", pi)], [("usb", c)])

            def pre_gen():
                need_kv_out = (mode == "sample") or ti >= NT - 2
                kf = yb[:, 0:8, 0:n]
                vf = yb[:, 8:16, :].rearrange("p c t -> p (c t)")
                if mode == "prompt":
                    kslots = [(2 * ti) % 6, (2 * ti + 1) % 6]
                for c in range(8):
                    pi = lin("win", 16 + c, xn, xk, n)
                    if mode == "prompt":
                        for tb in range(2):
                            cp("act", kring[:, c, kslots[tb], :], psL(pi, n)[:, tb * 128:(tb + 1) * 128], [("psL", pi)],
                               [("kr", c, kslots[tb])])
                    else:
                        cp("act", kring[:, c, 4, 0:16], psL(pi, n)[:, 0:16], [("psL", pi)], [("kr", c, 4)])
                        cp("act", kring[:, c, 5, 0:16], psL(pi, n)[:, 16:32], [("psL", pi)], [("kr", c, 5)])
                    if need_kv_out:
                        cp("dve", kf[:, c, :], psL(pi, n), [("psL", pi)], [("yb", c)])
                    yield
                if need_kv_out:
                    if mode == "prompt":
                        t0 = (ti - (NT - 2)) * T
                        dma(okT.rearrange("(c p) t -> p c t", p=128)[:, :, t0:t0 + T], kf, [("yb", c) for c in range(8)], [("okT", ti)])
                    else:
                        dma(oskT.rearrange("(c p) t -> p c t", p=128), kf, [("yb", c) for c in range(8)], [("oskT",)])
                for c in range(8):
                    wv, wk = wload("win", 24 + c)
                    if mode == "prompt":
                        for tb in range(2):
                            pi = plL.next()
                            mmg(psL(pi, 128), [(xn[:, k, tb * 128:(tb + 1) * 128], wv[:, k, :]) for k in range(16)], [wk] + xk, [("psL", pi)])
                            cp("act", vring[:, kslots[tb], c * 128:(c + 1) * 128], psL(pi, 128), [("psL", pi)], [("vr", kslots[tb], c)])
                            if need_kv_out:
                                cp("dve", vf[:, tb * 1024 + c * 128: tb * 1024 + (c + 1) * 128], psL(pi, 128), [("psL", pi)], [("yb", 8 + tb * 4 + c // 2)])
                    else:
                        for b2 in range(2):
                            pi = plL.next()
                            mmg(psL(pi, 128)[0:16, :], [(xn[:, k, b2 * 16:(b2 + 1) * 16], wv[:, k, :]) for k in range(16)], [wk] + xk, [("psL", pi)])
                            cp("act", vring[0:16, 4 + b2, c * 128:(c + 1) * 128], psL(pi, 128)[0:16, :], [("psL", pi)], [("vr", 4 + b2, c)])
                            cp("dve", vf[0:16, b2 * 1024 + c * 128: b2 * 1024 + (c + 1) * 128], psL(pi, 128)[0:16, :], [("psL", pi)], [("yb", 8 + b2 * 4 + c // 2)])
                    yield
                if need_kv_out:
                    vkeys = [("yb", 8 + i) for i in range(8)]
                    if mode == "prompt":
                        t0 = (ti - (NT - 2)) * T
                        dma(ov[t0:t0 + T, :].rearrange("(b p) f -> p b f", p=128), vf.rearrange("p (b f) -> p b f", b=2), vkeys, [("ov", ti)])
                    else:
                        dma(osv.rearrange("b p f -> p b f"), vf[0:16, :].rearrange("p (b f) -> p b f", b=2), vkeys, [("osv",)])
                for c in range(8):
                    pi = lin("win", 8 + c, xn, xk, n)
                    act_fn(qT[:, c, 0:n], psL(pi, n), AF.Copy, [("psL", pi)], [("q", c)], scale=0.125)
                    yield
                if mode == "prompt":
                    work = []
                    for c in range(8):
                        bb = c % 2
                        for qi in range(4):
                            qc = 4 * ti + qi
                            par = qc % 2
                            blocks = []
                            for b in range(5):
                                gb = qc // 2 - 4 + b
                                if gb < 0:
                                    continue
                                sl = gb % 6
                                typ = {(0, 3): 0, (0, 4): 1, (1, 0): 2, (1, 3): 3, (1, 4): 4}.get((par, b))
                                kfn = (lambda prow, sl=sl, c=c: kring[prow, c, sl, :])
                                vl = vring[:, sl, c * 128:(c + 1) * 128]
                                if typ is None:
                                    bfn = None
                                else:
                                    bfn = (lambda e2, typ=typ, bb=bb: (btl[:, bb, e2, typ, :], ("btl", bb), typ,
                                                                      (lambda t0, nr, e2=e2, bb=bb: btl[:, bb, e2, t0:t0 + nr, :].rearrange("p a q -> p (a q)"))))
                                blocks.append((kfn, vl, 128, bfn, [("kr", c, sl), ("vr", sl, c)]))
                            for e2 in range(2):
                                work.append((c, e2, slice(qi * 64, qi * 64 + 64), 64, blocks, ("oa", c)))
                    def pre(k):
                        if k % 8 == 0:
                            c_ = k // 8
                            bb_ = c_ % 2
                            dma(btl[:, bb_, :, :, :],
                                btd[:, (2 * c_) * 320:(2 * c_ + 2) * 320].rearrange("p (h t q) -> p h t q", h=2, t=5),
                                [], [("btl", bb_)])
                    yield from attn_gen(work, pre)
                else:
                    for b2 in range(2):
                        st = yb[:, :, :].rearrange("p c t -> p (c t)")
                        ybk = [("yb", i) for i in range(16)]
                        dma(st.rearrange("p (c t) -> p c t", c=8), ckT[b2].rearrange("(c p) t -> p c t", p=128), [], ybk)
                        for c in range(8):
                            cp("pool", kring[:, c, 0:4, :], st[:, c * 512:(c + 1) * 512].rearrange("p (s k) -> p s k", s=4), ybk,
                               [("kr", c, s) for s in range(4)])
                        dma(st.rearrange("p (s f) -> p s f", s=4), cv[b2].rearrange("(s p) f -> p s f", p=128), [], ybk)
                        for s in range(4):
                            cp("pool", vring[:, s, :], st[:, s * 1024:(s + 1) * 1024], ybk, [("vr", s, c) for c in range(8)])
                        swork = []
                        for c in range(8):
                            blocks = []
                            for b in range(4):
                                kfn = (lambda prow, b=b, c=c: kring[prow, c, b, :])
                                bfn = None if b < 3 else (lambda e2, c=c: (sbt[:, 0, 2 * c + e2, :], ("sbt",)))
                                blocks.append((kfn, vring[:, b, c * 128:(c + 1) * 128], 128, bfn, [("kr", c, b), ("vr", b, c)]))
                            kfn = (lambda prow, b2=b2, c=c: kring[prow, c, 4 + b2, 0:16])
                            blocks.append((kfn, vring[0:16, 4 + b2, c * 128:(c + 1) * 128], 16,
                                           (lambda e2, c=c: (sbt[0:16, 1, 2 * c + e2, :], ("sbt",))), [("kr", c, 4 + b2), ("vr", 4 + b2, c)]))
                            swork.append((c, 0, slice(b2 * 16, b2 * 16 + 16), 16, blocks, ("oa", c)))
                            swork.append((c, 1, slice(b2 * 16, b2 * 16 + 16), 16, blocks, ("oa", c)))
                        attn_run(swork)
                for c in range(8):
                    pi = lin("win", 32 + c, xn, xk, n)
                    act_fn(qT[:, c, 0:n], psL(pi, n), AF.Copy, [("psL", pi)], [("q", c)], scale=0.0625)
                    yield
                if mode == "prompt":
                    yield from memattn_gen(n, slice(0, n))
                else:
                    for b2 in range(2):
                        st = yb[:, :, :].rearrange("p c t -> p (c t)")
                        ybk = [("yb", i) for i in range(16)]
                        dma(st[:, 0:2048].rearrange("p (c t) -> p c t", c=8), cmkT[b2].rearrange("(c p) t -> p c t", p=128), [], ybk)
                        cp("pool", mkT[:, :, :], st[:, 0:2048].rearrange("p (c t) -> p c t", c=8), ybk, [("mkT",)])
                        dma(st[:, 2048:4096].rearrange("p (s f) -> p s f", s=2), cmv[b2].rearrange("(s p) f -> p s f", p=128), [], ybk)
                        cp("pool", mvb[:, :, :], st[:, 2048:4096].rearrange("p (s f) -> p s f", s=2), ybk, [("mvb",)])
                        memattn(16, slice(b2 * 16, b2 * 16 + 16))
            def mk_unit(j):
                def unit():
                    tms = []
                    for bi, wn, src, skey in ((1, "ba", oa, "oa"), (2, "bm", om, "om")):
                        pb_ = lin(wn, j, src, [(skey, c) for c in range(8)], n)
                        lin("win", 40 + 16 * bi + j, xn, xk, n, pi=pb_, col=256)
                        s1 = plM.next()
                        s2_ = plM.next()
                        act_fn(sm[:, s1, 0:n], ps_t[:, pb_, 256:256 + n], AF.Sigmoid, [("psL", pb_)], [("sm", s1)])
                        cp("act", sm[:, s2_, 0:n], psL(pb_, n), [("psL", pb_)], [("sm", s2_)])
                        tt("pool", sm[:, s1, 0:n], sm[:, s1, 0:n], sm[:, s2_, 0:n], ALU.mult, [("sm", s1), ("sm", s2_)], [("sm", s1)])
                        tms.append(s1)
                    tt("pool", yb[:, j, 0:n], sm[:, tms[0], 0:n], sm[:, tms[1], 0:n], ALU.add,
                       [("sm", tms[0]), ("sm", tms[1])], [("yb", j)])
                return unit
            def all_gen():
                yield from pre_gen()
                if TSTOP >= 7:
                    for j in range(DC):
                        mk_unit(j)()
                        yield
            fillers = all_gen()
            if mode != "prompt":
                for _ in fillers:
                    pass
            if mode == "prompt":
                stv = [(stt[:, 0, 0, :], stt[:, 0, 1, :], ("stt", 0))] * 4
                fo = None
                if ti == NT - 1:
                    fo = lambda: dma(ost, stt[:, 0, :, :], [("stt", 0)], [("ost",)])
                ssm(n, 64, 4, stv, fo, fillers, 116)
            else:
                s0 = spc("s0").rearrange("p (b r a) -> p b r a", b=2, r=2)
                cp("pool", stt[:, :, :, :], s0, [("spt",), ("stt", 0), ("stt", 1)], [("stt", 0), ("stt", 1)])
                stv = [(stt[:, b2, 0, :], stt[:, b2, 1, :], ("stt", b2)) for b2 in range(2)]
                ssm(n, 16, 2, stv, lambda: dma(osst, stt[:, :, :, :], [("stt", 0), ("stt", 1)], [("osst",)]), fillers, 0)
            if TSTOP < 7:
                return
            for j in range(DC):
                pb_ = lin("bs", j, os_, [("os", c) for c in range(8)], n)
                lin("win", 40 + j, xn, xk, n, pi=pb_, col=256)
                s1 = plM.next()
                act_fn(sm[:, s1, 0:n], ps_t[:, pb_, 256:256 + n], AF.Sigmoid, [("psL", pb_)], [("sm", s1)])
                tt("dve", sm[:, s1, 0:n], sm[:, s1, 0:n], psL(pb_, n), ALU.mult, [("sm", s1), ("psL", pb_)], [("sm", s1)])
                tt("pool", mg[:, j, 0:n], sm[:, s1, 0:n], yb[:, j, 0:n], ALU.add, [("sm", s1), ("yb", j)], [("mg", j)])
            for j in range(DC):
                pi = lin("wo", j, mg, [("mg", c) for c in range(DC)], n)
                cp("act", yb[:, j, 0:n], psL(pi, n), [("psL", pi)], [("yb", j)])
            postnorm_res("g_mpost", 1.0, n)
            if TSTOP < 8:
                return
            hooks = None
            if nxt is not None:
                src_next, nn = nxt
                dma(xpre[:, :, 0:nn], src_next, [], XPK)
                stt_box = {}

                def h_stats():
                    stt_box["st"] = prenorm_next_stats(nn)

                def h_rstd():
                    stt_box["s2"] = prenorm_next_rstd(stt_box["st"], nn)
                    plM.base, plM.n = 0, 8

                def h_apply():
                    prenorm_next_apply(stt_box["s2"], "g_f1pre", nn)
                hooks = {8: h_stats, 30: h_rstd, "mid": h_apply}
                plM.base, plM.n = 3, 5
            ffn("g_f2pre", "g_f2post", "f2g", "f2u", "f2d", n, hooks=hooks, defer_post=(nxt is not None))

        P.const.add(("rth",))
        P.op("act", lambda e: e.copy(out=sm[:, 0, 0:1], in_=ones_f[:, 0:1]), [("ones",)],
             [("pq", i) for i in range(10)] + [("stgf", i) for i in range(NSF)] + [("stgb", i) for i in range(NSB)]
             + [("yb", i) for i in range(DC)] + [("sm", 0), ("sqb",)])
        hk = [("h", c) for c in range(DC)]
        if STOP >= 2:
            dma(h[:, :, :], memT.rearrange("(c p) t -> p c t", p=128), [], hk)
            prenorm(h, "h", "g_mem", 256)
            xk_ = [("xn", c) for c in range(DC)]
            for c in range(8 if KSUB >= 1 else 0):
                pi = lin("mk", c, xn, xk_, 256)
                if KSUB2 >= 1:
                    cp("act", mkT[:, c, :], psL(pi, 256), [("psL", pi)], [("mkTc", c)])
                if KSUB2 >= 2:
                    cp("dve", yb[:, c, :], psL(pi, 256), [("psL", pi)], [("yb", c)])
            if KSUB2 >= 3:
                dma(omkT.rearrange("(c p) t -> p c t", p=128), yb[:, 0:8, :], [("yb", c) for c in range(8)], [("omkT",)])
            vfm = yb[:, 8:16, :].rearrange("p c t -> p (c t)")
            for c in range(8 if KSUB >= 2 else 0):
                wv, wk = wload("mv", c)
                for mb in range(2):
                    pi = plL.next()
                    mmg(psL(pi, 128), [(xn[:, k, mb * 128:(mb + 1) * 128], wv[:, k, :]) for k in range(16)], [wk] + xk_, [("psL", pi)])
                    cp("act", mvb[:, mb, c * 128:(c + 1) * 128], psL(pi, 128), [("psL", pi)], [("mvb",)])
                    cp("dve", vfm[:, mb * 1024 + c * 128: mb * 1024 + (c + 1) * 128], psL(pi, 128), [("psL", pi)], [("yb", 8 + mb * 4 + c // 2)])
            dma(omv.rearrange("(b p) f -> p b f", p=128), vfm.rearrange("p (b f) -> p b f", b=2), [("yb", 8 + i) for i in range(8)], [("omv",)])

        xTv = xT.rearrange("(c p) t -> p c t", p=128)
        yTv = yT.rearrange("(c p) t -> p c t", p=128)
        ntl = min(NT, NTILES) if STOP >= 3 else 0
        xsv = xsT.rearrange("(c p) t -> p c t", p=128)
        do_sample = STOP >= 10
        pre_done = False
        dhooks = None

        def make_deferred(store_fn, n_prev, n_cur):
            box = {}

            def d_stats():
                box["st"] = prenorm_next_stats(n_prev, yb, lambda c: [("yb", c)])

            def d_rstd():
                box["s2"] = prenorm_next_rstd(box["st"], n_prev)
                plM.base, plM.n = 0, 8

            def d_apply():
                s2 = box["s2"]
                g = spc("g_f2post")
                for c in range(DC):
                    stt_(yb[:, c, 0:n_prev], yb[:, c, 0:n_prev], g[:, c:c + 1], rsb[:, s2, 0:n_prev], ALU.mult, ALU.mult,
                         [("yb", c), ("rsb", s2), ("spt",)], [("yb", c)])
                for c in range(DC):
                    stt_(h[:, c, 0:n_prev], yb[:, c, 0:n_prev], 0.5, h[:, c, 0:n_prev], ALU.mult, ALU.add,
                         [("yb", c), ("h", c)], [("h", c)])

            def d_store():
                store_fn()
                cp("pool", h[:, :, 0:n_cur], xpre[:, :, 0:n_cur], XPK, hk)
            return {2: d_stats, 14: d_rstd, 20: d_apply, "mid": d_store}

        for ti in range(ntl):
            P.epoch = 1 + ti // EPOCH_TILES
            if not pre_done:
                dma(h[:, :, :], xTv[:, :, ti * T:(ti + 1) * T], [], hk)
            if ti + 1 < ntl:
                nxt = (xTv[:, :, (ti + 1) * T:(ti + 2) * T], T)
            elif do_sample and TSTOP >= 99:
                nxt = (xsv, TS)
            else:
                nxt = None
            if TSTOP < 99:
                nxt = None
            tile_body(T, "prompt", ti, pre_done, nxt, dhooks)
            store_fn = (lambda ti=ti: dma(yTv[:, :, ti * T:(ti + 1) * T], h[:, :, :], hk, [("yT", ti)]))
            if nxt is not None:
                dhooks = make_deferred(store_fn, T, nxt[1])
                pre_done = True
            else:
                store_fn()
                dhooks = None
                pre_done = False
        P.epoch += 1
        if do_sample:
            if not pre_done:
                dma(h[:, :, 0:TS], xsv, [], hk)
            tile_body(TS, "sample", 0, pre_done, None, dhooks)
            dma(ysT.rearrange("(c p) t -> p c t", p=128), h[:, :, 0:TS], hk, [("ysT",)])
        elif dhooks is not None:
            raise RuntimeError("deferred work pending")

        P.assign()
        semnames = {}
        sems = {}
        for e in ("pe", "act", "dve", "pool"):
            sems[e] = [es.enter_context(nc.semaphore(f"s_{e}_{i}")) for i in range(P.nepoch)]
        dsems = [es.enter_context(nc.semaphore(f"s_d_{i}")) for i in range(NDS)]
        block = es.enter_context(nc.Block())

        @block.sync
        def _(e):
            P.emit("sp", e, sems, dsems)

        @block.tensor
        def _(e):
            P.emit("pe", e, sems, dsems)

        @block.scalar
        def _(e):
            P.emit("act", e, sems, dsems)

        @block.vector
        def _(e):
            P.emit("dve", e, sems, dsems)

        @block.gpsimd
        def _(e):
            P.emit("pool", e, sems, dsems)
    return nc


def _tile_w(w, kc):
    K, N = w.shape
    nb = N // 128
    a = w.reshape(kc, 128, nb, 128).transpose(1, 2, 0, 3)
    return a.reshape(128, nb * kc * 128)


def _gcol(g, nch):
    return np.ascontiguousarray(g.reshape(nch, 128).T)


_NC_CACHE = {}


def kernel(**inp):
    f = lambda k: np.asarray(inp[k], dtype=np.float32)
    ws = {"f1g": f("ffn1_w_gate")[0], "f1u": f("ffn1_w_up")[0], "f1d": f("ffn1_w_down")[0], "win": f("w_in")[0],
          "glu": f("ssm_w_glu")[0], "bs": f("w_branch_ssm")[0], "ba": f("w_branch_att")[0], "bm": f("w_branch_mem")[0],
          "wo": f("w_out")[0], "f2g": f("ffn2_w_gate")[0], "f2u": f("ffn2_w_up")[0], "f2d": f("ffn2_w_down")[0],
          "mk": f("w_mem_k")[0], "mv": f("w_mem_v")[0]}
    wall = np.empty((128, WX), np.float32)
    for n, kc, nb in WTAB:
        o = WOFF[n][0]
        wall[:, o:o + kc * nb * 128] = _tile_w(ws[n], kc)

    a_re = f("ssm_a_re")[0]; a_im = f("ssm_a_im")[0]; ldt = f("ssm_log_dt")[0]
    b_re = f("ssm_b_re")[0]; b_im = f("ssm_b_im")[0]; c_re = f("ssm_c_re")[0]; c_im = f("ssm_c_im")[0]
    G = 64
    gidx = (2 * np.arange(32)[None, :] + (np.arange(128)[:, None] // 64))
    pidx = np.broadcast_to((np.arange(128) % 64)[:, None], (128, 32))
    are_m = a_re[gidx, pidx]; aim_m = a_im[gidx, pidx]; ldt_m = ldt[gidx]
    ssmB = np.zeros((128, 4, 5, 8, 128), np.float32)
    ssmC = np.zeros((128, 4, 2, 8, 128), np.float32)
    for pair in range(32):
        qt, pp = divmod(pair, 8)
        for gp in range(2):
            g = 2 * pair + gp
            gl = g % 8
            ms = slice(gp * 64, gp * 64 + 64)
            ssmB[:, qt, 0, pp, ms] = a_re[g][None, :]
            ssmB[:, qt, 1, pp, ms] = a_im[g][None, :]
            ssmB[:, qt, 2, pp, ms] = ldt[g]
            ssmB[gl * 16:(gl + 1) * 16, qt, 3, pp, ms] = b_re[g].T
            ssmB[gl * 16:(gl + 1) * 16, qt, 4, pp, ms] = b_im[g].T
            ssmC[ms, qt, 0, pp, gl * 16:(gl + 1) * 16] = c_re[g].T
            ssmC[ms, qt, 1, pp, gl * 16:(gl + 1) * 16] = c_im[g].T
    ssmB = ssmB.reshape(128, 4, 5, 1024)
    ssmC = ssmC.reshape(128, 4, 2048)

    rb = f("att_rel_bias")[0]
    btd = np.zeros((128, BT_TOT), np.float32)
    kk = np.arange(128)[:, None]; qq = np.arange(64)[None, :]
    types = [(0, 3), (0, 4), (1, 0), (1, 3), (1, 4)]
    bt = np.zeros((128, 16, 5, 64), np.float32)
    for t_, (par, b) in enumerate(types):
        rel = 64 * par + 128 * (4 - b) + qq - kk
        dcn = -par - 8 + 2 * b + kk // 64 + 0 * qq
        ok = (dcn >= -8) & (dcn <= 0)
        idx = np.clip(rel, -128, 128) + 128
        for hh in range(16):
            bt[:, hh, t_, :] = np.where(ok, rb[hh][idx], np.float32(NEG))
    btd[:, 0:BTW] = bt.reshape(128, BTW)
    q16 = np.arange(16)[None, :]
    rel3 = 128 + q16 - kk
    s3 = np.stack([rb[hh][np.clip(rel3, -128, 128) + 128] for hh in range(16)], 1)
    k16 = np.arange(16)[:, None]
    reln = q16 - k16
    sn = np.zeros((128, 16, 16), np.float32)
    sn[0:16] = np.stack([rb[hh][np.clip(reln, -128, 128) + 128] for hh in range(16)], 1)
    btd[:, SBT_OFF:SBT_OFF + 256] = s3.reshape(128, 256)
    btd[:, SBT_OFF + 256:SBT_OFF + 512] = sn.reshape(128, 256)

    def spack(s0=None):
        a = np.zeros((128, SPW), np.float32)

        def put(name, v):
            o, w_ = SPC[name]
            a[:, o:o + w_] = v
        put("g_f1pre", _gcol(f("ffn1_norm_pre")[0], 16)); put("g_f1post", _gcol(f("ffn1_norm_post")[0], 16))
        put("g_mpre", _gcol(f("mix_norm_pre")[0], 16)); put("g_mpost", _gcol(f("mix_norm_post")[0], 16))
        put("g_f2pre", _gcol(f("ffn2_norm_pre")[0], 16)); put("g_f2post", _gcol(f("ffn2_norm_post")[0], 16))
        put("g_mem", _gcol(f("mem_norm")[0], 16))
        put("dcol", _gcol(f("ssm_d")[0].reshape(-1), 8)); put("bglu", _gcol(f("ssm_b_glu")[0], 8))
        put("ch", np.broadcast_to(rb[:, 256][None, :], (128, 16)))
        put("are", are_m); put("aim", aim_m); put("ldt", ldt_m)
        put("jidx", np.broadcast_to(np.arange(1, 65, dtype=np.float32)[None, :], (128, 64)))
        if s0 is not None:
            put("s0", s0)
        return a

    xp = f("x_prompt"); xs = f("x_sample"); mp = f("mem_prompt")
    ck = f("cache_att_k")[0]; cvv = f("cache_att_v")[0]; cmk = f("cache_mem_k")[0]; cmvv = f("cache_mem_v")[0]
    sre = f("state_ssm_re")[0]; sim = f("state_ssm_im")[0]
    in_maps = []
    for c in range(8):
        s0 = np.zeros((128, 2, 2, 32), np.float32)
        for b2 in range(2):
            s0[:, b2, 0, :] = sre[2 * c + b2][gidx, pidx]
            s0[:, b2, 1, :] = sim[2 * c + b2][gidx, pidx]
        in_maps.append({
            "xT": np.ascontiguousarray(xp[c].T),
            "xsT": np.ascontiguousarray(xs[2 * c:2 * c + 2].reshape(TS, D).T),
            "memT": np.ascontiguousarray(mp[c].T),
            "ckT": np.ascontiguousarray(ck[2 * c:2 * c + 2].reshape(2, 512, 1024).transpose(0, 2, 1)),
            "cv": np.ascontiguousarray(cvv[2 * c:2 * c + 2].reshape(2, 512, 1024)),
            "cmkT": np.ascontiguousarray(cmk[2 * c:2 * c + 2].reshape(2, 256, 1024).transpose(0, 2, 1)),
            "cmv": np.ascontiguousarray(cmvv[2 * c:2 * c + 2].reshape(2, 256, 1024)),
            "wall": wall, "spk": spack(s0.reshape(128, 128)), "ssmB": ssmB, "ssmC": ssmC, "btd": btd,
        })
    if "nc" not in _NC_CACHE:
        _NC_CACHE["nc"] = build_nc()
    if os.environ.get('KTRACE'):
        res = run_bass_kernel_spmd(_NC_CACHE["nc"], in_maps[:NCORES], core_ids=list(range(NCORES)), trace=True)
        print("EXEC_TIME_NS", res.exec_time_ns)
    else:
        res = run_bass_kernel_spmd(_NC_CACHE["nc"], in_maps[:NCORES], core_ids=list(range(NCORES)))
    R_ = list(res.results)
    while len(R_) < 8:
        R_.append({k: np.zeros_like(v) for k, v in R_[0].items()})

    def unstate(a):
        o = np.zeros((64, 64), np.float32)
        o[gidx, pidx] = a
        return o
    y_p = np.stack([R_[c]["yT"].T for c in range(8)])
    y_s = np.concatenate([R_[c]["ysT"].T.reshape(2, 16, D) for c in range(8)])
    akp = np.stack([R_[c]["okT"].T.reshape(512, 16, 64) for c in range(8)])[None]
    avp = np.stack([R_[c]["ov"].reshape(512, 16, 64) for c in range(8)])[None]
    mkp = np.stack([R_[c]["omkT"].T.reshape(256, 4, 256) for c in range(8)])[None]
    mvp = np.stack([R_[c]["omv"].reshape(256, 4, 256) for c in range(8)])[None]
    srp = np.stack([unstate(R_[c]["ost"][:, 0, :]) for c in range(8)])[None]
    sip = np.stack([unstate(R_[c]["ost"][:, 1, :]) for c in range(8)])[None]
    aks = np.concatenate([R_[c]["oskT"].T.reshape(2, 16, 16, 64) for c in range(8)])[None]
    avs = np.concatenate([R_[c]["osv"].reshape(2, 16, 16, 64) for c in range(8)])[None]
    srs = np.stack([unstate(R_[c]["osst"][:, b2, 0, :]) for c in range(8) for b2 in range(2)])[None]
    sis = np.stack([unstate(R_[c]["osst"][:, b2, 1, :]) for c in range(8) for b2 in range(2)])[None]
    outs = (y_p, y_s, akp, avp, mkp, mvp, srp, sip, aks, avs, srs, sis)
    return tuple(np.ascontiguousarray(o, dtype=np.float32) for o in outs)
```
